# Optimizing a Trainium2 kernel written in Bass

```python
import math
import jax, jax.numpy as jnp
from jax import lax
import numpy as np

D_MODEL = 1024
BATCH = 8
SEQ = 2048
DEPTH = 1

GDN_HEAD_DIM = 128
GDN_HEADS = D_MODEL // GDN_HEAD_DIM
GDN_WIDTH = GDN_HEADS * GDN_HEAD_DIM
CONV_K = 4
CHUNK = 64
DIFF_HEAD_DIM = 64
DIFF_V_DIM = 2 * DIFF_HEAD_DIM
DIFF_HEADS = D_MODEL // DIFF_V_DIM
DIFF_QK_WIDTH = DIFF_HEADS * 2 * DIFF_HEAD_DIM
DIFF_WIDTH = DIFF_HEADS * DIFF_V_DIM
Q_BLOCK = 128
D_FF = -(-8 * D_MODEL // (3 * 256)) * 256
EPS = 1e-6

SPLIT_SIZES = (GDN_WIDTH, GDN_WIDTH, GDN_WIDTH, GDN_WIDTH, GDN_HEADS, GDN_HEADS,
               DIFF_QK_WIDTH, DIFF_QK_WIDTH, DIFF_WIDTH, D_MODEL, D_MODEL)
D_IN = sum(SPLIT_SIZES)
SPLIT_POINTS = tuple(sum(SPLIT_SIZES[:i + 1]) for i in range(len(SPLIT_SIZES) - 1))

kernel_name = "hybrid_gdn_diffattn_gated_merge_swiglu"


def rms_norm(x, w):
    xf = x.astype(jnp.float32)
    y = xf * lax.rsqrt(jnp.mean(xf * xf, axis=-1, keepdims=True) + EPS)
    return (y * w.astype(jnp.float32)).astype(x.dtype)


def l2_norm(x):
    return x * lax.rsqrt(jnp.sum(x * x, axis=-1, keepdims=True) + EPS)


def causal_depthwise_conv(x, w):
    return lax.conv_general_dilated(
        x, w[:, None, :], window_strides=(1,), padding=[(w.shape[0] - 1, 0)],
        dimension_numbers=('NWC', 'WIO', 'NWC'), feature_group_count=x.shape[-1])


def chunk_gated_delta_rule(q, k, v, g, beta):
    B, S, H, DK = q.shape
    DV = v.shape[-1]
    N = S // CHUNK

    def to_chunks(t):
        t = jnp.moveaxis(t, 2, 1)
        return t.reshape(t.shape[:2] + (N, CHUNK) + t.shape[3:])

    q, k, v, g, beta = (to_chunks(t) for t in (q * DK ** -0.5, k, v, g, beta))
    gc = jnp.cumsum(g, axis=-1)
    idx = jnp.arange(CHUNK)
    causal = idx[:, None] >= idx[None, :]
    strict = idx[:, None] > idx[None, :]
    decay = jnp.exp(jnp.where(causal, gc[..., :, None] - gc[..., None, :], -jnp.inf))
    kb = k * beta[..., None]
    lower = jnp.where(strict, jnp.einsum('bhnid,bhnjd->bhnij', kb, k) * decay, 0.0)
    eye = jnp.eye(CHUNK, dtype=q.dtype)
    t_inv = lax.linalg.triangular_solve(eye + lower, jnp.broadcast_to(eye, lower.shape),
                                        left_side=True, lower=True)
    u = t_inv @ (v * beta[..., None])
    w = t_inv @ (kb * jnp.exp(gc)[..., None])
    a_qk = jnp.einsum('bhnid,bhnjd->bhnij', q, k) * decay
    q_dec = q * jnp.exp(gc)[..., None]
    g_last = gc[..., -1]
    k_dec = k * jnp.exp(g_last[..., None] - gc)[..., None]

    def step(state, xs):
        u_n, w_n, qd_n, aqk_n, kd_n, gl_n = xs
        v_new = u_n - w_n @ state
        o_n = qd_n @ state + aqk_n @ v_new
        state = state * jnp.exp(gl_n)[..., None, None] + jnp.swapaxes(kd_n, -1, -2) @ v_new
        return state, o_n

    xs = tuple(jnp.moveaxis(t, 2, 0) for t in (u, w, q_dec, a_qk, k_dec, g_last))
    s0 = jnp.zeros((B, H, DK, DV), q.dtype)
    _, o = lax.scan(step, s0, xs)
    o = jnp.moveaxis(o, 0, 2).reshape(B, H, S, DV)
    return jnp.moveaxis(o, 1, 2)


def gated_delta_net(q, k, v, z, a, b, conv_w, a_log, dt_bias, norm_w):
    B, S, _ = q.shape
    f32 = jnp.float32
    qkv = jnp.concatenate([q, k, v], axis=-1).astype(f32)
    qkv = jax.nn.silu(causal_depthwise_conv(qkv, conv_w.astype(f32)))
    q, k, v = jnp.split(qkv, 3, axis=-1)
    q = l2_norm(q.reshape(B, S, GDN_HEADS, GDN_HEAD_DIM))
    k = l2_norm(k.reshape(B, S, GDN_HEADS, GDN_HEAD_DIM))
    v = v.reshape(B, S, GDN_HEADS, GDN_HEAD_DIM)
    beta = jax.nn.sigmoid(b.astype(f32))
    g = -jnp.exp(a_log.astype(f32)) * jax.nn.softplus(a.astype(f32) + dt_bias.astype(f32))
    o = chunk_gated_delta_rule(q, k, v, g, beta)
    o = rms_norm(o, norm_w) * jax.nn.silu(z.reshape(B, S, GDN_HEADS, GDN_HEAD_DIM).astype(f32))
    return o.reshape(B, S, GDN_WIDTH)


def diff_attention(q, k, v, q_norm_w, k_norm_w, lq1, lk1, lq2, lk2, subln_w, lambda_init):
    B, S, _ = q.shape
    f32 = jnp.float32
    q = rms_norm(q.reshape(B, S, DIFF_HEADS, 2, DIFF_HEAD_DIM), q_norm_w).astype(f32) * DIFF_HEAD_DIM ** -0.5
    k = rms_norm(k.reshape(B, S, DIFF_HEADS, 2, DIFF_HEAD_DIM), k_norm_w).astype(f32)
    v = v.reshape(B, S, DIFF_HEADS, DIFF_V_DIM).astype(f32)
    lam = (jnp.exp(jnp.sum(lq1.astype(f32) * lk1.astype(f32)))
           - jnp.exp(jnp.sum(lq2.astype(f32) * lk2.astype(f32))) + lambda_init)
    outs = []
    for blk in range(S // Q_BLOCK):
        start = blk * Q_BLOCK
        end = start + Q_BLOCK
        s = jnp.einsum('bqhcd,bkhcd->bhcqk', q[:, start:end], k[:, :end])
        qpos = start + jnp.arange(Q_BLOCK)
        kpos = jnp.arange(end)
        s = jnp.where(kpos[None, :] <= qpos[:, None], s, -jnp.inf)
        p = jax.nn.softmax(s, axis=-1)
        attn = p[:, :, 0] - lam * p[:, :, 1]
        outs.append(jnp.einsum('bhqk,bkhd->bqhd', attn, v[:, :end]))
    o = jnp.concatenate(outs, axis=1)
    o = rms_norm(o, subln_w) * (1.0 - lambda_init)
    return o.reshape(B, S, DIFF_WIDTH)


def setup_inputs(seed: int = 0) -> dict:
    key = jax.random.key(seed)
    ks = jax.random.split(key, 20)
    nrm = jax.random.normal
    f32 = jnp.float32
    dt = jnp.exp(jax.random.uniform(ks[5], (DEPTH, GDN_HEADS), f32, math.log(1e-3), math.log(1e-1)))
    return {
        "x": nrm(ks[0], (BATCH, SEQ, D_MODEL), f32),
        "norm1_w": 1.0 + 0.02 * nrm(ks[1], (DEPTH, D_MODEL), f32),
        "w_in": nrm(ks[2], (DEPTH, D_MODEL, D_IN), f32) * D_MODEL ** -0.5,
        "conv_w": nrm(ks[3], (DEPTH, CONV_K, 3 * GDN_WIDTH), f32) * CONV_K ** -0.5,
        "a_log": jnp.log(jax.random.uniform(ks[4], (DEPTH, GDN_HEADS), f32, 1.0, 16.0)),
        "dt_bias": dt + jnp.log(-jnp.expm1(-dt)),
        "gdn_norm_w": 1.0 + 0.02 * nrm(ks[6], (DEPTH, GDN_HEAD_DIM), f32),
        "q_norm_w": 1.0 + 0.02 * nrm(ks[7], (DEPTH, DIFF_HEAD_DIM), f32),
        "k_norm_w": 1.0 + 0.02 * nrm(ks[8], (DEPTH, DIFF_HEAD_DIM), f32),
        "lambda_q1": 0.1 * nrm(ks[9], (DEPTH, DIFF_HEAD_DIM), f32),
        "lambda_k1": 0.1 * nrm(ks[10], (DEPTH, DIFF_HEAD_DIM), f32),
        "lambda_q2": 0.1 * nrm(ks[11], (DEPTH, DIFF_HEAD_DIM), f32),
        "lambda_k2": 0.1 * nrm(ks[12], (DEPTH, DIFF_HEAD_DIM), f32),
        "subln_w": 1.0 + 0.02 * nrm(ks[13], (DEPTH, DIFF_V_DIM), f32),
        "w_out": nrm(ks[14], (DEPTH, D_MODEL, D_MODEL), f32) * D_MODEL ** -0.5,
        "norm2_w": 1.0 + 0.02 * nrm(ks[15], (DEPTH, D_MODEL), f32),
        "w_gate": nrm(ks[16], (DEPTH, D_MODEL, D_FF), f32) * D_MODEL ** -0.5,
        "w_up": nrm(ks[17], (DEPTH, D_MODEL, D_FF), f32) * D_MODEL ** -0.5,
        "w_down": nrm(ks[18], (DEPTH, D_FF, D_MODEL), f32) * D_FF ** -0.5,
    }


def reference(x, norm1_w, w_in, conv_w, a_log, dt_bias, gdn_norm_w, q_norm_w, k_norm_w,
              lambda_q1, lambda_k1, lambda_q2, lambda_k2, subln_w, w_out, norm2_w,
              w_gate, w_up, w_down):
    h = x
    for l in range(DEPTH):
        lambda_init = 0.8 - 0.6 * math.exp(-0.3 * l)
        u = rms_norm(h, norm1_w[l])
        proj = u @ w_in[l]
        (gq, gk, gv, gz, ga, gb, dq, dk, dv, gate_a, gate_b) = jnp.split(proj, SPLIT_POINTS, axis=-1)
        o_a = gated_delta_net(gq, gk, gv, gz, ga, gb, conv_w[l], a_log[l], dt_bias[l],
                              gdn_norm_w[l]).astype(h.dtype)
        o_b = diff_attention(dq, dk, dv, q_norm_w[l], k_norm_w[l], lambda_q1[l], lambda_k1[l],
                             lambda_q2[l], lambda_k2[l], subln_w[l], lambda_init).astype(h.dtype)
        mixed = jax.nn.sigmoid(gate_a) * o_a + jax.nn.sigmoid(gate_b) * o_b
        h = h + mixed @ w_out[l]
        u = rms_norm(h, norm2_w[l])
        h = h + (jax.nn.silu(u @ w_gate[l]) * (u @ w_up[l])) @ w_down[l]
    return h
```

```python
import contextlib
import numpy as np
import concourse.bass as bass
import concourse.mybir as mybir
from concourse.bass_utils import run_bass_kernel_spmd

F32 = mybir.dt.float32
BF16 = mybir.dt.bfloat16
AF = mybir.ActivationFunctionType
ALU = mybir.AluOpType
AX = mybir.AxisListType

SBUF_LO = 16512 + 64
SBUF_HI = 229376

S = 2048
D = 1024
NT = 16
H = 8
DFF = 2816
NFC = DFF // 128
EPS = 1e-6
import os
ATT_LOOK = int(os.environ.get('ATT_LOOK', '3'))
INP_LAG = int(os.environ.get('INP_LAG', '1'))
NHEADS = int(os.environ.get('NHEADS', '8'))
DCUT = int(os.environ.get('DCUT', '0'))
SAME_ENG_DIST = int(os.environ.get('SAME_ENG_DIST', '3'))
OVERLAP = int(os.environ.get('OVERLAP', '0'))
SCHED = int(os.environ.get('SCHED', '1'))
PRIO = float(os.environ.get('PRIO', '1'))
BSCALE = float(os.environ.get('BSCALE', '1'))
PREP_B = int(os.environ.get('PREP_B', '8'))
PE_C0 = float(os.environ.get('PE_C0', '0.036'))
PE_RATE = float(os.environ.get('PE_RATE', '2800'))
WRING = int(os.environ.get('WRING', '2'))
D_IN = 9232
COL = dict(gq=0, gk=1024, gv=2048, gz=3072, ga=4096, gb=4104, dq=4112, dk=5136, dv=6160,
           gate_a=7184, gate_b=8208)
HEAD_GROUPS = ("gq", "gk", "gv", "gz", "dq", "dk", "dv", "gate_a", "gate_b")


class Buf:
    __slots__ = ("name", "lw", "rd", "excl")

    def __init__(self, name, excl=False):
        self.name = name
        self.lw = None
        self.rd = set()
        self.excl = excl


class TT:
    def __init__(self, prog, name, h, excl=False):
        self.prog = prog
        self.name = name
        self.h = h
        self.ap = h.ap()
        self.slots = {}
        self.excl = excl

    def s(self, *key):
        b = self.slots.get(key)
        if b is None:
            b = Buf(f"{self.name}{key}", self.excl)
            self.slots[key] = b
            self.prog.allbufs.append(b)
        return b

    def __getitem__(self, idx):
        return self.ap[idx]


class Prog:
    COMPUTE = ("pe", "act", "dve", "pool")
    NDMASEM = 48

    def __init__(self, nc, same_engine_sync=True):
        self.nc = nc
        self.items = {k: [] for k in ("pe", "act", "dve", "pool", "sp")}
        self.seq = {k: 0 for k in self.COMPUTE}
        self.waited = {k: {} for k in self.items}
        self.allbufs = []
        self.same_engine_sync = same_engine_sync
        self.dma_tot = [0] * self.NDMASEM
        self.dma_rr = 0
        self.sb_ptr = SBUF_LO
        self.sb_peak = SBUF_LO
        self.nalloc = 0
        self.psum_banks = []
        self.ps_rr = 0
        self.ps_pinned = set()
        self.n_wait = 0
        self.n_ops = 0
        self.n_pe = 0
        self.marks = []
        self.pool_out = []
        self.nodes = []
        self.seg_bounds = []
        self.seg_stats = []
        self._busy = {}
        self.tag = ""
        self.crit = []

    def sbuf(self, name, shape, dtype):
        esz = 4 if dtype == F32 else 2
        per_part = int(np.prod(shape[1:])) * esz
        off = (self.sb_ptr + 63) // 64 * 64
        assert off + per_part <= SBUF_HI, f"SBUF overflow allocating {name}: {off}+{per_part}"
        self.sb_ptr = off + per_part
        self.sb_peak = max(self.sb_peak, self.sb_ptr)
        self.nalloc += 1
        h = self.nc.alloc_sbuf_tensor_at(f"{name}_{self.nalloc}", list(shape), dtype, offset=off)
        return TT(self, name, h)

    def sbuf_at(self, name, shape, dtype, off):
        esz = 4 if dtype == F32 else 2
        per_part = int(np.prod(shape[1:])) * esz
        assert off % 64 == 0 and off + per_part <= SBUF_HI
        self.nalloc += 1
        h = self.nc.alloc_sbuf_tensor_at(f"{name}_{self.nalloc}", list(shape), dtype, offset=off)
        return TT(self, name, h), off + per_part

    def mark(self):
        return self.sb_ptr

    def release(self, mark):
        self.sb_ptr = mark

    def init_psum(self):
        for i in range(8):
            h = self.nc.alloc_psum_tensor(f"psb{i}", [128, 512], F32)
            self.psum_banks.append(TT(self, f"psb{i}", h, excl=True))

    def psum(self):
        for _ in range(8):
            i = self.ps_rr
            self.ps_rr = (self.ps_rr + 1) % 8
            if i not in self.ps_pinned:
                return self.psum_banks[i]
        raise RuntimeError("all psum pinned")

    def psum_pin(self):
        t = self.psum()
        self.ps_pinned.add(self.psum_banks.index(t))
        return t

    def psum_unpin(self, t):
        self.ps_pinned.discard(self.psum_banks.index(t))

    def _node(self, eng, fn, reads, writes, cost, is_dma=False, ndesc=0, lat=0.0):
        writes = [w for w in writes if w is not None] + [r for r in reads if r is not None and r.excl]
        reads = [r for r in reads if r is not None and not r.excl]
        nid = len(self.nodes)
        deps = set()
        for r in reads:
            if r.lw is not None:
                deps.add(r.lw)
        for w in writes:
            if w.lw is not None:
                deps.add(w.lw)
            deps.update(w.rd)
        deps.discard(nid)
        self.nodes.append(dict(id=nid, eng=eng, fn=fn, deps=deps, cost=cost, dma=is_dma, ndesc=ndesc, lat=lat,
                               tag=self.tag))
        for r in reads:
            r.rd.add(nid)
        for w in writes:
            w.lw = nid
            w.rd = set()
        return nid

    @staticmethod
    def _nfree(ap):
        try:
            sh = list(ap.shape)
            n = 1
            for d in sh[1:]:
                n *= int(d)
            return n
        except Exception:
            return 128

    def _cost(self, eng, method, kw):
        out = kw.get("out", kw.get("ap"))
        n = self._nfree(out) if out is not None else 128
        if eng == "pe":
            if method == "transpose":
                return 0.12
            f32 = False
            try:
                f32 = kw["lhsT"].tensor.dtype == F32
            except Exception:
                pass
            return PE_C0 + (4.0 if f32 else 1.0) * max(64, n) / PE_RATE
        if eng == "act":
            return 0.22 + n / 1400.0 + (0.1 if "accum_out" in kw else 0.0)
        if eng == "dve":
            return 0.12 + n / 960.0 * (8.0 if method == "reciprocal" else 1.0)
        return (0.9 if method == 'tensor_tensor' and n <= 16 else 0.3) + n / 700.0

    def op(self, eng, method, R=(), W=(), **kw):
        def fn(e, method=method, kw=kw):
            return getattr(e, method)(**kw)
        self.n_ops += 1
        if eng == "pe":
            self.n_pe += 1
        self._node(eng, fn, R, W, self._cost(eng, method, kw))

    def mark_phase(self, name):
        self.marks.append((name, self.n_pe))

    def ops(self, eng, calls, R=(), W=()):
        calls = list(calls)
        if eng == "pe":
            self.n_pe += len(calls)
        self.n_ops += 1

        def fn(e, calls=calls):
            ins = None
            for (method, kw) in calls:
                ins = getattr(e, method)(**kw)
            return ins
        self._node(eng, fn, R, W, sum(self._cost(eng, m, kw) for m, kw in calls))

    def dma(self, queue, R=(), W=(), ndesc=64, **kw):
        def fn(e, kw=kw):
            return e.dma_start(**kw)
        nbytes = self._nfree(kw["out"]) * 128 * 4
        self._node(queue, fn, R, W, 1.2 if queue == "pool" else 0.1, is_dma=True, ndesc=ndesc,
                   lat=2.0 + nbytes / 1.5e5)

    def barrier(self):
        self.seg_bounds.append(len(self.nodes))
        for b in self.allbufs:
            b.lw = None
            b.rd = set()

    HOP = float(os.environ.get('HOP', '0.35'))
    POOL_DESC_CAP = int(os.environ.get('PCAP', '200'))

    def _schedule_segment(self, lo, hi):
        import heapq
        if not SCHED:
            return list(range(lo, hi))
        nodes = self.nodes
        ndep = {}
        users = {}
        for n in nodes[lo:hi]:
            d = [x for x in n["deps"] if x >= lo]
            ndep[n["id"]] = len(d)
            for x in d:
                users.setdefault(x, []).append(n["id"])
        blevel = {}
        for n in reversed(nodes[lo:hi]):
            nid = n["id"]
            best = 0.0
            for u in users.get(nid, ()):
                un = nodes[u]
                lat = 0.05 if (un["eng"] == n["eng"] and not n["dma"]) else self.HOP
                v = lat + blevel[u]
                if v > best:
                    best = v
            blevel[nid] = n["cost"] + n["lat"] + best
        fin = {}
        ready_t = {}
        engs = ("pe", "act", "dve", "pool", "sp")
        avail = {e: [] for e in engs}
        free_t = {e: 0.0 for e in engs}
        for n in nodes[lo:hi]:
            if ndep[n["id"]] == 0:
                avail[n["eng"]].append(n["id"])
                ready_t[n["id"]] = 0.0
        order = []
        last_on = {}
        crit_dep = {}
        remaining = hi - lo
        while remaining:
            best = None
            for e in engs:
                lst = avail[e]
                if not lst:
                    continue
                ft = free_t[e]
                cb = None
                for nid in lst:
                    rt = ready_t[nid]
                    st = rt if rt > ft else ft
                    key = (st, -blevel[nid] * PRIO, nid)
                    if cb is None or key < cb:
                        cb = key
                if best is None or cb < best[0]:
                    best = (cb, e)
            (st, _, nid), e = best
            avail[e].remove(nid)
            n = nodes[nid]
            rt = ready_t[nid]
            start = st
            n["start"] = start
            n["why"] = ("eng", last_on.get(e)) if free_t[e] >= rt and last_on.get(e) is not None else ("dep", crit_dep.get(nid))
            last_on[e] = nid
            free_t[e] = start + n["cost"]
            fin[nid] = start + n["cost"] + n["lat"]
            order.append(nid)
            remaining -= 1
            self._busy[e] = self._busy.get(e, 0.0) + n["cost"]
            for u in users.get(nid, ()):
                un = nodes[u]
                lat = 0.05 if (un["eng"] == e and not n["dma"]) else self.HOP
                if fin[nid] + lat >= ready_t.get(u, 0.0):
                    crit_dep[u] = nid
                ready_t[u] = max(ready_t.get(u, 0.0), fin[nid] + lat)
                ndep[u] -= 1
                if ndep[u] == 0:
                    avail[un["eng"]].append(u)
        self.seg_stats.append((lo, hi, max(fin.values()), dict(self._busy)))
        cur = max(fin, key=lambda k: fin[k])
        path = []
        while cur is not None:
            n = nodes[cur]
            path.append((cur, n["tag"], n["eng"], n["start"], n["cost"], n["why"][0]))
            cur = n["why"][1]
        self.crit.append(path[::-1])
        self._busy = {}
        return order

    def _lower(self):
        nodes = self.nodes
        bounds = [0] + [b for b in self.seg_bounds if b > 0]
        if bounds[-1] != len(nodes):
            bounds.append(len(nodes))
        items = {k: [] for k in ("pe", "act", "dve", "pool", "sp")}
        seq = {k: 0 for k in self.COMPUTE}
        waited = {k: {} for k in items}
        tok = {}
        dma_tot = [0] * self.NDMASEM
        dma_rr = 0
        dma_rr_sw = 0
        pool_out = []
        for si in range(len(bounds) - 1):
            lo, hi = bounds[si], bounds[si + 1]
            if hi <= lo:
                continue
            order = self._schedule_segment(lo, hi)
            for nid in order:
                n = nodes[nid]
                q = n["eng"]
                deps = [tok[d] for d in n["deps"] if d >= lo]
                if n["dma"]:
                    if q == "pool":
                        k = 16 + dma_rr_sw
                        dma_rr_sw = (dma_rr_sw + 1) % (self.NDMASEM - 16)
                    else:
                        k = dma_rr
                        dma_rr = (dma_rr + 1) % 16
                    key = f"dma{k}"
                    if dma_tot[k] > 0:
                        deps.append((key, dma_tot[k]))
                    if q == "pool":
                        while pool_out and sum(x for _, x in pool_out) + n["ndesc"] > self.POOL_DESC_CAP:
                            t, _ = pool_out.pop(0)
                            deps.append(t)
                    dma_tot[k] += 16
                    tok[nid] = (key, dma_tot[k])
                    semkey, inc = key, 16
                    if q == "pool":
                        pool_out.append((tok[nid], n["ndesc"]))
                else:
                    seq[q] += 1
                    tok[nid] = (q, seq[q])
                    semkey, inc = q, 1
                wd = waited[q]
                best = {}
                for (k, v) in deps:
                    if k == q:
                        if q == "pe":
                            continue
                        if q in ("act", "dve") and seq[q] - v >= SAME_ENG_DIST:
                            continue
                    if wd.get(k, 0) >= v:
                        continue
                    wd[k] = v
                    best[k] = max(best.get(k, 0), v)
                self.n_wait += len(best)
                items[q].append(("op", list(best.items()), n["fn"], semkey, inc))
            targets = [(k, seq[k]) for k in self.COMPUTE if seq[k] > 0]
            targets += [(f"dma{i}", v) for i, v in enumerate(dma_tot) if v > 0]
            for q in items:
                w = []
                for (k, v) in targets:
                    if k != q and waited[q].get(k, 0) < v:
                        waited[q][k] = v
                        w.append((k, v))
                if w:
                    items[q].append(("wait", w, None, None, 0))
        self.items = items

    def emit(self):
        nc = self.nc
        self._lower()
        with contextlib.ExitStack() as st:
            sems = {}
            for k in self.COMPUTE:
                sems[k] = st.enter_context(nc.semaphore(f"s_{k}"))
            for i in range(self.NDMASEM):
                sems[f"dma{i}"] = st.enter_context(nc.semaphore(f"s_dma{i}"))
            block = st.enter_context(nc.Block())
            items = self.items
            targets = {k: set() for k in self.COMPUTE}
            for lst in items.values():
                for (kind, waits, fn, semkey, inc) in lst:
                    for (k, v) in waits:
                        if k in targets:
                            targets[k].add(v)
            rank = {k: {v: i + 1 for i, v in enumerate(sorted(vs))} for k, vs in targets.items()}
            self.n_sig = {k: len(v) for k, v in targets.items()}

            def run(e, lst):
                idx = 0
                for (kind, waits, fn, semkey, inc) in lst:
                    for (k, v) in waits:
                        e.wait_ge(sems[k], rank[k][v] if k in rank else v)
                    if kind == "op":
                        ins = fn(e)
                        if semkey in rank:
                            idx += 1
                            if idx in rank[semkey]:
                                ins.then_inc(sems[semkey], 1)
                        else:
                            ins.then_inc(sems[semkey], inc)

            @block.tensor
            def _(e):
                run(e, items["pe"])

            @block.scalar
            def _(e):
                run(e, items["act"])

            @block.vector
            def _(e):
                run(e, items["dve"])

            @block.gpsimd
            def _(e):
                run(e, items["pool"])

            @block.sync
            def _(e):
                run(e, items["sp"])


def build_program(stop_after=None, taps=()):
    nc = bass.Bass("TRN2", target_bir_lowering=False)
    P = Prog(nc)
    P.init_psum()
    taps = set(taps)

    def din(name, shape):
        return nc.dram_tensor(name, list(shape), F32, kind="ExternalInput").ap()

    x_d = din("x", [S, D])
    w_in_d = din("w_in", [D, D_IN])
    w_out_d = din("w_out", [D, D])
    w_gate_d = din("w_gate", [D, DFF])
    w_up_d = din("w_up", [D, DFF])
    w_down_d = din("w_down", [DFF, D])
    norm1_d = din("norm1_w", [1, D])
    norm2_d = din("norm2_w", [1, D])
    gnw_d = din("gdn_norm_w", [1, 128])
    slw_d = din("subln_w", [1, 128])
    qnw_d = din("q_norm_w", [1, 64])
    knw_d = din("k_norm_w", [1, 64])
    lqk_d = din("lqk", [1, 256])
    alog_d = din("a_log", [1, 8])
    dtb_d = din("dt_bias", [1, 8])
    convw_d = din("conv_wl", [128, 96])
    cmat_d = din("cmat", [128, 5 * 128])
    out_d = nc.dram_tensor("out", [S, D], F32, kind="ExternalOutput").ap()
    tap_d = {}

    def tap(name, tt, sl, shape, reads):
        if name not in taps:
            return
        d = nc.dram_tensor("tap_" + name, list(shape), F32, kind="ExternalOutput").ap()
        tap_d[name] = d
        P.dma("sp", R=reads, out=d, in_=sl)

    cm = P.sbuf("cm", [128, 5, 128], F32)
    identb = P.sbuf("identb", [128, 128], BF16)
    onesb = P.sbuf("onesb", [128, 128], BF16)
    gnw = P.sbuf("gnw", [128, 128], F32)
    slw8 = P.sbuf("slw8", [128, 128], F32)
    qkw = P.sbuf("qkw", [128, 128], F32)
    lqk = P.sbuf("lqk", [128, 256], F32)
    alog = P.sbuf("alog", [128, 8], F32)
    dtb = P.sbuf("dtb", [128, 8], F32)
    convw = P.sbuf("convw", [128, 96], F32)
    sc = P.sbuf("sc", [128, 16], F32)
    negA = P.sbuf("negA", [128, 8], F32)
    cm05 = P.sbuf("cm05", [128, 64], F32)
    epsc = P.sbuf("epsc", [128, 1], F32)
    ident_f = cm[:, 0, :]
    ltri_f = cm[:, 1, :]
    ones_f = cm[:, 2, :]
    sel_f = [cm[:, 3, :], cm[:, 4, :]]
    CM = cm.s()

    P.dma("sp", W=[CM], out=cm[:].rearrange("p a b -> p (a b)"), in_=cmat_d[:, :])
    P.dma("pool", W=[identb.s()], out=identb[:], in_=cmat_d[:, 0:128])
    P.dma("pool", W=[onesb.s()], out=onesb[:], in_=cmat_d[:, 256:384])
    P.dma("sp", W=[gnw.s()], out=gnw[:], in_=gnw_d.partition_broadcast(128))
    P.dma("sp", W=[slw8.s()], out=slw8[:], in_=slw_d.partition_broadcast(128))
    P.dma("sp", W=[qkw.s()], out=qkw[:, 0:64], in_=qnw_d.partition_broadcast(128))
    P.dma("sp", W=[qkw.s()], out=qkw[:, 64:128], in_=knw_d.partition_broadcast(128))
    P.dma("sp", W=[lqk.s()], out=lqk[:], in_=lqk_d.partition_broadcast(128))
    P.dma("sp", W=[alog.s()], out=alog[:], in_=alog_d.partition_broadcast(128))
    P.dma("sp", W=[dtb.s()], out=dtb[:], in_=dtb_d.partition_broadcast(128))
    P.dma("sp", W=[convw.s()], out=convw[:], in_=convw_d[:, :])

    P.op("pool", "memset", W=[cm05.s()], ap=cm05[:], constant=-0.5)
    P.op("pool", "memset", W=[epsc.s()], ap=epsc[:], constant=EPS)
    P.op("pool", "tensor_scalar", R=[slw8.s()], W=[slw8.s()], out=slw8[:], in0=slw8[:], scalar1=0.8,
         scalar2=None, op0=ALU.mult)
    P.op("pool", "tensor_scalar", R=[qkw.s()], W=[qkw.s()], out=qkw[:, 0:64], in0=qkw[:, 0:64],
         scalar1=64 ** -0.5, scalar2=None, op0=ALU.mult)
    P.op("dve", "tensor_tensor", R=[lqk.s()], W=[lqk.s()], out=lqk[:, 0:64], in0=lqk[:, 0:64],
         in1=lqk[:, 64:128], op=ALU.mult)
    P.op("dve", "tensor_tensor", R=[lqk.s()], W=[lqk.s()], out=lqk[:, 128:192], in0=lqk[:, 128:192],
         in1=lqk[:, 192:256], op=ALU.mult)
    P.op("dve", "tensor_reduce", R=[lqk.s()], W=[sc.s()], out=sc[:, 0:1], in_=lqk[:, 0:64], axis=AX.X,
         op=ALU.add)
    P.op("dve", "tensor_reduce", R=[lqk.s(), sc.s()], W=[sc.s()], out=sc[:, 1:2], in_=lqk[:, 128:192],
         axis=AX.X, op=ALU.add)
    P.op("act", "activation", R=[sc.s()], W=[sc.s()], out=sc[:, 2:4], in_=sc[:, 0:2], func=AF.Exp)
    P.op("dve", "tensor_tensor", R=[sc.s()], W=[sc.s()], out=sc[:, 4:5], in0=sc[:, 2:3], in1=sc[:, 3:4],
         op=ALU.subtract)
    P.op("dve", "tensor_scalar", R=[sc.s()], W=[sc.s()], out=sc[:, 5:6], in0=sc[:, 4:5], scalar1=0.2,
         scalar2=-1.0, op0=ALU.add, op1=ALU.mult)
    P.op("act", "activation", R=[alog.s()], W=[negA.s()], out=negA[:], in_=alog[:], func=AF.Exp)
    P.op("dve", "tensor_scalar", R=[negA.s()], W=[negA.s()], out=negA[:], in0=negA[:], scalar1=-1.0,
         scalar2=None, op0=ALU.mult)
    nlam = sc[:, 5:6]

    def cut_here(tag):
        if stop_after == tag:
            d = nc.dram_tensor("tap_sc", [128, 16], F32, kind="ExternalOutput").ap()
            P.dma("sp", R=[sc.s()], out=d, in_=sc[:])
            P.barrier()
            P.emit()
            return True
        return False
    if cut_here("A0"):
        return nc, P

    uT = P.sbuf("uT", [128, 8, S], BF16)
    gcs = P.sbuf("gcs", [128, NT, 8], F32)
    lbs = P.sbuf("lbs", [128, NT, 8], F32)
    glt = P.sbuf("glt", [128, NT, 8], F32)
    egs = P.sbuf("egs", [128, NT, 2, 8], F32)

    mA = P.mark()
    xall = P.sbuf("xall", [128, NT, D], F32)
    wrow1 = P.sbuf("wrow1", [128, D], F32)
    P.dma("sp", W=[wrow1.s()], out=wrow1[:], in_=norm1_d.partition_broadcast(128))
    junk = P.sbuf("junk", [128, D], BF16)
    ssA = P.sbuf("ssA", [128, NT], F32)
    rstd1 = P.sbuf("rstd1", [128, NT], F32)
    xn32 = [P.sbuf(f"xn32_{i}", [128, D], F32) for i in range(2)]
    uT32 = [P.sbuf(f"uT32_{i}", [128, D], F32) for i in range(2)]
    wab = P.sbuf("wab", [128, 8, 16], F32)
    gab = P.sbuf("gab", [128, NT, 16], F32)
    arg = P.sbuf("arg", [128, NT, 16], F32)
    g32 = P.sbuf("g32", [128, NT, 8], F32)

    P.dma("sp", W=[wab.s()], out=wab[:],
          in_=w_in_d[:, COL["ga"]:COL["ga"] + 16].rearrange("(k p) c -> p k c", p=128))
    for tt in range(NT):
        P.dma("sp", W=[xall.s(tt)], out=xall[:, tt, :], in_=x_d[tt * 128:(tt + 1) * 128, :])
    for tt in range(NT):
        P.op("act", "activation", R=[xall.s(tt)], W=[junk.s(), ssA.s(tt)], out=junk[:], in_=xall[:, tt, :],
             func=AF.Square, accum_out=ssA[:, tt:tt + 1])
    if cut_here("A1"):
        return nc, P
    for tt in range(NT):
        P.op("dve", "tensor_scalar", R=[ssA.s(tt)], W=[rstd1.s(tt)], out=rstd1[:, tt:tt + 1], in0=ssA[:, tt:tt + 1],
             scalar1=1.0 / D, scalar2=EPS, op0=ALU.mult, op1=ALU.add)
        P.op("pool", "tensor_tensor", R=[rstd1.s(tt), cm05.s()], W=[rstd1.s(tt)], out=rstd1[:, tt:tt + 1],
             in0=rstd1[:, tt:tt + 1], in1=cm05[:, 0:1], op=ALU.pow)
    psG = P.psum_pin()
    for tt in range(NT):
        xn = xn32[tt % 2]
        u32 = uT32[tt % 2]
        P.op("dve", "scalar_tensor_tensor", R=[xall.s(tt), rstd1.s(tt), wrow1.s()], W=[xn.s()], out=xn[:],
             in0=xall[:, tt, :], scalar=rstd1[:, tt:tt + 1], in1=wrow1[:], op0=ALU.mult, op1=ALU.mult)
        for half in range(2):
            ps = P.psum()
            P.ops("pe", [("transpose", dict(out=ps[:, j * 128:(j + 1) * 128],
                                            in_=xn[:, (half * 4 + j) * 128:(half * 4 + j + 1) * 128],
                                            identity=ident_f)) for j in range(4)],
                  R=[xn.s(), CM], W=[ps.s()])
            if half == 0:
                P.op("act", "activation", R=[ps.s()], W=[u32.s(half)], out=u32[:, 0:512], in_=ps[:, 0:512],
                     func=AF.Copy)
            else:
                P.op("dve", "tensor_copy", R=[ps.s()], W=[u32.s(half)], out=u32[:, 512:1024], in_=ps[:, 0:512])
        P.op("pool", "tensor_copy", R=[u32.s(0), u32.s(1)], W=[uT.s(tt)], out=uT[:, :, tt * 128:(tt + 1) * 128],
             in_=u32[:].rearrange("p (k t) -> p k t", k=8))
        P.ops("pe", [("matmul", dict(out=psG[:, tt * 16:(tt + 1) * 16], lhsT=u32[:, kc * 128:(kc + 1) * 128],
                                     rhs=wab[:, kc, :], start=(kc == 0), stop=(kc == 7))) for kc in range(8)],
              R=[u32.s(0), u32.s(1), wab.s()], W=[psG.s()])
    if cut_here("A2"):
        return nc, P
    P.op("act", "activation", R=[psG.s()], W=[gab.s()], out=gab[:].rearrange("p a b -> p (a b)"),
         in_=psG[:, 0:NT * 16], func=AF.Copy)
    P.psum_unpin(psG)
    for tt in range(NT):
        P.op("dve", "tensor_tensor", R=[gab.s(), dtb.s()], W=[arg.s()], out=arg[:, tt, 0:8], in0=gab[:, tt, 0:8],
             in1=dtb[:], op=ALU.add)
    P.op("dve", "tensor_scalar", R=[gab.s()], W=[arg.s()], out=arg[:, :, 8:16], in0=gab[:, :, 8:16],
         scalar1=-1.0, scalar2=None, op0=ALU.mult)
    argf = arg[:].rearrange("p a b -> p (a b)")
    P.op("act", "activation", R=[arg.s()], W=[arg.s()], out=argf, in_=argf, func=AF.Exp)
    P.op("act", "activation", R=[arg.s()], W=[arg.s()], out=argf, in_=argf, func=AF.Ln, bias=1.0, scale=1.0)
    for tt in range(NT):
        P.op("dve", "tensor_tensor", R=[arg.s(), negA.s()], W=[g32.s()], out=g32[:, tt, :], in0=arg[:, tt, 0:8],
             in1=negA[:], op=ALU.mult)
    P.op("dve", "tensor_scalar", R=[arg.s()], W=[lbs.s()], out=lbs[:], in0=arg[:, :, 8:16], scalar1=-1.0,
         scalar2=None, op0=ALU.mult)
    if cut_here("A3"):
        return nc, P
    ps = P.psum()
    P.ops("pe", [("matmul", dict(out=ps[:, tt * 8:(tt + 1) * 8], lhsT=ltri_f, rhs=g32[:, tt, :], start=True,
                                 stop=True)) for tt in range(NT)], R=[g32.s(), CM], W=[ps.s()])
    P.op("act", "activation", R=[ps.s()], W=[gcs.s()], out=gcs[:].rearrange("p a b -> p (a b)"),
         in_=ps[:, 0:NT * 8], func=AF.Copy)
    ps = P.psum()
    P.ops("pe", [("matmul", dict(out=ps[:, (tt * 2 + c) * 8:(tt * 2 + c + 1) * 8], lhsT=sel_f[c],
                                 rhs=gcs[:, tt, :], start=True, stop=True))
                 for tt in range(NT) for c in range(2)], R=[gcs.s(), CM], W=[ps.s()])
    P.op("act", "activation", R=[ps.s()], W=[egs.s()], out=egs[:].rearrange("p a c b -> p (a c b)"),
         in_=ps[:, 0:NT * 16], func=AF.Exp)
    if cut_here("A4"):
        return nc, P
    psv = ps[:, 0:NT * 16].rearrange("p (a c b) -> p a c b", a=NT, c=2)
    P.op("dve", "tensor_copy", R=[ps.s()], W=[glt.s()], out=glt[:, :, :], in_=psv[:, :, 1, :])
    if cut_here("A5"):
        return nc, P
    tap("gcs", gcs, gcs[:], [128, NT, 8], [gcs.s()])
    tap("lbs", lbs, lbs[:], [128, NT, 8], [lbs.s()])
    tap("egs", egs, egs[:], [128, NT, 2, 8], [egs.s()])
    tap("glt", glt, glt[:], [128, NT, 8], [glt.s()])
    P.barrier()
    P.release(mA)

    if stop_after == "A":
        u_dbg = nc.dram_tensor("tap_uT", [128, 8, S], BF16, kind="ExternalOutput").ap()
        P.dma("sp", R=[uT.s(tt) for tt in range(NT)], out=u_dbg, in_=uT[:])
        P.barrier()
        P.emit()
        return nc, P


    MIXT_OFF = (P.sb_ptr + 63) // 64 * 64
    mixT = P.sbuf("mixT", [128, H, S], BF16)
    mB = P.mark()
    wh = [P.sbuf("wh0", [128, 8, 9 * 128], BF16)] * 2
    xc = P.sbuf("xc", [128, 2, S + 4], BF16)
    dg = P.sbuf("dg", [128, 12, 128], BF16)
    kqv = P.sbuf("kqv", [128, 3, S], BF16)
    sgt = [P.sbuf("sgt0", [128, 512], BF16)]
    GAb = [P.sbuf(f"GA{i}", [128, NT, 128], BF16) for i in range(2)]
    GBt = P.sbuf("GBt", [128, NT, 128], BF16)
    vd = P.sbuf("vd", [128, NT, 130], BF16)
    dqT = P.sbuf("dqT", [128, S], BF16)
    dkT = P.sbuf("dkT", [128, S], BF16)
    tz = [P.sbuf(f"tz{i}", [128, 384], BF16) for i in range(2)]
    sq4 = [P.sbuf(f"sq4{i}", [128, 256], BF16) for i in range(2)]
    rs4 = [P.sbuf(f"rs4{i}", [128, 4], F32) for i in range(2)]
    qkn = [P.sbuf(f"qkn{i}", [128, 256], BF16) for i in range(2)]
    t1 = [P.sbuf(f"t1{i}", [128, 128], BF16) for i in range(2)]
    t2 = [P.sbuf(f"t2{i}", [128, 128], BF16) for i in range(2)]
    ssq = P.sbuf("ssq", [128, NT, 2], F32)
    tsc = P.sbuf("tsc", [128, 10, NT], F32)
    kbg = P.sbuf("kbg", [128, NT, 128], BF16)
    kdec = P.sbuf("kdec", [128, NT, 128], BF16)
    vb = P.sbuf("vb", [128, NT, 128], BF16)
    dgf = [P.sbuf(f"dgf{i}", [128, 256], F32) for i in range(2)] * 2
    eaq = [P.sbuf(f"eaq{i}", [128, 256], F32) for i in range(2)] * 2
    wlv = [[P.sbuf(f"wlv{i}_{k}", [128, 384], BF16) for k in range(2)] for i in range(PREP_B)]
    aqt = P.sbuf("aqt", [128, NT, 128], BF16)
    u32 = P.sbuf("u32", [128, NT, 128], F32)
    wT = P.sbuf("wT", [128, S], BF16)
    S32 = P.sbuf("S32", [128, 128], F32)
    Sb = P.sbuf("Sb", [128, 128], BF16)
    vn = P.sbuf("vn", [128, 2, 128], BF16)
    o1s = P.sbuf("o1s", [128, 2, 128], F32)
    oa = P.sbuf("oa", [128, NT, 128], BF16)
    sso = P.sbuf("sso", [128, NT], F32)
    junk2 = P.sbuf("junk2", [128, 128], BF16)
    rso = P.sbuf("rso", [128, NT], F32)
    pt = [P.sbuf(f"pt{i}", [128, 512], BF16) for i in range(4)]
    ob = [P.sbuf(f"ob{i}", [128, 128], F32) for i in range(4)]
    tb = [P.sbuf(f"tb{i}", [128, 128], F32) for i in range(4)]
    rsum = [P.sbuf(f"rsum{i}", [128, 4], F32) for i in range(4)]
    ob16 = P.sbuf("ob16", [128, NT, 128], BF16)
    rs16 = P.sbuf("rs16", [128, NT], F32)
    junk4 = P.sbuf("junk4", [128, 128], BF16)
    rr16 = P.sbuf("rr16", [128, NT], F32)
    TS_LRQ, TS_LRK, TS_CA, TS_CQ, TS_BJ, TS_SK1, TS_SK2, TS_BETA, TS_F, TS_TMP = range(10)

    P.op("pool", "memset", W=[xc.s(g, "pad") for g in range(2)], ap=xc[:, :, 0:3], constant=0.0)
    P.op("pool", "memset", W=[vd.s(tt) for tt in range(NT)], ap=vd[:, :, 128:130], constant=1.0)

    def load_head_weights(h):
        w = wh[h % 2]
        for g, nm in enumerate(HEAD_GROUPS):
            c0 = COL[nm] + h * 128
            P.dma("pool", W=[w.s(g)], out=w[:, :, g * 128:(g + 1) * 128],
                  in_=w_in_d[:, c0:c0 + 128].rearrange("(k p) c -> p k c", p=128))

    evac_rr = [0]

    def evac_copy(out, in_, R, W):
        evac_rr[0] ^= 1
        if evac_rr[0]:
            P.op("act", "activation", R=R, W=W, out=out, in_=in_, func=AF.Copy)
        else:
            P.op("dve", "tensor_copy", R=R, W=W, out=out, in_=in_)

    def inproj_T(h):
        w = wh[h % 2]
        GA = GAb[h % 2]
        pst = {}

        def s12(tt):
            i2 = tt % 2
            psA = P.psum()
            psB = P.psum()
            pst[tt] = psA
            lhs = lambda kc: uT[:, kc, tt * 128:(tt + 1) * 128]
            P.ops("pe", [("matmul", dict(out=psA[:, 0:512], lhsT=lhs(kc), rhs=w[:, kc, 3 * 128:7 * 128],
                                         start=(kc == 0), stop=(kc == 7))) for kc in range(8)],
                  R=[uT.s(tt)] + [w.s(g) for g in (3, 4, 5, 6)], W=[psA.s()])
            P.ops("pe", [("matmul", dict(out=psB[:, 0:256], lhsT=lhs(kc), rhs=w[:, kc, 7 * 128:9 * 128],
                                         start=(kc == 0), stop=(kc == 7))) for kc in range(8)],
                  R=[uT.s(tt)] + [w.s(g) for g in (7, 8)], W=[psB.s()])
            z = tz[i2]
            P.op("act", "activation", R=[psA.s()], W=[z.s()], out=z[:, 0:128], in_=psA[:, 0:128], func=AF.Sigmoid)
            P.op("act", "activation", R=[psB.s(), z.s()], W=[z.s()], out=z[:, 128:384], in_=psB[:, 0:256],
                 func=AF.Sigmoid)
            P.op("act", "activation", R=[psA.s()], W=[sq4[i2].s()], out=sq4[i2][:], in_=psA[:, 128:384],
                 func=AF.Square)
            P.op("act", "activation", R=[psA.s()], W=[vd.s(tt)], out=vd[:, tt, 0:128], in_=psA[:, 384:512],
                 func=AF.Copy)
            P.op("dve", "tensor_tensor", R=[psA.s(), z.s()], W=[t1[i2].s()], out=t1[i2][:], in0=psA[:, 0:128],
                 in1=z[:, 0:128], op=ALU.mult)
            P.op("pool", "tensor_tensor", R=[z.s(), gnw.s()], W=[t2[i2].s()], out=t2[i2][:], in0=z[:, 128:256],
                 in1=gnw[:], op=ALU.mult)
            P.op("pool", "tensor_tensor", R=[t1[i2].s(), t2[i2].s()], W=[GA.s(tt)], out=GA[:, tt, :],
                 in0=t1[i2][:], in1=t2[i2][:], op=ALU.mult)
            P.op("pool", "tensor_tensor", R=[z.s(), slw8.s()], W=[GBt.s(tt)], out=GBt[:, tt, :],
                 in0=z[:, 256:384], in1=slw8[:], op=ALU.mult)
            r4 = rs4[i2]
            P.op("dve", "tensor_reduce", R=[sq4[i2].s()], W=[r4.s()], out=r4[:],
                 in_=sq4[i2][:].rearrange("p (a b) -> p a b", a=4), axis=AX.X, op=ALU.add)
            P.op("dve", "tensor_scalar", R=[r4.s()], W=[r4.s()], out=r4[:], in0=r4[:], scalar1=1.0 / 64,
                 scalar2=EPS, op0=ALU.mult, op1=ALU.add)
            P.op("pool", "tensor_tensor", R=[r4.s(), cm05.s()], W=[r4.s()], out=r4[:], in0=r4[:],
                 in1=cm05[:, 0:4], op=ALU.pow)

        def s3(tt):
            i2 = tt % 2
            psA = pst.pop(tt)
            r4 = rs4[i2]
            qn = qkn[i2]
            for sgi in range(4):
                wsl = qkw[:, 0:64] if sgi < 2 else qkw[:, 64:128]
                P.op("dve", "scalar_tensor_tensor", R=[psA.s(), r4.s(), qkw.s()], W=[qn.s()],
                     out=qn[:, sgi * 64:(sgi + 1) * 64], in0=psA[:, 128 + sgi * 64:128 + (sgi + 1) * 64],
                     scalar=r4[:, sgi:sgi + 1], in1=wsl, op0=ALU.mult, op1=ALU.mult)
            ps = P.psum()
            psb = ps.ap.bitcast(BF16)
            P.ops("pe", [("transpose", dict(out=psb[:, j * 128:(j + 1) * 128], in_=qn[:, j * 128:(j + 1) * 128],
                                            identity=identb[:])) for j in range(2)],
                  R=[qn.s(), identb.s()], W=[ps.s()])
            P.op("act", "activation", R=[ps.s()], W=[dqT.s(tt)], out=dqT[:, tt * 128:(tt + 1) * 128],
                 in_=psb[:, 0:128], func=AF.Copy)
            P.op("dve", "tensor_copy", R=[ps.s()], W=[dkT.s(tt)], out=dkT[:, tt * 128:(tt + 1) * 128],
                 in_=psb[:, 128:256])

        for tt in range(NT):
            s12(tt)
            s3(tt)
            yield 2.0

    def inproj_F(h):
        w = wh[h % 2]
        for g in range(3):
            for j in range(4):
                col = (g * 8 + h) * 4 + j
                P.op("act", "activation", R=[identb.s(), convw.s()], W=[dg.s(g)], out=dg[:, g * 4 + j, :],
                     in_=identb[:], func=AF.Copy, scale=convw[:, col:col + 1])
        for g in range(3):
            xs = g % 2
            for tc in range(4):
                ps = P.psum()
                P.ops("pe", [("matmul", dict(out=ps[:, 0:512], lhsT=w[:, kc, g * 128:(g + 1) * 128],
                                             rhs=uT[:, kc, tc * 512:(tc + 1) * 512], start=(kc == 0),
                                             stop=(kc == 7))) for kc in range(8)],
                      R=[w.s(g)] + [uT.s(tt) for tt in range(4 * tc, 4 * tc + 4)], W=[ps.s()])
                evac_copy(xc[:, xs, 3 + tc * 512:3 + (tc + 1) * 512], ps[:, 0:512], [ps.s()], [xc.s(xs, tc)])
            for tc in range(4):
                ps = P.psum()
                R = [dg.s(g), xc.s(xs, tc)] + ([xc.s(xs, tc - 1)] if tc > 0 else [xc.s(xs, "pad")])
                P.ops("pe", [("matmul", dict(out=ps[:, 0:512], lhsT=dg[:, g * 4 + j, :],
                                             rhs=xc[:, xs, tc * 512 + j:tc * 512 + j + 512], start=(j == 0),
                                             stop=(j == 3))) for j in range(4)], R=R, W=[ps.s()])
                sg = sgt[0]
                P.op("act", "activation", R=[ps.s()], W=[sg.s()], out=sg[:], in_=ps[:, 0:512], func=AF.Sigmoid)
                P.op("dve", "tensor_tensor", R=[ps.s(), sg.s()], W=[kqv.s(g, tc)],
                     out=kqv[:, g, tc * 512:(tc + 1) * 512], in0=ps[:, 0:512], in1=sg[:], op=ALU.mult)

    def tsl(i):
        return tsc[:, i, :]

    def gdn_scalars(h):
        for g in range(2):
            P.op("act", "activation", R=[kqv.s(g, tc) for tc in range(4)], W=[xc.s(g, tc) for tc in range(4)],
                 out=xc[:, g, 3:3 + S], in_=kqv[:, g, :], func=AF.Square)
        ps = P.psum()
        P.ops("pe", [("matmul", dict(out=ps[:, tt * 2 + g:tt * 2 + g + 1],
                                     lhsT=xc[:, g, 3 + tt * 128:3 + (tt + 1) * 128],
                                     rhs=onesb[:, 0:1], start=True, stop=True))
                     for tt in range(NT) for g in range(2)],
              R=[xc.s(g, tc) for g in range(2) for tc in range(4)] + [onesb.s()], W=[ps.s()])
        P.op("act", "activation", R=[ps.s()], W=[ssq.s()], out=ssq[:].rearrange("p a b -> p (a b)"),
             in_=ps[:, 0:2 * NT], func=AF.Ln, bias=epsc[:, 0:1], scale=1.0)
        T = tsc.s()
        gch = gcs[:, :, h]
        lbh = lbs[:, :, h]
        glh = glt[:, :, h]
        P.op("dve", "tensor_scalar", R=[ssq.s()], W=[T], out=tsl(TS_LRQ), in0=ssq[:, :, 0], scalar1=-0.5,
             scalar2=-0.5 * float(np.log(128.0)), op0=ALU.mult, op1=ALU.add)
        P.op("dve", "tensor_scalar", R=[ssq.s(), T], W=[T], out=tsl(TS_LRK), in0=ssq[:, :, 1], scalar1=-0.5,
             scalar2=None, op0=ALU.mult)
        P.op("dve", "tensor_tensor", R=[gcs.s(), T], W=[T], out=tsl(TS_CQ), in0=gch, in1=tsl(TS_LRQ), op=ALU.add)
        P.op("dve", "tensor_tensor", R=[gcs.s(), T], W=[T], out=tsl(TS_TMP), in0=gch, in1=tsl(TS_LRK), op=ALU.add)
        P.op("dve", "tensor_tensor", R=[lbs.s(), T], W=[T], out=tsl(TS_CA), in0=tsl(TS_TMP), in1=lbh, op=ALU.add)
        P.op("dve", "tensor_tensor", R=[gcs.s(), T], W=[T], out=tsl(TS_BJ), in0=tsl(TS_LRK), in1=gch,
             op=ALU.subtract)
        P.op("act", "activation", R=[T], W=[T], out=tsl(TS_SK1), in_=tsl(TS_CA), func=AF.Exp)
        P.op("dve", "tensor_tensor", R=[glt.s(), T], W=[T], out=tsl(TS_TMP), in0=tsl(TS_BJ), in1=glh, op=ALU.add)
        P.op("act", "activation", R=[T], W=[T], out=tsl(TS_SK2), in_=tsl(TS_TMP), func=AF.Exp)
        P.op("act", "activation", R=[lbs.s(), T], W=[T], out=tsl(TS_BETA), in_=lbh, func=AF.Exp)
        P.op("act", "activation", R=[T], W=[T], out=tsl(TS_F), in_=tsl(TS_CQ), func=AF.Exp)

    def gdn_prep(h):
        T = tsc.s()
        for t0 in range(0, NT, PREP_B):
            tiles = list(range(t0, t0 + PREP_B))
            for i, tt in enumerate(tiles):
                tc = tt // 4
                tsl_ = slice(tt * 128, (tt + 1) * 128)
                ps = P.psum()
                psb = ps.ap.bitcast(BF16)
                P.ops("pe", [("transpose", dict(out=psb[:, 0:128], in_=kqv[:, 1, tsl_], identity=identb[:])),
                             ("transpose", dict(out=psb[:, 128:256], in_=kqv[:, 2, tsl_], identity=identb[:]))],
                      R=[kqv.s(1, tc), kqv.s(2, tc), identb.s()], W=[ps.s()])
                P.op("act", "activation", R=[ps.s(), T], W=[kbg.s(tt)], out=kbg[:, tt, :], in_=psb[:, 0:128],
                     func=AF.Copy, scale=tsc[:, TS_SK1, tt:tt + 1])
                P.op("dve", "tensor_scalar", R=[ps.s(), T], W=[kdec.s(tt)], out=kdec[:, tt, :], in0=psb[:, 0:128],
                     scalar1=tsc[:, TS_SK2, tt:tt + 1], scalar2=None, op0=ALU.mult)
                P.op("act", "activation", R=[ps.s(), T], W=[vb.s(tt)], out=vb[:, tt, :], in_=psb[:, 128:256],
                     func=AF.Copy, scale=tsc[:, TS_BETA, tt:tt + 1])
                d = dgf[i % 2]
                P.op("act", "activation", R=[CM, T], W=[d.s()], out=d[:, 0:128], in_=ident_f, func=AF.Copy,
                     scale=tsc[:, TS_CA, tt:tt + 1])
                P.op("act", "activation", R=[CM, T, d.s()], W=[d.s()], out=d[:, 128:256], in_=ident_f, func=AF.Copy,
                     scale=tsc[:, TS_CQ, tt:tt + 1])
                psE = P.psum()
                P.ops("pe", [("matmul", dict(out=psE[:, 0:128], lhsT=ones_f, rhs=d[:, 0:128], start=True, stop=True)),
                             ("matmul", dict(out=psE[:, 128:256], lhsT=ones_f, rhs=d[:, 128:256], start=True,
                                             stop=True))], R=[CM, d.s()], W=[psE.s()])
                e = eaq[i % 2]
                P.op("act", "activation", R=[psE.s(), T], W=[e.s()], out=e[:], in_=psE[:, 0:256], func=AF.Exp,
                     bias=tsc[:, TS_BJ, tt:tt + 1], scale=1.0)
                P.op("pool", "affine_select", R=[e.s()], W=[e.s()], out=e[:, 0:128], in_=e[:, 0:128],
                     pattern=[[1, 128]], compare_op=ALU.is_gt, fill=0.0, base=0, channel_multiplier=-1)
                P.op("pool", "affine_select", R=[e.s()], W=[e.s()], out=e[:, 128:256], in_=e[:, 128:256],
                     pattern=[[1, 128]], compare_op=ALU.is_ge, fill=0.0, base=0, channel_multiplier=-1)
                psK = P.psum()
                P.ops("pe", [("matmul", dict(out=psK[:, 0:128], lhsT=kqv[:, 1, tsl_], rhs=kqv[:, 1, tsl_],
                                             start=True, stop=True)),
                             ("matmul", dict(out=psK[:, 128:256], lhsT=kqv[:, 1, tsl_], rhs=kqv[:, 0, tsl_],
                                             start=True, stop=True))],
                      R=[kqv.s(0, tc), kqv.s(1, tc)], W=[psK.s()])
                w0 = wlv[i][0]
                P.op("dve", "scalar_tensor_tensor", R=[psK.s(), e.s()], W=[w0.s()], out=w0[:, 0:128],
                     in0=psK[:, 0:128], scalar=-1.0, in1=e[:, 0:128], op0=ALU.mult, op1=ALU.mult)
                P.op("dve", "tensor_tensor", R=[psK.s(), e.s()], W=[aqt.s(tt)], out=aqt[:, tt, :],
                     in0=psK[:, 128:256], in1=e[:, 128:256], op=ALU.mult)
                P.op("pool", "tensor_copy", R=[identb.s(), w0.s()], W=[w0.s()], out=w0[:, 128:256], in_=identb[:])
                psN = P.psum()
                psNb = psN.ap.bitcast(BF16)
                P.op("pe", "transpose", R=[w0.s(), identb.s()], W=[psN.s()], out=psNb[:, 0:128], in_=w0[:, 0:128],
                     identity=identb[:])
                P.op("act", "activation", R=[psN.s(), w0.s()], W=[w0.s()], out=w0[:, 256:384], in_=psNb[:, 0:128],
                     func=AF.Copy)
                yield 4.0
            for lvl in range(7):
                for i, tt in enumerate(tiles):
                    wc = wlv[i][lvl % 2]
                    wn = wlv[i][(lvl + 1) % 2]
                    ps = P.psum()
                    Mk, Rk, Nk = wc[:, 0:128], wc[:, 128:256], wc[:, 256:384]
                    calls = []
                    if lvl <= 4:
                        calls.append(("matmul", dict(out=ps[:, 0:256], lhsT=Nk, rhs=wc[:, 0:256], start=True, stop=True)))
                    else:
                        calls.append(("matmul", dict(out=ps[:, 128:256], lhsT=Nk, rhs=Rk, start=True, stop=True)))
                    if lvl <= 5:
                        calls.append(("matmul", dict(out=ps[:, 256:384], lhsT=Mk, rhs=Nk, start=True, stop=True)))
                    P.ops("pe", calls, R=[wc.s()], W=[ps.s()])
                    P.op("dve", "tensor_tensor", R=[ps.s(), wc.s()], W=[wn.s()], out=wn[:, 128:256],
                         in0=ps[:, 128:256], in1=Rk, op=ALU.add)
                    if lvl <= 4:
                        o3 = wn[:, 0:384].rearrange("p (a b) -> p a b", a=3)[:, 0:3:2, :]
                        i3 = ps[:, 0:384].rearrange("p (a b) -> p a b", a=3)[:, 0:3:2, :]
                    elif lvl == 5:
                        o3, i3 = wn[:, 256:384], ps[:, 256:384]
                    if lvl <= 5:
                        if i % 2 == 0:
                            P.op("act", "activation", R=[ps.s(), wn.s()], W=[wn.s()], out=o3, in_=i3, func=AF.Copy)
                        else:
                            P.op("dve", "tensor_copy", R=[ps.s(), wn.s()], W=[wn.s()], out=o3, in_=i3)
                yield 5.0
            for i, tt in enumerate(tiles):
                wf = wlv[i][1]
                ps = P.psum()
                P.ops("pe", [("matmul", dict(out=ps[:, 0:128], lhsT=wf[:, 128:256], rhs=vb[:, tt, :], start=True,
                                             stop=True)),
                             ("matmul", dict(out=ps[:, 128:256], lhsT=kbg[:, tt, :], rhs=wf[:, 128:256], start=True,
                                             stop=True))], R=[wf.s(), vb.s(tt), kbg.s(tt)], W=[ps.s()])
                P.op("act", "activation", R=[ps.s()], W=[u32.s(tt)], out=u32[:, tt, :], in_=ps[:, 0:128], func=AF.Copy)
                P.op("dve", "tensor_copy", R=[ps.s()], W=[wT.s(tt)], out=wT[:, tt * 128:(tt + 1) * 128],
                     in_=ps[:, 128:256])
            yield 2.0

    def gdn_recurrence(h):
        T = tsc.s()
        GA = GAb[h % 2]
        P.op("pool", "memset", W=[S32.s()], ap=S32[:], constant=0.0)
        P.op("pool", "memset", W=[Sb.s()], ap=Sb[:], constant=0.0)
        for tt in range(NT):
            tc = tt // 4
            for c in (1,):
                rows = slice(0, 128)
                cols = slice(tt * 128, (tt + 1) * 128)
                ps = P.psum()
                P.ops("pe", [("matmul", dict(out=ps[rows, 0:128], lhsT=wT[:, cols], rhs=Sb[:], start=True, stop=True)),
                             ("matmul", dict(out=ps[rows, 128:256], lhsT=kqv[:, 0, cols], rhs=Sb[:], start=True,
                                             stop=True))], R=[wT.s(tt), kqv.s(0, tc), Sb.s()], W=[ps.s()])
                P.op("dve", "tensor_tensor", R=[ps.s(), u32.s(tt)], W=[vn.s(tt % 2, c)], out=vn[rows, tt % 2, :],
                     in0=u32[rows, tt, :], in1=ps[rows, 0:128], op=ALU.subtract)
                P.op("act", "activation", R=[ps.s(), T], W=[o1s.s(tt % 2, c)], out=o1s[rows, tt % 2, :], in_=ps[rows, 128:256],
                     func=AF.Copy, scale=tsc[rows, TS_F, tt:tt + 1])
                psS = P.psum()
                P.op("pe", "matmul", R=[kdec.s(tt), vn.s(tt % 2, c)], W=[psS.s()], out=psS[:, 0:128],
                     lhsT=kdec[rows, tt, :], rhs=vn[rows, tt % 2, :], start=True, stop=True)
                eg = egs[:, tt, c, h:h + 1]
                P.op("dve", "scalar_tensor_tensor", R=[psS.s(), S32.s(), egs.s()], W=[Sb.s()], out=Sb[:],
                     in0=S32[:], scalar=eg, in1=psS[:, 0:128], op0=ALU.mult, op1=ALU.add)
                P.op("dve", "scalar_tensor_tensor", R=[psS.s(), S32.s(), egs.s()], W=[S32.s()], out=S32[:],
                     in0=S32[:], scalar=eg, in1=psS[:, 0:128], op0=ALU.mult, op1=ALU.add)
                yield 3.0
            ps = P.psum()
            P.op("pe", "matmul", R=[aqt.s(tt), vn.s(tt % 2, 1)], W=[ps.s()], out=ps[:, 0:128],
                 lhsT=aqt[:, tt, :], rhs=vn[:, tt % 2, :], start=True, stop=True)
            P.op("dve", "tensor_tensor", R=[ps.s(), o1s.s(tt % 2, 1)], W=[oa.s(tt)], out=oa[:, tt, :],
                 in0=ps[:, 0:128], in1=o1s[:, tt % 2, :], op=ALU.add)
            P.op("act", "activation", R=[oa.s(tt)], W=[junk2.s(), sso.s(tt)], out=junk2[:], in_=oa[:, tt, :],
                 func=AF.Square, accum_out=sso[:, tt:tt + 1])
        allso = [sso.s(tt) for tt in range(NT)]
        P.op("dve", "tensor_scalar", R=allso, W=[rso.s()], out=rso[:], in0=sso[:], scalar1=1.0 / 128,
             scalar2=EPS, op0=ALU.mult, op1=ALU.add)
        P.op("pool", "tensor_tensor", R=[rso.s(), cm05.s()], W=[rso.s()], out=rso[:], in0=rso[:],
             in1=cm05[:, 0:NT], op=ALU.pow)
        for tt in range(NT):
            P.op("dve", "scalar_tensor_tensor", R=[oa.s(tt), rso.s(), GA.s(tt)], W=[oa.s(tt)], out=oa[:, tt, :],
                 in0=oa[:, tt, :], scalar=rso[:, tt:tt + 1], in1=GA[:, tt, :], op0=ALU.mult, op1=ALU.mult)

    def attention(h):
        ptk = [0]
        for qc in range(4):
            accA = P.psum_pin()
            accB = P.psum_pin()
            accC = P.psum_pin()
            accs = (accA, accB, accC)

            def acc_ap(c, ql):
                if ql < 3:
                    return (accA, accB)[c], ql * 129
                return accC, c * 129
            first = {id(a): True for a in accs}
            nkb = 4 * qc + 4
            steps = [(kb, c) for kb in range(nkb) for c in range(2)]
            pbuf = {}

            def emit_qk(i):
                kb, c = steps[i]
                ql0 = max(0, kb - 4 * qc)
                ncol = (4 - ql0) * 128
                q0 = qc * 512 + ql0 * 128
                ps = P.psum()
                P.op("pe", "matmul", R=[dkT.s(kb)] + [dqT.s(4 * qc + ql) for ql in range(ql0, 4)], W=[ps.s()],
                     out=ps[:, 0:ncol], lhsT=dkT[c * 64:(c + 1) * 64, kb * 128:(kb + 1) * 128],
                     rhs=dqT[c * 64:(c + 1) * 64, q0:q0 + ncol], start=True, stop=True)
                p = pt[ptk[0] % len(pt)]
                ptk[0] += 1
                pbuf[i] = p
                P.op("act", "activation", R=[ps.s()], W=[p.s()], out=p[:, 0:ncol], in_=ps[:, 0:ncol], func=AF.Exp)
                if kb >= 4 * qc:
                    P.op("pool", "affine_select", R=[p.s()], W=[p.s()], out=p[:, 0:128], in_=p[:, 0:128],
                         pattern=[[1, 128]], compare_op=ALU.is_ge, fill=0.0, base=0, channel_multiplier=-1)

            def emit_pv(i):
                kb, c = steps[i]
                ql0 = max(0, kb - 4 * qc)
                p = pbuf.pop(i)
                calls = []
                touched = []
                for ql in range(ql0, 4):
                    a, off = acc_ap(c, ql)
                    st = first[id(a)]
                    first[id(a)] = False
                    calls.append(("matmul", dict(out=a[:, off:off + 129],
                                                 lhsT=p[:, (ql - ql0) * 128:(ql - ql0 + 1) * 128],
                                                 rhs=vd[:, kb, 0:129], start=st, stop=False,
                                                 skip_group_check=True)))
                    if a not in touched:
                        touched.append(a)
                P.ops("pe", calls, R=[p.s(), vd.s(kb)], W=[a.s() for a in touched])

            LOOK = ATT_LOOK
            nst = len(steps) + LOOK
            for i in range(nst):
                if i < len(steps):
                    emit_qk(i)
                if i - LOOK >= 0:
                    emit_pv(i - LOOK)
                yield 1.8
            QL = range(4)
            accp = {ql: (acc_ap(0, ql), acc_ap(1, ql)) for ql in QL}
            for ql in QL:
                (a0, off0), (a1, off1) = accp[ql]
                rs = rsum[ql]
                P.op("dve", "reciprocal", R=[a0.s()], W=[rs.s()], out=rs[:, 0:1], in_=a0[:, off0 + 128:off0 + 129])
                P.op("dve", "reciprocal", R=[a1.s(), rs.s()], W=[rs.s()], out=rs[:, 1:2],
                     in_=a1[:, off1 + 128:off1 + 129])
            for ql in QL:
                (a0, off0), (a1, off1) = accp[ql]
                rs = rsum[ql]
                P.op("act", "activation", R=[a0.s(), rs.s()], W=[ob[ql].s()], out=ob[ql][:], in_=a0[:, off0:off0 + 128],
                     func=AF.Copy, scale=rs[:, 0:1])
                P.op("act", "activation", R=[a1.s(), rs.s()], W=[tb[ql].s()], out=tb[ql][:], in_=a1[:, off1:off1 + 128],
                     func=AF.Copy, scale=rs[:, 1:2])
            for a in accs:
                P.psum_unpin(a)
            yield 1.0
            for ql in QL:
                qb = 4 * qc + ql
                P.op("dve", "scalar_tensor_tensor", R=[ob[ql].s(), tb[ql].s(), sc.s()], W=[ob16.s(qb)],
                     out=ob16[:, qb, :], in0=tb[ql][:], scalar=nlam, in1=ob[ql][:], op0=ALU.mult, op1=ALU.add)
                P.op("act", "activation", R=[ob16.s(qb)], W=[junk4.s(), rs16.s(qb)], out=junk4[:], in_=ob16[:, qb, :],
                     func=AF.Square, accum_out=rs16[:, qb:qb + 1])
            csl = slice(4 * qc, 4 * qc + 4)
            P.op("dve", "tensor_scalar", R=[rs16.s(4 * qc + ql) for ql in QL], W=[rr16.s(qc)], out=rr16[:, csl],
                 in0=rs16[:, csl], scalar1=1.0 / 128, scalar2=EPS, op0=ALU.mult, op1=ALU.add)
            P.op("pool", "tensor_tensor", R=[rr16.s(qc), cm05.s()], W=[rr16.s(qc)], out=rr16[:, csl], in0=rr16[:, csl],
                 in1=cm05[:, 0:4], op=ALU.pow)
            for ql in QL:
                qb = 4 * qc + ql
                P.op("dve", "scalar_tensor_tensor", R=[ob16.s(qb), rr16.s(qc), GBt.s(qb)], W=[ob16.s(qb)],
                     out=ob16[:, qb, :], in0=ob16[:, qb, :], scalar=rr16[:, qb:qb + 1], in1=GBt[:, qb, :],
                     op0=ALU.mult, op1=ALU.mult)
            yield 1.0

    def attn_post(h):
        for q0 in range(0, NT, 4):
            QB = range(q0, q0 + 4)
            for qb in QB:
                m = ob[qb % 4]
                P.op("pool", "tensor_tensor", R=[ob16.s(qb), oa.s(qb)], W=[m.s()], out=m[:], in0=ob16[:, qb, :],
                     in1=oa[:, qb, :], op=ALU.add)
            for qb in QB:
                m = ob[qb % 4]
                ps = P.psum()
                P.op("pe", "transpose", R=[m.s(), CM], W=[ps.s()], out=ps[:, 0:128], in_=m[:], identity=ident_f)
                evac_copy(mixT[:, h, qb * 128:(qb + 1) * 128], ps[:, 0:128], [ps.s()], [mixT.s(h, qb)])

    def interleave(ga, gb):
        ca = cb = 0.0
        da = db = False
        while not (da and db):
            if not da and (db or ca <= cb):
                try:
                    P.tag = "gdn"
                    ca += next(ga)
                except StopIteration:
                    da = True
            elif not db:
                try:
                    P.tag = "attn+inT"
                    cb += next(gb) * BSCALE
                except StopIteration:
                    db = True

    def chain(*gens):
        for g in gens:
            yield from g

    heads = list(range(NHEADS)) if stop_after not in ("B0",) else [0]
    load_head_weights(heads[0])
    P.tag = "inproj"
    for _ in inproj_T(heads[0]):
        pass
    inproj_F(heads[0])
    for hi, h in enumerate(heads):
        nxt = heads[hi + 1] if hi + 1 < len(heads) else None
        if nxt is not None:
            load_head_weights(nxt)
        P.mark_phase(f"h{h}.mix")
        P.tag = "mix"
        gdn_scalars(h)
        sb = [attention(h)] + ([inproj_T(nxt)] if nxt is not None else [])
        interleave(chain(gdn_prep(h), gdn_recurrence(h)), chain(*sb))
        P.tag = "attn_post"
        attn_post(h)
        if nxt is not None:
            P.tag = "inproj"
            inproj_F(nxt)
        if "oa" in taps and h == 0:
            tap("oa", oa, oa[:], [128, NT, 128], [oa.s(tt) for tt in range(NT)])
            tap("u32", u32, u32[:], [128, NT, 128], [u32.s(tt) for tt in range(NT)])
    P.mark_phase("C")
    P.barrier()
    if stop_after in ("B", "B0"):
        d = nc.dram_tensor("tap_mixT", [128, H, S], BF16, kind="ExternalOutput").ap()
        P.dma("sp", out=d, in_=mixT[:])
        P.barrier()
        P.emit()
        return nc, P
    P.release(mB)

    h1 = P.sbuf("h1", [128, NT, D], F32)
    mC = P.mark()
    wout = P.sbuf("wout", [128, 8, D], BF16)
    wrow2 = P.sbuf("wrow2", [128, D], F32)
    xb = [P.sbuf(f"xb{i}", [128, D], F32) for i in range(2)]
    xn2 = [P.sbuf(f"xn2{i}", [128, D], F32) for i in range(2)]
    ssB = P.sbuf("ssB", [128, NT], F32)
    rstd2 = P.sbuf("rstd2", [128, NT], F32)
    junk3 = P.sbuf("junk3", [128, D], BF16)
    P.dma("sp", W=[wrow2.s()], out=wrow2[:], in_=norm2_d.partition_broadcast(128))
    for hh in range(8):
        P.dma("pool", W=[wout.s(hh)], out=wout[:, hh, :], in_=w_out_d[hh * 128:(hh + 1) * 128, :])
    for tt in range(NT):
        x_ = xb[tt % 2]
        P.dma("sp", W=[x_.s()], out=x_[:], in_=x_d[tt * 128:(tt + 1) * 128, :])
        for n in range(2):
            ps = P.psum()
            P.ops("pe", [("matmul", dict(out=ps[:, 0:512], lhsT=mixT[:, hh, tt * 128:(tt + 1) * 128],
                                         rhs=wout[:, hh, n * 512:(n + 1) * 512], start=(hh == 0), stop=(hh == 7)))
                         for hh in range(8)],
                  R=[mixT.s(hh, tt) for hh in range(8)] + [wout.s(hh) for hh in range(8)], W=[ps.s()])
            P.op("dve", "tensor_tensor", R=[ps.s(), x_.s()], W=[h1.s(tt, n)], out=h1[:, tt, n * 512:(n + 1) * 512],
                 in0=ps[:, 0:512], in1=x_[:, n * 512:(n + 1) * 512], op=ALU.add)
        P.op("act", "activation", R=[h1.s(tt, 0), h1.s(tt, 1)], W=[junk3.s(), ssB.s(tt)], out=junk3[:],
             in_=h1[:, tt, :], func=AF.Square, accum_out=ssB[:, tt:tt + 1])
    P.op("dve", "tensor_scalar", R=[ssB.s(tt) for tt in range(NT)], W=[rstd2.s()], out=rstd2[:], in0=ssB[:],
         scalar1=1.0 / D, scalar2=EPS, op0=ALU.mult, op1=ALU.add)
    P.op("pool", "tensor_tensor", R=[rstd2.s(), cm05.s()], W=[rstd2.s()], out=rstd2[:], in0=rstd2[:],
         in1=cm05[:, 0:NT], op=ALU.pow)
    for tt in range(NT):
        xn = xn2[tt % 2]
        P.op("dve", "scalar_tensor_tensor", R=[h1.s(tt, 0), h1.s(tt, 1), rstd2.s(), wrow2.s()], W=[xn.s()],
             out=xn[:], in0=h1[:, tt, :], scalar=rstd2[:, tt:tt + 1], in1=wrow2[:], op0=ALU.mult, op1=ALU.mult)
        for half in range(2):
            ps = P.psum()
            P.ops("pe", [("transpose", dict(out=ps[:, j * 128:(j + 1) * 128],
                                            in_=xn[:, (half * 4 + j) * 128:(half * 4 + j + 1) * 128],
                                            identity=ident_f)) for j in range(4)],
                  R=[xn.s(), CM], W=[ps.s()])
            evac_copy(uT[:, half * 4:(half + 1) * 4, tt * 128:(tt + 1) * 128],
                      ps[:, 0:512].rearrange("p (k t) -> p k t", k=4), [ps.s()], [uT.s(tt, half)])
    P.barrier()
    if stop_after == "C":
        for tt in range(NT):
            P.dma("sp", out=out_d[tt * 128:(tt + 1) * 128, :], in_=h1[:, tt, :])
        d = nc.dram_tensor("tap_uT2", [128, 8, S], BF16, kind="ExternalOutput").ap()
        P.dma("sp", out=d, in_=uT[:])
        P.barrier()
        P.emit()
        return nc, P
    P.release(mC)

    P.mark_phase("D")
    r1 = MIXT_OFF
    wgu = []
    for i in range(4):
        a, r1 = P.sbuf_at(f"wg{i}", [128, 8, 128], BF16, r1)
        b, r1 = P.sbuf_at(f"wu{i}", [128, 8, 128], BF16, r1)
        wgu.append((a, b))
    sil = []
    for i in range(2):
        a, r1 = P.sbuf_at(f"sil{i}", [128, 512], BF16, r1)
        sil.append(a)
    ost = []
    for i in range(2):
        a, r1 = P.sbuf_at(f"ost{i}", [128, 256], F32, r1)
        ost.append(a)
    assert r1 <= MIXT_OFF + H * S * 2
    actT = P.sbuf("actT", [128, NFC, 1024], BF16)
    wdb = [P.sbuf(f"wdb{i}", [128, NFC, 256], BF16) for i in range(2)]
    wdk = 0
    for hf in range(2 if DCUT == 0 else 1):
        for fc in range(NFC):
            wg_, wu_ = wgu[fc % WRING]
            P.dma("pool", W=[wg_.s()], out=wg_[:],
                  in_=w_gate_d[:, fc * 128:(fc + 1) * 128].rearrange("(k p) c -> p k c", p=128))
            P.dma("pool", W=[wu_.s()], out=wu_[:],
                  in_=w_up_d[:, fc * 128:(fc + 1) * 128].rearrange("(k p) c -> p k c", p=128))
            for tcl in range(2):
                tok = slice(hf * 1024 + tcl * 512, hf * 1024 + (tcl + 1) * 512)
                tts = [(hf * 1024 + tcl * 512) // 128 + j for j in range(4)]
                Ru = [uT.s(tt, half) for tt in tts for half in range(2)]
                psg = P.psum()
                psu = P.psum()
                P.ops("pe", [("matmul", dict(out=psg[:, 0:512], lhsT=wg_[:, kc, :], rhs=uT[:, kc, tok],
                                             start=(kc == 0), stop=(kc == 7))) for kc in range(8)],
                      R=Ru + [wg_.s()], W=[psg.s()])
                P.ops("pe", [("matmul", dict(out=psu[:, 0:512], lhsT=wu_[:, kc, :], rhs=uT[:, kc, tok],
                                             start=(kc == 0), stop=(kc == 7))) for kc in range(8)],
                      R=Ru + [wu_.s()], W=[psu.s()])
                sl = sil[tcl]
                P.op("act", "activation", R=[psg.s()], W=[sl.s()], out=sl[:], in_=psg[:, 0:512], func=AF.Silu)
                P.op("dve", "tensor_tensor", R=[psu.s(), sl.s()], W=[actT.s(fc, tcl)],
                     out=actT[:, fc, tcl * 512:(tcl + 1) * 512], in0=psu[:, 0:512], in1=sl[:], op=ALU.mult)
        for n4 in range(4 if DCUT != 1 else 0):
            wd_ = wdb[wdk % 2]
            wdk += 1
            for fc in range(NFC):
                P.dma("pool", ndesc=8, W=[wd_.s(fc)], out=wd_[:, fc, :],
                      in_=w_down_d[fc * 128:(fc + 1) * 128, n4 * 256:(n4 + 1) * 256])
            for tl in range(8):
                tt = hf * 8 + tl
                ps = P.psum()
                P.ops("pe", [("matmul", dict(out=ps[:, 0:256], lhsT=actT[:, fc, tl * 128:(tl + 1) * 128],
                                             rhs=wd_[:, fc, :], start=(fc == 0), stop=(fc == NFC - 1)))
                             for fc in range(NFC)],
                      R=[actT.s(fc, tl // 4) for fc in range(NFC)] + [wd_.s(fc) for fc in range(NFC)], W=[ps.s()])
                o_ = ost[(n4 * 8 + tl) % 2]
                P.op("dve", "tensor_tensor", R=[ps.s(), h1.s(tt, n4 // 2)], W=[o_.s()], out=o_[:], in0=ps[:, 0:256],
                     in1=h1[:, tt, n4 * 256:(n4 + 1) * 256], op=ALU.add)
                P.dma("sp", R=[o_.s()], out=out_d[tt * 128:(tt + 1) * 128, n4 * 256:(n4 + 1) * 256], in_=o_[:])
    P.barrier()
    P.emit()
    return nc, P


def _host_consts():
    ident = np.eye(128, dtype=np.float32)
    p = np.arange(128)
    ltri = (p[:, None] <= p[None, :]).astype(np.float32)
    ones = np.ones((128, 128), np.float32)
    sel63 = np.zeros((128, 128), np.float32)
    sel63[63, :] = 1.0
    sel127 = np.zeros((128, 128), np.float32)
    sel127[127, :] = 1.0
    return np.ascontiguousarray(np.concatenate([ident, ltri, ones, sel63, sel127], axis=1))


def make_in_maps(inputs, cores):
    f = lambda k: np.ascontiguousarray(np.asarray(inputs[k], dtype=np.float32))
    conv_w = f("conv_w")[0]
    conv_wl = np.ascontiguousarray(conv_w.T.reshape(24, 128, 4).transpose(1, 0, 2).reshape(128, 96))
    lqk = np.ascontiguousarray(np.concatenate([f("lambda_q1")[0], f("lambda_k1")[0], f("lambda_q2")[0],
                                               f("lambda_k2")[0]])[None, :])
    shared = {
        "w_in": f("w_in")[0], "w_out": f("w_out")[0], "w_gate": f("w_gate")[0], "w_up": f("w_up")[0],
        "w_down": f("w_down")[0], "norm1_w": f("norm1_w"), "norm2_w": f("norm2_w"),
        "gdn_norm_w": f("gdn_norm_w"), "subln_w": f("subln_w"), "q_norm_w": f("q_norm_w"),
        "k_norm_w": f("k_norm_w"), "lqk": lqk, "a_log": f("a_log"), "dt_bias": f("dt_bias"),
        "conv_wl": conv_wl, "cmat": _host_consts(),
    }
    x = f("x")
    return [dict(shared, x=np.ascontiguousarray(x[b])) for b in cores]


def kernel(**inputs):
    nc, _ = build_program()
    in_maps = make_in_maps(inputs, list(range(8)))
    res = run_bass_kernel_spmd(nc, in_maps, core_ids=list(range(8)))
    return np.stack([np.asarray(r["out"], dtype=np.float32) for r in res.results], axis=0)
```

```python
import contextlib
import numpy as np
import concourse.bass as bass
import concourse.mybir as mybir
from concourse.bass_utils import run_bass_kernel_spmd

F32 = mybir.dt.float32
BF16 = mybir.dt.bfloat16
AF = mybir.ActivationFunctionType
ALU = mybir.AluOpType
AX = mybir.AxisListType

SBUF_LO = 16512 + 64
SBUF_HI = 229376

S = 2048
D = 1024
NT = 16
H = 8
DFF = 2816
NFC = DFF // 128
EPS = 1e-6
import os
ATT_LOOK = int(os.environ.get('ATT_LOOK', '3'))
INP_LAG = int(os.environ.get('INP_LAG', '1'))
NHEADS = int(os.environ.get('NHEADS', '8'))
DCUT = int(os.environ.get('DCUT', '0'))
SAME_ENG_DIST = int(os.environ.get('SAME_ENG_DIST', '3'))
OVERLAP = int(os.environ.get('OVERLAP', '0'))
SCHED = int(os.environ.get('SCHED', '1'))
PRIO = float(os.environ.get('PRIO', '1'))
BSCALE = float(os.environ.get('BSCALE', '1'))
PREP_B = int(os.environ.get('PREP_B', '8'))
PE_C0 = float(os.environ.get('PE_C0', '0.036'))
PE_RATE = float(os.environ.get('PE_RATE', '2800'))
WRING = int(os.environ.get('WRING', '2'))
D_IN = 9232
COL = dict(gq=0, gk=1024, gv=2048, gz=3072, ga=4096, gb=4104, dq=4112, dk=5136, dv=6160,
           gate_a=7184, gate_b=8208)
HEAD_GROUPS = ("gq", "gk", "gv", "gz", "dq", "dk", "dv", "gate_a", "gate_b")


class Buf:
    __slots__ = ("name", "lw", "rd", "excl")

    def __init__(self, name, excl=False):
        self.name = name
        self.lw = None
        self.rd = set()
        self.excl = excl


class TT:
    def __init__(self, prog, name, h, excl=False):
        self.prog = prog
        self.name = name
        self.h = h
        self.ap = h.ap()
        self.slots = {}
        self.excl = excl

    def s(self, *key):
        b = self.slots.get(key)
        if b is None:
            b = Buf(f"{self.name}{key}", self.excl)
            self.slots[key] = b
            self.prog.allbufs.append(b)
        return b

    def __getitem__(self, idx):
        return self.ap[idx]


class Prog:
    COMPUTE = ("pe", "act", "dve", "pool")
    NDMASEM = 48

    def __init__(self, nc, same_engine_sync=True):
        self.nc = nc
        self.items = {k: [] for k in ("pe", "act", "dve", "pool", "sp")}
        self.seq = {k: 0 for k in self.COMPUTE}
        self.waited = {k: {} for k in self.items}
        self.allbufs = []
        self.same_engine_sync = same_engine_sync
        self.dma_tot = [0] * self.NDMASEM
        self.dma_rr = 0
        self.sb_ptr = SBUF_LO
        self.sb_peak = SBUF_LO
        self.nalloc = 0
        self.psum_banks = []
        self.ps_rr = 0
        self.ps_pinned = set()
        self.n_wait = 0
        self.n_ops = 0
        self.n_pe = 0
        self.marks = []
        self.pool_out = []
        self.nodes = []
        self.seg_bounds = []
        self.seg_stats = []
        self._busy = {}
        self.tag = ""
        self.crit = []

    def sbuf(self, name, shape, dtype):
        esz = 4 if dtype == F32 else 2
        per_part = int(np.prod(shape[1:])) * esz
        off = (self.sb_ptr + 63) // 64 * 64
        assert off + per_part <= SBUF_HI, f"SBUF overflow allocating {name}: {off}+{per_part}"
        self.sb_ptr = off + per_part
        self.sb_peak = max(self.sb_peak, self.sb_ptr)
        self.nalloc += 1
        h = self.nc.alloc_sbuf_tensor_at(f"{name}_{self.nalloc}", list(shape), dtype, offset=off)
        return TT(self, name, h)

    def sbuf_at(self, name, shape, dtype, off):
        esz = 4 if dtype == F32 else 2
        per_part = int(np.prod(shape[1:])) * esz
        assert off % 64 == 0 and off + per_part <= SBUF_HI
        self.nalloc += 1
        h = self.nc.alloc_sbuf_tensor_at(f"{name}_{self.nalloc}", list(shape), dtype, offset=off)
        return TT(self, name, h), off + per_part

    def mark(self):
        return self.sb_ptr

    def release(self, mark):
        self.sb_ptr = mark

    def init_psum(self):
        for i in range(8):
            h = self.nc.alloc_psum_tensor(f"psb{i}", [128, 512], F32)
            self.psum_banks.append(TT(self, f"psb{i}", h, excl=True))

    def psum(self):
        for _ in range(8):
            i = self.ps_rr
            self.ps_rr = (self.ps_rr + 1) % 8
            if i not in self.ps_pinned:
                return self.psum_banks[i]
        raise RuntimeError("all psum pinned")

    def psum_pin(self):
        t = self.psum()
        self.ps_pinned.add(self.psum_banks.index(t))
        return t

    def psum_unpin(self, t):
        self.ps_pinned.discard(self.psum_banks.index(t))

    def _node(self, eng, fn, reads, writes, cost, is_dma=False, ndesc=0, lat=0.0):
        writes = [w for w in writes if w is not None] + [r for r in reads if r is not None and r.excl]
        reads = [r for r in reads if r is not None and not r.excl]
        nid = len(self.nodes)
        deps = set()
        for r in reads:
            if r.lw is not None:
                deps.add(r.lw)
        for w in writes:
            if w.lw is not None:
                deps.add(w.lw)
            deps.update(w.rd)
        deps.discard(nid)
        self.nodes.append(dict(id=nid, eng=eng, fn=fn, deps=deps, cost=cost, dma=is_dma, ndesc=ndesc, lat=lat,
                               tag=self.tag))
        for r in reads:
            r.rd.add(nid)
        for w in writes:
            w.lw = nid
            w.rd = set()
        return nid

    @staticmethod
    def _nfree(ap):
        try:
            sh = list(ap.shape)
            n = 1
            for d in sh[1:]:
                n *= int(d)
            return n
        except Exception:
            return 128

    def _cost(self, eng, method, kw):
        out = kw.get("out", kw.get("ap"))
        n = self._nfree(out) if out is not None else 128
        if eng == "pe":
            if method == "transpose":
                return 0.12
            f32 = False
            try:
                f32 = kw["lhsT"].tensor.dtype == F32
            except Exception:
                pass
            return PE_C0 + (4.0 if f32 else 1.0) * max(64, n) / PE_RATE
        if eng == "act":
            return 0.22 + n / 1400.0 + (0.1 if "accum_out" in kw else 0.0)
        if eng == "dve":
            return 0.12 + n / 960.0 * (8.0 if method == "reciprocal" else 1.0)
        return (0.9 if method == 'tensor_tensor' and n <= 16 else 0.3) + n / 700.0

    def op(self, eng, method, R=(), W=(), **kw):
        def fn(e, method=method, kw=kw):
            return getattr(e, method)(**kw)
        self.n_ops += 1
        if eng == "pe":
            self.n_pe += 1
        self._node(eng, fn, R, W, self._cost(eng, method, kw))

    def mark_phase(self, name):
        self.marks.append((name, self.n_pe))

    def ops(self, eng, calls, R=(), W=()):
        calls = list(calls)
        if eng == "pe":
            self.n_pe += len(calls)
        self.n_ops += 1

        def fn(e, calls=calls):
            ins = None
            for (method, kw) in calls:
                ins = getattr(e, method)(**kw)
            return ins
        self._node(eng, fn, R, W, sum(self._cost(eng, m, kw) for m, kw in calls))

    def dma(self, queue, R=(), W=(), ndesc=64, **kw):
        def fn(e, kw=kw):
            return e.dma_start(**kw)
        nbytes = self._nfree(kw["out"]) * 128 * 4
        self._node(queue, fn, R, W, 1.2 if queue == "pool" else 0.1, is_dma=True, ndesc=ndesc,
                   lat=2.0 + nbytes / 1.5e5)

    def barrier(self):
        self.seg_bounds.append(len(self.nodes))
        for b in self.allbufs:
            b.lw = None
            b.rd = set()

    HOP = float(os.environ.get('HOP', '0.35'))
    POOL_DESC_CAP = int(os.environ.get('PCAP', '200'))

    def _schedule_segment(self, lo, hi):
        import heapq
        if not SCHED:
            return list(range(lo, hi))
        nodes = self.nodes
        ndep = {}
        users = {}
        for n in nodes[lo:hi]:
            d = [x for x in n["deps"] if x >= lo]
            ndep[n["id"]] = len(d)
            for x in d:
                users.setdefault(x, []).append(n["id"])
        blevel = {}
        for n in reversed(nodes[lo:hi]):
            nid = n["id"]
            best = 0.0
            for u in users.get(nid, ()):
                un = nodes[u]
                lat = 0.05 if (un["eng"] == n["eng"] and not n["dma"]) else self.HOP
                v = lat + blevel[u]
                if v > best:
                    best = v
            blevel[nid] = n["cost"] + n["lat"] + best
        fin = {}
        ready_t = {}
        engs = ("pe", "act", "dve", "pool", "sp")
        avail = {e: [] for e in engs}
        free_t = {e: 0.0 for e in engs}
        for n in nodes[lo:hi]:
            if ndep[n["id"]] == 0:
                avail[n["eng"]].append(n["id"])
                ready_t[n["id"]] = 0.0
        order = []
        last_on = {}
        crit_dep = {}
        remaining = hi - lo
        while remaining:
            best = None
            for e in engs:
                lst = avail[e]
                if not lst:
                    continue
                ft = free_t[e]
                cb = None
                for nid in lst:
                    rt = ready_t[nid]
                    st = rt if rt > ft else ft
                    key = (st, -blevel[nid] * PRIO, nid)
                    if cb is None or key < cb:
                        cb = key
                if best is None or cb < best[0]:
                    best = (cb, e)
            (st, _, nid), e = best
            avail[e].remove(nid)
            n = nodes[nid]
            rt = ready_t[nid]
            start = st
            n["start"] = start
            n["why"] = ("eng", last_on.get(e)) if free_t[e] >= rt and last_on.get(e) is not None else ("dep", crit_dep.get(nid))
            last_on[e] = nid
            free_t[e] = start + n["cost"]
            fin[nid] = start + n["cost"] + n["lat"]
            order.append(nid)
            remaining -= 1
            self._busy[e] = self._busy.get(e, 0.0) + n["cost"]
            for u in users.get(nid, ()):
                un = nodes[u]
                lat = 0.05 if (un["eng"] == e and not n["dma"]) else self.HOP
                if fin[nid] + lat >= ready_t.get(u, 0.0):
                    crit_dep[u] = nid
                ready_t[u] = max(ready_t.get(u, 0.0), fin[nid] + lat)
                ndep[u] -= 1
                if ndep[u] == 0:
                    avail[un["eng"]].append(u)
        self.seg_stats.append((lo, hi, max(fin.values()), dict(self._busy)))
        cur = max(fin, key=lambda k: fin[k])
        path = []
        while cur is not None:
            n = nodes[cur]
            path.append((cur, n["tag"], n["eng"], n["start"], n["cost"], n["why"][0]))
            cur = n["why"][1]
        self.crit.append(path[::-1])
        self._busy = {}
        return order

    def _lower(self):
        nodes = self.nodes
        bounds = [0] + [b for b in self.seg_bounds if b > 0]
        if bounds[-1] != len(nodes):
            bounds.append(len(nodes))
        items = {k: [] for k in ("pe", "act", "dve", "pool", "sp")}
        seq = {k: 0 for k in self.COMPUTE}
        waited = {k: {} for k in items}
        tok = {}
        dma_tot = [0] * self.NDMASEM
        dma_rr = 0
        dma_rr_sw = 0
        pool_out = []
        for si in range(len(bounds) - 1):
            lo, hi = bounds[si], bounds[si + 1]
            if hi <= lo:
                continue
            order = self._schedule_segment(lo, hi)
            for nid in order:
                n = nodes[nid]
                q = n["eng"]
                deps = [tok[d] for d in n["deps"] if d >= lo]
                if n["dma"]:
                    if q == "pool":
                        k = 16 + dma_rr_sw
                        dma_rr_sw = (dma_rr_sw + 1) % (self.NDMASEM - 16)
                    else:
                        k = dma_rr
                        dma_rr = (dma_rr + 1) % 16
                    key = f"dma{k}"
                    if dma_tot[k] > 0:
                        deps.append((key, dma_tot[k]))
                    if q == "pool":
                        while pool_out and sum(x for _, x in pool_out) + n["ndesc"] > self.POOL_DESC_CAP:
                            t, _ = pool_out.pop(0)
                            deps.append(t)
                    dma_tot[k] += 16
                    tok[nid] = (key, dma_tot[k])
                    semkey, inc = key, 16
                    if q == "pool":
                        pool_out.append((tok[nid], n["ndesc"]))
                else:
                    seq[q] += 1
                    tok[nid] = (q, seq[q])
                    semkey, inc = q, 1
                wd = waited[q]
                best = {}
                for (k, v) in deps:
                    if k == q:
                        if q == "pe":
                            continue
                        if q in ("act", "dve") and seq[q] - v >= SAME_ENG_DIST:
                            continue
                    if wd.get(k, 0) >= v:
                        continue
                    wd[k] = v
                    best[k] = max(best.get(k, 0), v)
                self.n_wait += len(best)
                items[q].append(("op", list(best.items()), n["fn"], semkey, inc))
            targets = [(k, seq[k]) for k in self.COMPUTE if seq[k] > 0]
            targets += [(f"dma{i}", v) for i, v in enumerate(dma_tot) if v > 0]
            for q in items:
                w = []
                for (k, v) in targets:
                    if k != q and waited[q].get(k, 0) < v:
                        waited[q][k] = v
                        w.append((k, v))
                if w:
                    items[q].append(("wait", w, None, None, 0))
        self.items = items

    def emit(self):
        nc = self.nc
        self._lower()
        with contextlib.ExitStack() as st:
            sems = {}
            for k in self.COMPUTE:
                sems[k] = st.enter_context(nc.semaphore(f"s_{k}"))
            for i in range(self.NDMASEM):
                sems[f"dma{i}"] = st.enter_context(nc.semaphore(f"s_dma{i}"))
            block = st.enter_context(nc.Block())
            items = self.items
            targets = {k: set() for k in self.COMPUTE}
            for lst in items.values():
                for (kind, waits, fn, semkey, inc) in lst:
                    for (k, v) in waits:
                        if k in targets:
                            targets[k].add(v)
            rank = {k: {v: i + 1 for i, v in enumerate(sorted(vs))} for k, vs in targets.items()}
            self.n_sig = {k: len(v) for k, v in targets.items()}

            def run(e, lst):
                idx = 0
                for (kind, waits, fn, semkey, inc) in lst:
                    for (k, v) in waits:
                        e.wait_ge(sems[k], rank[k][v] if k in rank else v)
                    if kind == "op":
                        ins = fn(e)
                        if semkey in rank:
                            idx += 1
                            if idx in rank[semkey]:
                                ins.then_inc(sems[semkey], 1)
                        else:
                            ins.then_inc(sems[semkey], inc)

            @block.tensor
            def _(e):
                run(e, items["pe"])

            @block.scalar
            def _(e):
                run(e, items["act"])

            @block.vector
            def _(e):
                run(e, items["dve"])

            @block.gpsimd
            def _(e):
                run(e, items["pool"])

            @block.sync
            def _(e):
                run(e, items["sp"])


def build_program(stop_after=None, taps=()):
    nc = bass.Bass("TRN2", target_bir_lowering=False)
    P = Prog(nc)
    P.init_psum()
    taps = set(taps)

    def din(name, shape):
        return nc.dram_tensor(name, list(shape), F32, kind="ExternalInput").ap()

    x_d = din("x", [S, D])
    w_in_d = din("w_in", [D, D_IN])
    w_out_d = din("w_out", [D, D])
    w_gate_d = din("w_gate", [D, DFF])
    w_up_d = din("w_up", [D, DFF])
    w_down_d = din("w_down", [DFF, D])
    norm1_d = din("norm1_w", [1, D])
    norm2_d = din("norm2_w", [1, D])
    gnw_d = din("gdn_norm_w", [1, 128])
    slw_d = din("subln_w", [1, 128])
    qnw_d = din("q_norm_w", [1, 64])
    knw_d = din("k_norm_w", [1, 64])
    lqk_d = din("lqk", [1, 256])
    alog_d = din("a_log", [1, 8])
    dtb_d = din("dt_bias", [1, 8])
    convw_d = din("conv_wl", [128, 96])
    cmat_d = din("cmat", [128, 5 * 128])
    out_d = nc.dram_tensor("out", [S, D], F32, kind="ExternalOutput").ap()
    tap_d = {}

    def tap(name, tt, sl, shape, reads):
        if name not in taps:
            return
        d = nc.dram_tensor("tap_" + name, list(shape), F32, kind="ExternalOutput").ap()
        tap_d[name] = d
        P.dma("sp", R=reads, out=d, in_=sl)

    cm = P.sbuf("cm", [128, 5, 128], F32)
    identb = P.sbuf("identb", [128, 128], BF16)
    onesb = P.sbuf("onesb", [128, 128], BF16)
    gnw = P.sbuf("gnw", [128, 128], F32)
    slw8 = P.sbuf("slw8", [128, 128], F32)
    qkw = P.sbuf("qkw", [128, 128], F32)
    lqk = P.sbuf("lqk", [128, 256], F32)
    alog = P.sbuf("alog", [128, 8], F32)
    dtb = P.sbuf("dtb", [128, 8], F32)
    convw = P.sbuf("convw", [128, 96], F32)
    sc = P.sbuf("sc", [128, 16], F32)
    negA = P.sbuf("negA", [128, 8], F32)
    cm05 = P.sbuf("cm05", [128, 64], F32)
    epsc = P.sbuf("epsc", [128, 1], F32)
    ident_f = cm[:, 0, :]
    ltri_f = cm[:, 1, :]
    ones_f = cm[:, 2, :]
    sel_f = [cm[:, 3, :], cm[:, 4, :]]
    CM = cm.s()

    P.dma("sp", W=[CM], out=cm[:].rearrange("p a b -> p (a b)"), in_=cmat_d[:, :])
    P.dma("pool", W=[identb.s()], out=identb[:], in_=cmat_d[:, 0:128])
    P.dma("pool", W=[onesb.s()], out=onesb[:], in_=cmat_d[:, 256:384])
    P.dma("sp", W=[gnw.s()], out=gnw[:], in_=gnw_d.partition_broadcast(128))
    P.dma("sp", W=[slw8.s()], out=slw8[:], in_=slw_d.partition_broadcast(128))
    P.dma("sp", W=[qkw.s()], out=qkw[:, 0:64], in_=qnw_d.partition_broadcast(128))
    P.dma("sp", W=[qkw.s()], out=qkw[:, 64:128], in_=knw_d.partition_broadcast(128))
    P.dma("sp", W=[lqk.s()], out=lqk[:], in_=lqk_d.partition_broadcast(128))
    P.dma("sp", W=[alog.s()], out=alog[:], in_=alog_d.partition_broadcast(128))
    P.dma("sp", W=[dtb.s()], out=dtb[:], in_=dtb_d.partition_broadcast(128))
    P.dma("sp", W=[convw.s()], out=convw[:], in_=convw_d[:, :])

    P.op("pool", "memset", W=[cm05.s()], ap=cm05[:], constant=-0.5)
    P.op("pool", "memset", W=[epsc.s()], ap=epsc[:], constant=EPS)
    P.op("pool", "tensor_scalar", R=[slw8.s()], W=[slw8.s()], out=slw8[:], in0=slw8[:], scalar1=0.8,
         scalar2=None, op0=ALU.mult)
    P.op("pool", "tensor_scalar", R=[qkw.s()], W=[qkw.s()], out=qkw[:, 0:64], in0=qkw[:, 0:64],
         scalar1=64 ** -0.5, scalar2=None, op0=ALU.mult)
    P.op("dve", "tensor_tensor", R=[lqk.s()], W=[lqk.s()], out=lqk[:, 0:64], in0=lqk[:, 0:64],
         in1=lqk[:, 64:128], op=ALU.mult)
    P.op("dve", "tensor_tensor", R=[lqk.s()], W=[lqk.s()], out=lqk[:, 128:192], in0=lqk[:, 128:192],
         in1=lqk[:, 192:256], op=ALU.mult)
    P.op("dve", "tensor_reduce", R=[lqk.s()], W=[sc.s()], out=sc[:, 0:1], in_=lqk[:, 0:64], axis=AX.X,
         op=ALU.add)
    P.op("dve", "tensor_reduce", R=[lqk.s(), sc.s()], W=[sc.s()], out=sc[:, 1:2], in_=lqk[:, 128:192],
         axis=AX.X, op=ALU.add)
    P.op("act", "activation", R=[sc.s()], W=[sc.s()], out=sc[:, 2:4], in_=sc[:, 0:2], func=AF.Exp)
    P.op("dve", "tensor_tensor", R=[sc.s()], W=[sc.s()], out=sc[:, 4:5], in0=sc[:, 2:3], in1=sc[:, 3:4],
         op=ALU.subtract)
    P.op("dve", "tensor_scalar", R=[sc.s()], W=[sc.s()], out=sc[:, 5:6], in0=sc[:, 4:5], scalar1=0.2,
         scalar2=-1.0, op0=ALU.add, op1=ALU.mult)
    P.op("act", "activation", R=[alog.s()], W=[negA.s()], out=negA[:], in_=alog[:], func=AF.Exp)
    P.op("dve", "tensor_scalar", R=[negA.s()], W=[negA.s()], out=negA[:], in0=negA[:], scalar1=-1.0,
         scalar2=None, op0=ALU.mult)
    nlam = sc[:, 5:6]

    def cut_here(tag):
        if stop_after == tag:
            d = nc.dram_tensor("tap_sc", [128, 16], F32, kind="ExternalOutput").ap()
            P.dma("sp", R=[sc.s()], out=d, in_=sc[:])
            P.barrier()
            P.emit()
            return True
        return False
    if cut_here("A0"):
        return nc, P

    uT = P.sbuf("uT", [128, 8, S], BF16)
    gcs = P.sbuf("gcs", [128, NT, 8], F32)
    lbs = P.sbuf("lbs", [128, NT, 8], F32)
    glt = P.sbuf("glt", [128, NT, 8], F32)
    egs = P.sbuf("egs", [128, NT, 2, 8], F32)

    mA = P.mark()
    xall = P.sbuf("xall", [128, NT, D], F32)
    wrow1 = P.sbuf("wrow1", [128, D], F32)
    P.dma("sp", W=[wrow1.s()], out=wrow1[:], in_=norm1_d.partition_broadcast(128))
    junk = P.sbuf("junk", [128, D], BF16)
    ssA = P.sbuf("ssA", [128, NT], F32)
    rstd1 = P.sbuf("rstd1", [128, NT], F32)
    xn32 = [P.sbuf(f"xn32_{i}", [128, D], F32) for i in range(2)]
    uT32 = [P.sbuf(f"uT32_{i}", [128, D], F32) for i in range(2)]
    wab = P.sbuf("wab", [128, 8, 16], F32)
    gab = P.sbuf("gab", [128, NT, 16], F32)
    arg = P.sbuf("arg", [128, NT, 16], F32)
    g32 = P.sbuf("g32", [128, NT, 8], F32)

    P.dma("sp", W=[wab.s()], out=wab[:],
          in_=w_in_d[:, COL["ga"]:COL["ga"] + 16].rearrange("(k p) c -> p k c", p=128))
    for tt in range(NT):
        P.dma("sp", W=[xall.s(tt)], out=xall[:, tt, :], in_=x_d[tt * 128:(tt + 1) * 128, :])
    for tt in range(NT):
        P.op("act", "activation", R=[xall.s(tt)], W=[junk.s(), ssA.s(tt)], out=junk[:], in_=xall[:, tt, :],
             func=AF.Square, accum_out=ssA[:, tt:tt + 1])
    if cut_here("A1"):
        return nc, P
    for tt in range(NT):
        P.op("dve", "tensor_scalar", R=[ssA.s(tt)], W=[rstd1.s(tt)], out=rstd1[:, tt:tt + 1], in0=ssA[:, tt:tt + 1],
             scalar1=1.0 / D, scalar2=EPS, op0=ALU.mult, op1=ALU.add)
        P.op("pool", "tensor_tensor", R=[rstd1.s(tt), cm05.s()], W=[rstd1.s(tt)], out=rstd1[:, tt:tt + 1],
             in0=rstd1[:, tt:tt + 1], in1=cm05[:, 0:1], op=ALU.pow)
    psG = P.psum_pin()
    for tt in range(NT):
        xn = xn32[tt % 2]
        u32 = uT32[tt % 2]
        P.op("dve", "scalar_tensor_tensor", R=[xall.s(tt), rstd1.s(tt), wrow1.s()], W=[xn.s()], out=xn[:],
             in0=xall[:, tt, :], scalar=rstd1[:, tt:tt + 1], in1=wrow1[:], op0=ALU.mult, op1=ALU.mult)
        for half in range(2):
            ps = P.psum()
            P.ops("pe", [("transpose", dict(out=ps[:, j * 128:(j + 1) * 128],
                                            in_=xn[:, (half * 4 + j) * 128:(half * 4 + j + 1) * 128],
                                            identity=ident_f)) for j in range(4)],
                  R=[xn.s(), CM], W=[ps.s()])
            if half == 0:
                P.op("act", "activation", R=[ps.s()], W=[u32.s(half)], out=u32[:, 0:512], in_=ps[:, 0:512],
                     func=AF.Copy)
            else:
                P.op("dve", "tensor_copy", R=[ps.s()], W=[u32.s(half)], out=u32[:, 512:1024], in_=ps[:, 0:512])
        P.op("pool", "tensor_copy", R=[u32.s(0), u32.s(1)], W=[uT.s(tt)], out=uT[:, :, tt * 128:(tt + 1) * 128],
             in_=u32[:].rearrange("p (k t) -> p k t", k=8))
        P.ops("pe", [("matmul", dict(out=psG[:, tt * 16:(tt + 1) * 16], lhsT=u32[:, kc * 128:(kc + 1) * 128],
                                     rhs=wab[:, kc, :], start=(kc == 0), stop=(kc == 7))) for kc in range(8)],
              R=[u32.s(0), u32.s(1), wab.s()], W=[psG.s()])
    if cut_here("A2"):
        return nc, P
    P.op("act", "activation", R=[psG.s()], W=[gab.s()], out=gab[:].rearrange("p a b -> p (a b)"),
         in_=psG[:, 0:NT * 16], func=AF.Copy)
    P.psum_unpin(psG)
    for tt in range(NT):
        P.op("dve", "tensor_tensor", R=[gab.s(), dtb.s()], W=[arg.s()], out=arg[:, tt, 0:8], in0=gab[:, tt, 0:8],
             in1=dtb[:], op=ALU.add)
    P.op("dve", "tensor_scalar", R=[gab.s()], W=[arg.s()], out=arg[:, :, 8:16], in0=gab[:, :, 8:16],
         scalar1=-1.0, scalar2=None, op0=ALU.mult)
    argf = arg[:].rearrange("p a b -> p (a b)")
    P.op("act", "activation", R=[arg.s()], W=[arg.s()], out=argf, in_=argf, func=AF.Exp)
    P.op("act", "activation", R=[arg.s()], W=[arg.s()], out=argf, in_=argf, func=AF.Ln, bias=1.0, scale=1.0)
    for tt in range(NT):
        P.op("dve", "tensor_tensor", R=[arg.s(), negA.s()], W=[g32.s()], out=g32[:, tt, :], in0=arg[:, tt, 0:8],
             in1=negA[:], op=ALU.mult)
    P.op("dve", "tensor_scalar", R=[arg.s()], W=[lbs.s()], out=lbs[:], in0=arg[:, :, 8:16], scalar1=-1.0,
         scalar2=None, op0=ALU.mult)
    if cut_here("A3"):
        return nc, P
    ps = P.psum()
    P.ops("pe", [("matmul", dict(out=ps[:, tt * 8:(tt + 1) * 8], lhsT=ltri_f, rhs=g32[:, tt, :], start=True,
                                 stop=True)) for tt in range(NT)], R=[g32.s(), CM], W=[ps.s()])
    P.op("act", "activation", R=[ps.s()], W=[gcs.s()], out=gcs[:].rearrange("p a b -> p (a b)"),
         in_=ps[:, 0:NT * 8], func=AF.Copy)
    ps = P.psum()
    P.ops("pe", [("matmul", dict(out=ps[:, (tt * 2 + c) * 8:(tt * 2 + c + 1) * 8], lhsT=sel_f[c],
                                 rhs=gcs[:, tt, :], start=True, stop=True))
                 for tt in range(NT) for c in range(2)], R=[gcs.s(), CM], W=[ps.s()])
    P.op("act", "activation", R=[ps.s()], W=[egs.s()], out=egs[:].rearrange("p a c b -> p (a c b)"),
         in_=ps[:, 0:NT * 16], func=AF.Exp)
    if cut_here("A4"):
        return nc, P
    psv = ps[:, 0:NT * 16].rearrange("p (a c b) -> p a c b", a=NT, c=2)
    P.op("dve", "tensor_copy", R=[ps.s()], W=[glt.s()], out=glt[:, :, :], in_=psv[:, :, 1, :])
    if cut_here("A5"):
        return nc, P
    tap("gcs", gcs, gcs[:], [128, NT, 8], [gcs.s()])
    tap("lbs", lbs, lbs[:], [128, NT, 8], [lbs.s()])
    tap("egs", egs, egs[:], [128, NT, 2, 8], [egs.s()])
    tap("glt", glt, glt[:], [128, NT, 8], [glt.s()])
    P.barrier()
    P.release(mA)

    if stop_after == "A":
        u_dbg = nc.dram_tensor("tap_uT", [128, 8, S], BF16, kind="ExternalOutput").ap()
        P.dma("sp", R=[uT.s(tt) for tt in range(NT)], out=u_dbg, in_=uT[:])
        P.barrier()
        P.emit()
        return nc, P


    MIXT_OFF = (P.sb_ptr + 63) // 64 * 64
    mixT = P.sbuf("mixT", [128, H, S], BF16)
    mB = P.mark()
    wh = [P.sbuf("wh0", [128, 8, 9 * 128], BF16)] * 2
    xc = P.sbuf("xc", [128, 2, S + 4], BF16)
    dg = P.sbuf("dg", [128, 12, 128], BF16)
    kqv = P.sbuf("kqv", [128, 3, S], BF16)
    sgt = [P.sbuf("sgt0", [128, 512], BF16)]
    GAb = [P.sbuf(f"GA{i}", [128, NT, 128], BF16) for i in range(2)]
    GBt = P.sbuf("GBt", [128, NT, 128], BF16)
    vd = P.sbuf("vd", [128, NT, 130], BF16)
    dqT = P.sbuf("dqT", [128, S], BF16)
    dkT = P.sbuf("dkT", [128, S], BF16)
    tz = [P.sbuf(f"tz{i}", [128, 384], BF16) for i in range(2)]
    sq4 = [P.sbuf(f"sq4{i}", [128, 256], BF16) for i in range(2)]
    rs4 = [P.sbuf(f"rs4{i}", [128, 4], F32) for i in range(2)]
    qkn = [P.sbuf(f"qkn{i}", [128, 256], BF16) for i in range(2)]
    t1 = [P.sbuf(f"t1{i}", [128, 128], BF16) for i in range(2)]
    t2 = [P.sbuf(f"t2{i}", [128, 128], BF16) for i in range(2)]
    ssq = P.sbuf("ssq", [128, NT, 2], F32)
    tsc = P.sbuf("tsc", [128, 10, NT], F32)
    kbg = P.sbuf("kbg", [128, NT, 128], BF16)
    kdec = P.sbuf("kdec", [128, NT, 128], BF16)
    vb = P.sbuf("vb", [128, NT, 128], BF16)
    dgf = [P.sbuf(f"dgf{i}", [128, 256], F32) for i in range(2)] * 2
    eaq = [P.sbuf(f"eaq{i}", [128, 256], F32) for i in range(2)] * 2
    wlv = [[P.sbuf(f"wlv{i}_{k}", [128, 384], BF16) for k in range(2)] for i in range(PREP_B)]
    aqt = P.sbuf("aqt", [128, NT, 128], BF16)
    u32 = P.sbuf("u32", [128, NT, 128], F32)
    wT = P.sbuf("wT", [128, S], BF16)
    S32 = P.sbuf("S32", [128, 128], F32)
    Sb = P.sbuf("Sb", [128, 128], BF16)
    vn = P.sbuf("vn", [128, 2, 128], BF16)
    o1s = P.sbuf("o1s", [128, 2, 128], F32)
    oa = P.sbuf("oa", [128, NT, 128], BF16)
    sso = P.sbuf("sso", [128, NT], F32)
    junk2 = P.sbuf("junk2", [128, 128], BF16)
    rso = P.sbuf("rso", [128, NT], F32)
    pt = [P.sbuf(f"pt{i}", [128, 512], BF16) for i in range(4)]
    ob = [P.sbuf(f"ob{i}", [128, 128], F32) for i in range(4)]
    tb = [P.sbuf(f"tb{i}", [128, 128], F32) for i in range(4)]
    rsum = [P.sbuf(f"rsum{i}", [128, 4], F32) for i in range(4)]
    ob16 = P.sbuf("ob16", [128, NT, 128], BF16)
    rs16 = P.sbuf("rs16", [128, NT], F32)
    junk4 = P.sbuf("junk4", [128, 128], BF16)
    rr16 = P.sbuf("rr16", [128, NT], F32)
    TS_LRQ, TS_LRK, TS_CA, TS_CQ, TS_BJ, TS_SK1, TS_SK2, TS_BETA, TS_F, TS_TMP = range(10)

    P.op("pool", "memset", W=[xc.s(g, "pad") for g in range(2)], ap=xc[:, :, 0:3], constant=0.0)
    P.op("pool", "memset", W=[vd.s(tt) for tt in range(NT)], ap=vd[:, :, 128:130], constant=1.0)

    def load_head_weights(h):
        w = wh[h % 2]
        for g, nm in enumerate(HEAD_GROUPS):
            c0 = COL[nm] + h * 128
            P.dma("pool", W=[w.s(g)], out=w[:, :, g * 128:(g + 1) * 128],
                  in_=w_in_d[:, c0:c0 + 128].rearrange("(k p) c -> p k c", p=128))

    evac_rr = [0]

    def evac_copy(out, in_, R, W):
        evac_rr[0] ^= 1
        if evac_rr[0]:
            P.op("act", "activation", R=R, W=W, out=out, in_=in_, func=AF.Copy)
        else:
            P.op("dve", "tensor_copy", R=R, W=W, out=out, in_=in_)

    def inproj_T(h):
        w = wh[h % 2]
        GA = GAb[h % 2]
        pst = {}

        def s12(tt):
            i2 = tt % 2
            psA = P.psum()
            psB = P.psum()
            pst[tt] = psA
            lhs = lambda kc: uT[:, kc, tt * 128:(tt + 1) * 128]
            P.ops("pe", [("matmul", dict(out=psA[:, 0:512], lhsT=lhs(kc), rhs=w[:, kc, 3 * 128:7 * 128],
                                         start=(kc == 0), stop=(kc == 7))) for kc in range(8)],
                  R=[uT.s(tt)] + [w.s(g) for g in (3, 4, 5, 6)], W=[psA.s()])
            P.ops("pe", [("matmul", dict(out=psB[:, 0:256], lhsT=lhs(kc), rhs=w[:, kc, 7 * 128:9 * 128],
                                         start=(kc == 0), stop=(kc == 7))) for kc in range(8)],
                  R=[uT.s(tt)] + [w.s(g) for g in (7, 8)], W=[psB.s()])
            z = tz[i2]
            P.op("act", "activation", R=[psA.s()], W=[z.s()], out=z[:, 0:128], in_=psA[:, 0:128], func=AF.Sigmoid)
            P.op("act", "activation", R=[psB.s(), z.s()], W=[z.s()], out=z[:, 128:384], in_=psB[:, 0:256],
                 func=AF.Sigmoid)
            P.op("act", "activation", R=[psA.s()], W=[sq4[i2].s()], out=sq4[i2][:], in_=psA[:, 128:384],
                 func=AF.Square)
            P.op("act", "activation", R=[psA.s()], W=[vd.s(tt)], out=vd[:, tt, 0:128], in_=psA[:, 384:512],
                 func=AF.Copy)
            P.op("dve", "tensor_tensor", R=[psA.s(), z.s()], W=[t1[i2].s()], out=t1[i2][:], in0=psA[:, 0:128],
                 in1=z[:, 0:128], op=ALU.mult)
            P.op("pool", "tensor_tensor", R=[z.s(), gnw.s()], W=[t2[i2].s()], out=t2[i2][:], in0=z[:, 128:256],
                 in1=gnw[:], op=ALU.mult)
            P.op("pool", "tensor_tensor", R=[t1[i2].s(), t2[i2].s()], W=[GA.s(tt)], out=GA[:, tt, :],
                 in0=t1[i2][:], in1=t2[i2][:], op=ALU.mult)
            P.op("pool", "tensor_tensor", R=[z.s(), slw8.s()], W=[GBt.s(tt)], out=GBt[:, tt, :],
                 in0=z[:, 256:384], in1=slw8[:], op=ALU.mult)
            r4 = rs4[i2]
            P.op("dve", "tensor_reduce", R=[sq4[i2].s()], W=[r4.s()], out=r4[:],
                 in_=sq4[i2][:].rearrange("p (a b) -> p a b", a=4), axis=AX.X, op=ALU.add)
            P.op("dve", "tensor_scalar", R=[r4.s()], W=[r4.s()], out=r4[:], in0=r4[:], scalar1=1.0 / 64,
                 scalar2=EPS, op0=ALU.mult, op1=ALU.add)
            P.op("pool", "tensor_tensor", R=[r4.s(), cm05.s()], W=[r4.s()], out=r4[:], in0=r4[:],
                 in1=cm05[:, 0:4], op=ALU.pow)

        def s3(tt):
            i2 = tt % 2
            psA = pst.pop(tt)
            r4 = rs4[i2]
            qn = qkn[i2]
            for sgi in range(4):
                wsl = qkw[:, 0:64] if sgi < 2 else qkw[:, 64:128]
                P.op("dve", "scalar_tensor_tensor", R=[psA.s(), r4.s(), qkw.s()], W=[qn.s()],
                     out=qn[:, sgi * 64:(sgi + 1) * 64], in0=psA[:, 128 + sgi * 64:128 + (sgi + 1) * 64],
                     scalar=r4[:, sgi:sgi + 1], in1=wsl, op0=ALU.mult, op1=ALU.mult)
            ps = P.psum()
            psb = ps.ap.bitcast(BF16)
            P.ops("pe", [("transpose", dict(out=psb[:, j * 128:(j + 1) * 128], in_=qn[:, j * 128:(j + 1) * 128],
                                            identity=identb[:])) for j in range(2)],
                  R=[qn.s(), identb.s()], W=[ps.s()])
            P.op("act", "activation", R=[ps.s()], W=[dqT.s(tt)], out=dqT[:, tt * 128:(tt + 1) * 128],
                 in_=psb[:, 0:128], func=AF.Copy)
            P.op("dve", "tensor_copy", R=[ps.s()], W=[dkT.s(tt)], out=dkT[:, tt * 128:(tt + 1) * 128],
                 in_=psb[:, 128:256])

        for tt in range(NT):
            s12(tt)
            s3(tt)
            yield 2.0

    def inproj_F(h):
        w = wh[h % 2]
        for g in range(3):
            for j in range(4):
                col = (g * 8 + h) * 4 + j
                P.op("act", "activation", R=[identb.s(), convw.s()], W=[dg.s(g)], out=dg[:, g * 4 + j, :],
                     in_=identb[:], func=AF.Copy, scale=convw[:, col:col + 1])
        for g in range(3):
            xs = g % 2
            for tc in range(4):
                ps = P.psum()
                P.ops("pe", [("matmul", dict(out=ps[:, 0:512], lhsT=w[:, kc, g * 128:(g + 1) * 128],
                                             rhs=uT[:, kc, tc * 512:(tc + 1) * 512], start=(kc == 0),
                                             stop=(kc == 7))) for kc in range(8)],
                      R=[w.s(g)] + [uT.s(tt) for tt in range(4 * tc, 4 * tc + 4)], W=[ps.s()])
                evac_copy(xc[:, xs, 3 + tc * 512:3 + (tc + 1) * 512], ps[:, 0:512], [ps.s()], [xc.s(xs, tc)])
            for tc in range(4):
                ps = P.psum()
                R = [dg.s(g), xc.s(xs, tc)] + ([xc.s(xs, tc - 1)] if tc > 0 else [xc.s(xs, "pad")])
                P.ops("pe", [("matmul", dict(out=ps[:, 0:512], lhsT=dg[:, g * 4 + j, :],
                                             rhs=xc[:, xs, tc * 512 + j:tc * 512 + j + 512], start=(j == 0),
                                             stop=(j == 3))) for j in range(4)], R=R, W=[ps.s()])
                sg = sgt[0]
                P.op("act", "activation", R=[ps.s()], W=[sg.s()], out=sg[:], in_=ps[:, 0:512], func=AF.Sigmoid)
                P.op("dve", "tensor_tensor", R=[ps.s(), sg.s()], W=[kqv.s(g, tc)],
                     out=kqv[:, g, tc * 512:(tc + 1) * 512], in0=ps[:, 0:512], in1=sg[:], op=ALU.mult)

    def tsl(i):
        return tsc[:, i, :]

    def gdn_scalars(h):
        for g in range(2):
            P.op("act", "activation", R=[kqv.s(g, tc) for tc in range(4)], W=[xc.s(g, tc) for tc in range(4)],
                 out=xc[:, g, 3:3 + S], in_=kqv[:, g, :], func=AF.Square)
        ps = P.psum()
        P.ops("pe", [("matmul", dict(out=ps[:, tt * 2 + g:tt * 2 + g + 1],
                                     lhsT=xc[:, g, 3 + tt * 128:3 + (tt + 1) * 128],
                                     rhs=onesb[:, 0:1], start=True, stop=True))
                     for tt in range(NT) for g in range(2)],
              R=[xc.s(g, tc) for g in range(2) for tc in range(4)] + [onesb.s()], W=[ps.s()])
        P.op("act", "activation", R=[ps.s()], W=[ssq.s()], out=ssq[:].rearrange("p a b -> p (a b)"),
             in_=ps[:, 0:2 * NT], func=AF.Ln, bias=epsc[:, 0:1], scale=1.0)
        T = tsc.s()
        gch = gcs[:, :, h]
        lbh = lbs[:, :, h]
        glh = glt[:, :, h]
        P.op("dve", "tensor_scalar", R=[ssq.s()], W=[T], out=tsl(TS_LRQ), in0=ssq[:, :, 0], scalar1=-0.5,
             scalar2=-0.5 * float(np.log(128.0)), op0=ALU.mult, op1=ALU.add)
        P.op("dve", "tensor_scalar", R=[ssq.s(), T], W=[T], out=tsl(TS_LRK), in0=ssq[:, :, 1], scalar1=-0.5,
             scalar2=None, op0=ALU.mult)
        P.op("dve", "tensor_tensor", R=[gcs.s(), T], W=[T], out=tsl(TS_CQ), in0=gch, in1=tsl(TS_LRQ), op=ALU.add)
        P.op("dve", "tensor_tensor", R=[gcs.s(), T], W=[T], out=tsl(TS_TMP), in0=gch, in1=tsl(TS_LRK), op=ALU.add)
        P.op("dve", "tensor_tensor", R=[lbs.s(), T], W=[T], out=tsl(TS_CA), in0=tsl(TS_TMP), in1=lbh, op=ALU.add)
        P.op("dve", "tensor_tensor", R=[gcs.s(), T], W=[T], out=tsl(TS_BJ), in0=tsl(TS_LRK), in1=gch,
             op=ALU.subtract)
        P.op("act", "activation", R=[T], W=[T], out=tsl(TS_SK1), in_=tsl(TS_CA), func=AF.Exp)
        P.op("dve", "tensor_tensor", R=[glt.s(), T], W=[T], out=tsl(TS_TMP), in0=tsl(TS_BJ), in1=glh, op=ALU.add)
        P.op("act", "activation", R=[T], W=[T], out=tsl(TS_SK2), in_=tsl(TS_TMP), func=AF.Exp)
        P.op("act", "activation", R=[lbs.s(), T], W=[T], out=tsl(TS_BETA), in_=lbh, func=AF.Exp)
        P.op("act", "activation", R=[T], W=[T], out=tsl(TS_F), in_=tsl(TS_CQ), func=AF.Exp)

    def gdn_prep(h):
        T = tsc.s()
        for t0 in range(0, NT, PREP_B):
            tiles = list(range(t0, t0 + PREP_B))
            for i, tt in enumerate(tiles):
                tc = tt // 4
                tsl_ = slice(tt * 128, (tt + 1) * 128)
                ps = P.psum()
                psb = ps.ap.bitcast(BF16)
                P.ops("pe", [("transpose", dict(out=psb[:, 0:128], in_=kqv[:, 1, tsl_], identity=identb[:])),
                             ("transpose", dict(out=psb[:, 128:256], in_=kqv[:, 2, tsl_], identity=identb[:]))],
                      R=[kqv.s(1, tc), kqv.s(2, tc), identb.s()], W=[ps.s()])
                P.op("act", "activation", R=[ps.s(), T], W=[kbg.s(tt)], out=kbg[:, tt, :], in_=psb[:, 0:128],
                     func=AF.Copy, scale=tsc[:, TS_SK1, tt:tt + 1])
                P.op("dve", "tensor_scalar", R=[ps.s(), T], W=[kdec.s(tt)], out=kdec[:, tt, :], in0=psb[:, 0:128],
                     scalar1=tsc[:, TS_SK2, tt:tt + 1], scalar2=None, op0=ALU.mult)
                P.op("act", "activation", R=[ps.s(), T], W=[vb.s(tt)], out=vb[:, tt, :], in_=psb[:, 128:256],
                     func=AF.Copy, scale=tsc[:, TS_BETA, tt:tt + 1])
                d = dgf[i % 2]
                P.op("act", "activation", R=[CM, T], W=[d.s()], out=d[:, 0:128], in_=ident_f, func=AF.Copy,
                     scale=tsc[:, TS_CA, tt:tt + 1])
                P.op("act", "activation", R=[CM, T, d.s()], W=[d.s()], out=d[:, 128:256], in_=ident_f, func=AF.Copy,
                     scale=tsc[:, TS_CQ, tt:tt + 1])
                psE = P.psum()
                P.ops("pe", [("matmul", dict(out=psE[:, 0:128], lhsT=ones_f, rhs=d[:, 0:128], start=True, stop=True)),
                             ("matmul", dict(out=psE[:, 128:256], lhsT=ones_f, rhs=d[:, 128:256], start=True,
                                             stop=True))], R=[CM, d.s()], W=[psE.s()])
                e = eaq[i % 2]
                P.op("act", "activation", R=[psE.s(), T], W=[e.s()], out=e[:], in_=psE[:, 0:256], func=AF.Exp,
                     bias=tsc[:, TS_BJ, tt:tt + 1], scale=1.0)
                P.op("pool", "affine_select", R=[e.s()], W=[e.s()], out=e[:, 0:128], in_=e[:, 0:128],
                     pattern=[[1, 128]], compare_op=ALU.is_gt, fill=0.0, base=0, channel_multiplier=-1)
                P.op("pool", "affine_select", R=[e.s()], W=[e.s()], out=e[:, 128:256], in_=e[:, 128:256],
                     pattern=[[1, 128]], compare_op=ALU.is_ge, fill=0.0, base=0, channel_multiplier=-1)
                psK = P.psum()
                P.ops("pe", [("matmul", dict(out=psK[:, 0:128], lhsT=kqv[:, 1, tsl_], rhs=kqv[:, 1, tsl_],
                                             start=True, stop=True)),
                             ("matmul", dict(out=psK[:, 128:256], lhsT=kqv[:, 1, tsl_], rhs=kqv[:, 0, tsl_],
                                             start=True, stop=True))],
                      R=[kqv.s(0, tc), kqv.s(1, tc)], W=[psK.s()])
                w0 = wlv[i][0]
                P.op("dve", "scalar_tensor_tensor", R=[psK.s(), e.s()], W=[w0.s()], out=w0[:, 0:128],
                     in0=psK[:, 0:128], scalar=-1.0, in1=e[:, 0:128], op0=ALU.mult, op1=ALU.mult)
                P.op("dve", "tensor_tensor", R=[psK.s(), e.s()], W=[aqt.s(tt)], out=aqt[:, tt, :],
                     in0=psK[:, 128:256], in1=e[:, 128:256], op=ALU.mult)
                P.op("pool", "tensor_copy", R=[identb.s(), w0.s()], W=[w0.s()], out=w0[:, 128:256], in_=identb[:])
                psN = P.psum()
                psNb = psN.ap.bitcast(BF16)
                P.op("pe", "transpose", R=[w0.s(), identb.s()], W=[psN.s()], out=psNb[:, 0:128], in_=w0[:, 0:128],
                     identity=identb[:])
                P.op("act", "activation", R=[psN.s(), w0.s()], W=[w0.s()], out=w0[:, 256:384], in_=psNb[:, 0:128],
                     func=AF.Copy)
                yield 4.0
            for lvl in range(7):
                for i, tt in enumerate(tiles):
                    wc = wlv[i][lvl % 2]
                    wn = wlv[i][(lvl + 1) % 2]
                    ps = P.psum()
                    Mk, Rk, Nk = wc[:, 0:128], wc[:, 128:256], wc[:, 256:384]
                    calls = []
                    if lvl <= 4:
                        calls.append(("matmul", dict(out=ps[:, 0:256], lhsT=Nk, rhs=wc[:, 0:256], start=True, stop=True)))
                    else:
                        calls.append(("matmul", dict(out=ps[:, 128:256], lhsT=Nk, rhs=Rk, start=True, stop=True)))
                    if lvl <= 5:
                        calls.append(("matmul", dict(out=ps[:, 256:384], lhsT=Mk, rhs=Nk, start=True, stop=True)))
                    P.ops("pe", calls, R=[wc.s()], W=[ps.s()])
                    P.op("dve", "tensor_tensor", R=[ps.s(), wc.s()], W=[wn.s()], out=wn[:, 128:256],
                         in0=ps[:, 128:256], in1=Rk, op=ALU.add)
                    if lvl <= 4:
                        o3 = wn[:, 0:384].rearrange("p (a b) -> p a b", a=3)[:, 0:3:2, :]
                        i3 = ps[:, 0:384].rearrange("p (a b) -> p a b", a=3)[:, 0:3:2, :]
                    elif lvl == 5:
                        o3, i3 = wn[:, 256:384], ps[:, 256:384]
                    if lvl <= 5:
                        if i % 2 == 0:
                            P.op("act", "activation", R=[ps.s(), wn.s()], W=[wn.s()], out=o3, in_=i3, func=AF.Copy)
                        else:
                            P.op("dve", "tensor_copy", R=[ps.s(), wn.s()], W=[wn.s()], out=o3, in_=i3)
                yield 5.0
            for i, tt in enumerate(tiles):
                wf = wlv[i][1]
                ps = P.psum()
                P.ops("pe", [("matmul", dict(out=ps[:, 0:128], lhsT=wf[:, 128:256], rhs=vb[:, tt, :], start=True,
                                             stop=True)),
                             ("matmul", dict(out=ps[:, 128:256], lhsT=kbg[:, tt, :], rhs=wf[:, 128:256], start=True,
                                             stop=True))], R=[wf.s(), vb.s(tt), kbg.s(tt)], W=[ps.s()])
                P.op("act", "activation", R=[ps.s()], W=[u32.s(tt)], out=u32[:, tt, :], in_=ps[:, 0:128], func=AF.Copy)
                P.op("dve", "tensor_copy", R=[ps.s()], W=[wT.s(tt)], out=wT[:, tt * 128:(tt + 1) * 128],
                     in_=ps[:, 128:256])
            yield 2.0

    def gdn_recurrence(h):
        T = tsc.s()
        GA = GAb[h % 2]
        P.op("pool", "memset", W=[S32.s()], ap=S32[:], constant=0.0)
        P.op("pool", "memset", W=[Sb.s()], ap=Sb[:], constant=0.0)
        for tt in range(NT):
            tc = tt // 4
            for c in (1,):
                rows = slice(0, 128)
                cols = slice(tt * 128, (tt + 1) * 128)
                ps = P.psum()
                P.ops("pe", [("matmul", dict(out=ps[rows, 0:128], lhsT=wT[:, cols], rhs=Sb[:], start=True, stop=True)),
                             ("matmul", dict(out=ps[rows, 128:256], lhsT=kqv[:, 0, cols], rhs=Sb[:], start=True,
                                             stop=True))], R=[wT.s(tt), kqv.s(0, tc), Sb.s()], W=[ps.s()])
                P.op("dve", "tensor_tensor", R=[ps.s(), u32.s(tt)], W=[vn.s(tt % 2, c)], out=vn[rows, tt % 2, :],
                     in0=u32[rows, tt, :], in1=ps[rows, 0:128], op=ALU.subtract)
                P.op("act", "activation", R=[ps.s(), T], W=[o1s.s(tt % 2, c)], out=o1s[rows, tt % 2, :], in_=ps[rows, 128:256],
                     func=AF.Copy, scale=tsc[rows, TS_F, tt:tt + 1])
                psS = P.psum()
                P.op("pe", "matmul", R=[kdec.s(tt), vn.s(tt % 2, c)], W=[psS.s()], out=psS[:, 0:128],
                     lhsT=kdec[rows, tt, :], rhs=vn[rows, tt % 2, :], start=True, stop=True)
                eg = egs[:, tt, c, h:h + 1]
                P.op("dve", "scalar_tensor_tensor", R=[psS.s(), S32.s(), egs.s()], W=[Sb.s()], out=Sb[:],
                     in0=S32[:], scalar=eg, in1=psS[:, 0:128], op0=ALU.mult, op1=ALU.add)
                P.op("dve", "scalar_tensor_tensor", R=[psS.s(), S32.s(), egs.s()], W=[S32.s()], out=S32[:],
                     in0=S32[:], scalar=eg, in1=psS[:, 0:128], op0=ALU.mult, op1=ALU.add)
                yield 3.0
            ps = P.psum()
            P.op("pe", "matmul", R=[aqt.s(tt), vn.s(tt % 2, 1)], W=[ps.s()], out=ps[:, 0:128],
                 lhsT=aqt[:, tt, :], rhs=vn[:, tt % 2, :], start=True, stop=True)
            P.op("dve", "tensor_tensor", R=[ps.s(), o1s.s(tt % 2, 1)], W=[oa.s(tt)], out=oa[:, tt, :],
                 in0=ps[:, 0:128], in1=o1s[:, tt % 2, :], op=ALU.add)
            P.op("act", "activation", R=[oa.s(tt)], W=[junk2.s(), sso.s(tt)], out=junk2[:], in_=oa[:, tt, :],
                 func=AF.Square, accum_out=sso[:, tt:tt + 1])
        allso = [sso.s(tt) for tt in range(NT)]
        P.op("dve", "tensor_scalar", R=allso, W=[rso.s()], out=rso[:], in0=sso[:], scalar1=1.0 / 128,
             scalar2=EPS, op0=ALU.mult, op1=ALU.add)
        P.op("pool", "tensor_tensor", R=[rso.s(), cm05.s()], W=[rso.s()], out=rso[:], in0=rso[:],
             in1=cm05[:, 0:NT], op=ALU.pow)
        for tt in range(NT):
            P.op("dve", "scalar_tensor_tensor", R=[oa.s(tt), rso.s(), GA.s(tt)], W=[oa.s(tt)], out=oa[:, tt, :],
                 in0=oa[:, tt, :], scalar=rso[:, tt:tt + 1], in1=GA[:, tt, :], op0=ALU.mult, op1=ALU.mult)

    def attention(h):
        ptk = [0]
        for qc in range(4):
            accA = P.psum_pin()
            accB = P.psum_pin()
            accC = P.psum_pin()
            accs = (accA, accB, accC)

            def acc_ap(c, ql):
                if ql < 3:
                    return (accA, accB)[c], ql * 129
                return accC, c * 129
            first = {id(a): True for a in accs}
            nkb = 4 * qc + 4
            steps = [(kb, c) for kb in range(nkb) for c in range(2)]
            pbuf = {}

            def emit_qk(i):
                kb, c = steps[i]
                ql0 = max(0, kb - 4 * qc)
                ncol = (4 - ql0) * 128
                q0 = qc * 512 + ql0 * 128
                ps = P.psum()
                P.op("pe", "matmul", R=[dkT.s(kb)] + [dqT.s(4 * qc + ql) for ql in range(ql0, 4)], W=[ps.s()],
                     out=ps[:, 0:ncol], lhsT=dkT[c * 64:(c + 1) * 64, kb * 128:(kb + 1) * 128],
                     rhs=dqT[c * 64:(c + 1) * 64, q0:q0 + ncol], start=True, stop=True)
                p = pt[ptk[0] % len(pt)]
                ptk[0] += 1
                pbuf[i] = p
                P.op("act", "activation", R=[ps.s()], W=[p.s()], out=p[:, 0:ncol], in_=ps[:, 0:ncol], func=AF.Exp)
                if kb >= 4 * qc:
                    P.op("pool", "affine_select", R=[p.s()], W=[p.s()], out=p[:, 0:128], in_=p[:, 0:128],
                         pattern=[[1, 128]], compare_op=ALU.is_ge, fill=0.0, base=0, channel_multiplier=-1)

            def emit_pv(i):
                kb, c = steps[i]
                ql0 = max(0, kb - 4 * qc)
                p = pbuf.pop(i)
                calls = []
                touched = []
                for ql in range(ql0, 4):
                    a, off = acc_ap(c, ql)
                    st = first[id(a)]
                    first[id(a)] = False
                    calls.append(("matmul", dict(out=a[:, off:off + 129],
                                                 lhsT=p[:, (ql - ql0) * 128:(ql - ql0 + 1) * 128],
                                                 rhs=vd[:, kb, 0:129], start=st, stop=False,
                                                 skip_group_check=True)))
                    if a not in touched:
                        touched.append(a)
                P.ops("pe", calls, R=[p.s(), vd.s(kb)], W=[a.s() for a in touched])

            LOOK = ATT_LOOK
            nst = len(steps) + LOOK
            for i in range(nst):
                if i < len(steps):
                    emit_qk(i)
                if i - LOOK >= 0:
                    emit_pv(i - LOOK)
                yield 1.8
            QL = range(4)
            accp = {ql: (acc_ap(0, ql), acc_ap(1, ql)) for ql in QL}
            for ql in QL:
                (a0, off0), (a1, off1) = accp[ql]
                rs = rsum[ql]
                P.op("dve", "reciprocal", R=[a0.s()], W=[rs.s()], out=rs[:, 0:1], in_=a0[:, off0 + 128:off0 + 129])
                P.op("dve", "reciprocal", R=[a1.s(), rs.s()], W=[rs.s()], out=rs[:, 1:2],
                     in_=a1[:, off1 + 128:off1 + 129])
            for ql in QL:
                (a0, off0), (a1, off1) = accp[ql]
                rs = rsum[ql]
                P.op("act", "activation", R=[a0.s(), rs.s()], W=[ob[ql].s()], out=ob[ql][:], in_=a0[:, off0:off0 + 128],
                     func=AF.Copy, scale=rs[:, 0:1])
                P.op("act", "activation", R=[a1.s(), rs.s()], W=[tb[ql].s()], out=tb[ql][:], in_=a1[:, off1:off1 + 128],
                     func=AF.Copy, scale=rs[:, 1:2])
            for a in accs:
                P.psum_unpin(a)
            yield 1.0
            for ql in QL:
                qb = 4 * qc + ql
                P.op("dve", "scalar_tensor_tensor", R=[ob[ql].s(), tb[ql].s(), sc.s()], W=[ob16.s(qb)],
                     out=ob16[:, qb, :], in0=tb[ql][:], scalar=nlam, in1=ob[ql][:], op0=ALU.mult, op1=ALU.add)
                P.op("act", "activation", R=[ob16.s(qb)], W=[junk4.s(), rs16.s(qb)], out=junk4[:], in_=ob16[:, qb, :],
                     func=AF.Square, accum_out=rs16[:, qb:qb + 1])
            csl = slice(4 * qc, 4 * qc + 4)
            P.op("dve", "tensor_scalar", R=[rs16.s(4 * qc + ql) for ql in QL], W=[rr16.s(qc)], out=rr16[:, csl],
                 in0=rs16[:, csl], scalar1=1.0 / 128, scalar2=EPS, op0=ALU.mult, op1=ALU.add)
            P.op("pool", "tensor_tensor", R=[rr16.s(qc), cm05.s()], W=[rr16.s(qc)], out=rr16[:, csl], in0=rr16[:, csl],
                 in1=cm05[:, 0:4], op=ALU.pow)
            for ql in QL:
                qb = 4 * qc + ql
                P.op("dve", "scalar_tensor_tensor", R=[ob16.s(qb), rr16.s(qc), GBt.s(qb)], W=[ob16.s(qb)],
                     out=ob16[:, qb, :], in0=ob16[:, qb, :], scalar=rr16[:, qb:qb + 1], in1=GBt[:, qb, :],
                     op0=ALU.mult, op1=ALU.mult)
            yield 1.0

    def attn_post(h):
        for q0 in range(0, NT, 4):
            QB = range(q0, q0 + 4)
            for qb in QB:
                m = ob[qb % 4]
                P.op("pool", "tensor_tensor", R=[ob16.s(qb), oa.s(qb)], W=[m.s()], out=m[:], in0=ob16[:, qb, :],
                     in1=oa[:, qb, :], op=ALU.add)
            for qb in QB:
                m = ob[qb % 4]
                ps = P.psum()
                P.op("pe", "transpose", R=[m.s(), CM], W=[ps.s()], out=ps[:, 0:128], in_=m[:], identity=ident_f)
                evac_copy(mixT[:, h, qb * 128:(qb + 1) * 128], ps[:, 0:128], [ps.s()], [mixT.s(h, qb)])

    def interleave(ga, gb):
        ca = cb = 0.0
        da = db = False
        while not (da and db):
            if not da and (db or ca <= cb):
                try:
                    P.tag = "gdn"
                    ca += next(ga)
                except StopIteration:
                    da = True
            elif not db:
                try:
                    P.tag = "attn+inT"
                    cb += next(gb) * BSCALE
                except StopIteration:
                    db = True

    def chain(*gens):
        for g in gens:
            yield from g

    heads = list(range(NHEADS)) if stop_after not in ("B0",) else [0]
    load_head_weights(heads[0])
    P.tag = "inproj"
    for _ in inproj_T(heads[0]):
        pass
    inproj_F(heads[0])
    for hi, h in enumerate(heads):
        nxt = heads[hi + 1] if hi + 1 < len(heads) else None
        if nxt is not None:
            load_head_weights(nxt)
        P.mark_phase(f"h{h}.mix")
        P.tag = "mix"
        gdn_scalars(h)
        sb = [attention(h)] + ([inproj_T(nxt)] if nxt is not None else [])
        interleave(chain(gdn_prep(h), gdn_recurrence(h)), chain(*sb))
        P.tag = "attn_post"
        attn_post(h)
        if nxt is not None:
            P.tag = "inproj"
            inproj_F(nxt)
        if "oa" in taps and h == 0:
            tap("oa", oa, oa[:], [128, NT, 128], [oa.s(tt) for tt in range(NT)])
            tap("u32", u32, u32[:], [128, NT, 128], [u32.s(tt) for tt in range(NT)])
    P.mark_phase("C")
    P.barrier()
    if stop_after in ("B", "B0"):
        d = nc.dram_tensor("tap_mixT", [128, H, S], BF16, kind="ExternalOutput").ap()
        P.dma("sp", out=d, in_=mixT[:])
        P.barrier()
        P.emit()
        return nc, P
    P.release(mB)

    h1 = P.sbuf("h1", [128, NT, D], F32)
    mC = P.mark()
    wout = P.sbuf("wout", [128, 8, D], BF16)
    wrow2 = P.sbuf("wrow2", [128, D], F32)
    xb = [P.sbuf(f"xb{i}", [128, D], F32) for i in range(2)]
    xn2 = [P.sbuf(f"xn2{i}", [128, D], F32) for i in range(2)]
    ssB = P.sbuf("ssB", [128, NT], F32)
    rstd2 = P.sbuf("rstd2", [128, NT], F32)
    junk3 = P.sbuf("junk3", [128, D], BF16)
    P.dma("sp", W=[wrow2.s()], out=wrow2[:], in_=norm2_d.partition_broadcast(128))
    for hh in range(8):
        P.dma("pool", W=[wout.s(hh)], out=wout[:, hh, :], in_=w_out_d[hh * 128:(hh + 1) * 128, :])
    for tt in range(NT):
        x_ = xb[tt % 2]
        P.dma("sp", W=[x_.s()], out=x_[:], in_=x_d[tt * 128:(tt + 1) * 128, :])
        for n in range(2):
            ps = P.psum()
            P.ops("pe", [("matmul", dict(out=ps[:, 0:512], lhsT=mixT[:, hh, tt * 128:(tt + 1) * 128],
                                         rhs=wout[:, hh, n * 512:(n + 1) * 512], start=(hh == 0), stop=(hh == 7)))
                         for hh in range(8)],
                  R=[mixT.s(hh, tt) for hh in range(8)] + [wout.s(hh) for hh in range(8)], W=[ps.s()])
            P.op("dve", "tensor_tensor", R=[ps.s(), x_.s()], W=[h1.s(tt, n)], out=h1[:, tt, n * 512:(n + 1) * 512],
                 in0=ps[:, 0:512], in1=x_[:, n * 512:(n + 1) * 512], op=ALU.add)
        P.op("act", "activation", R=[h1.s(tt, 0), h1.s(tt, 1)], W=[junk3.s(), ssB.s(tt)], out=junk3[:],
             in_=h1[:, tt, :], func=AF.Square, accum_out=ssB[:, tt:tt + 1])
    for tt in range(NT):
        P.op("dve", "tensor_scalar", R=[ssB.s(tt)], W=[rstd2.s(tt)], out=rstd2[:, tt:tt + 1], in0=ssB[:, tt:tt + 1],
             scalar1=1.0 / D, scalar2=EPS, op0=ALU.mult, op1=ALU.add)
        P.op("pool", "tensor_tensor", R=[rstd2.s(tt), cm05.s()], W=[rstd2.s(tt)], out=rstd2[:, tt:tt + 1],
             in0=rstd2[:, tt:tt + 1], in1=cm05[:, 0:1], op=ALU.pow)
    for tt in range(NT):
        xn = xn2[tt % 2]
        P.op("dve", "scalar_tensor_tensor", R=[h1.s(tt, 0), h1.s(tt, 1), rstd2.s(tt), wrow2.s()], W=[xn.s()],
             out=xn[:], in0=h1[:, tt, :], scalar=rstd2[:, tt:tt + 1], in1=wrow2[:], op0=ALU.mult, op1=ALU.mult)
        for half in range(2):
            ps = P.psum()
            P.ops("pe", [("transpose", dict(out=ps[:, j * 128:(j + 1) * 128],
                                            in_=xn[:, (half * 4 + j) * 128:(half * 4 + j + 1) * 128],
                                            identity=ident_f)) for j in range(4)],
                  R=[xn.s(), CM], W=[ps.s()])
            evac_copy(uT[:, half * 4:(half + 1) * 4, tt * 128:(tt + 1) * 128],
                      ps[:, 0:512].rearrange("p (k t) -> p k t", k=4), [ps.s()], [uT.s(tt, half)])
    P.barrier()
    if stop_after == "C":
        for tt in range(NT):
            P.dma("sp", out=out_d[tt * 128:(tt + 1) * 128, :], in_=h1[:, tt, :])
        d = nc.dram_tensor("tap_uT2", [128, 8, S], BF16, kind="ExternalOutput").ap()
        P.dma("sp", out=d, in_=uT[:])
        P.barrier()
        P.emit()
        return nc, P
    P.release(mC)

    P.mark_phase("D")
    r1 = MIXT_OFF
    wgu = []
    for i in range(4):
        a, r1 = P.sbuf_at(f"wg{i}", [128, 8, 128], BF16, r1)
        b, r1 = P.sbuf_at(f"wu{i}", [128, 8, 128], BF16, r1)
        wgu.append((a, b))
    sil = []
    for i in range(2):
        a, r1 = P.sbuf_at(f"sil{i}", [128, 512], BF16, r1)
        sil.append(a)
    ost = []
    for i in range(2):
        a, r1 = P.sbuf_at(f"ost{i}", [128, 256], F32, r1)
        ost.append(a)
    assert r1 <= MIXT_OFF + H * S * 2
    actT = P.sbuf("actT", [128, NFC, 1024], BF16)
    wdb = [P.sbuf(f"wdb{i}", [128, NFC, 256], BF16) for i in range(2)]
    wdk = 0
    for hf in range(2 if DCUT == 0 else 1):
        for fc in range(NFC):
            wg_, wu_ = wgu[fc % WRING]
            P.dma("pool", W=[wg_.s()], out=wg_[:],
                  in_=w_gate_d[:, fc * 128:(fc + 1) * 128].rearrange("(k p) c -> p k c", p=128))
            P.dma("pool", W=[wu_.s()], out=wu_[:],
                  in_=w_up_d[:, fc * 128:(fc + 1) * 128].rearrange("(k p) c -> p k c", p=128))
            for tcl in range(2):
                tok = slice(hf * 1024 + tcl * 512, hf * 1024 + (tcl + 1) * 512)
                tts = [(hf * 1024 + tcl * 512) // 128 + j for j in range(4)]
                Ru = [uT.s(tt, half) for tt in tts for half in range(2)]
                psg = P.psum()
                psu = P.psum()
                P.ops("pe", [("matmul", dict(out=psg[:, 0:512], lhsT=wg_[:, kc, :], rhs=uT[:, kc, tok],
                                             start=(kc == 0), stop=(kc == 7))) for kc in range(8)],
                      R=Ru + [wg_.s()], W=[psg.s()])
                P.ops("pe", [("matmul", dict(out=psu[:, 0:512], lhsT=wu_[:, kc, :], rhs=uT[:, kc, tok],
                                             start=(kc == 0), stop=(kc == 7))) for kc in range(8)],
                      R=Ru + [wu_.s()], W=[psu.s()])
                sl = sil[tcl]
                P.op("act", "activation", R=[psg.s()], W=[sl.s()], out=sl[:], in_=psg[:, 0:512], func=AF.Silu)
                P.op("dve", "tensor_tensor", R=[psu.s(), sl.s()], W=[actT.s(fc, tcl)],
                     out=actT[:, fc, tcl * 512:(tcl + 1) * 512], in0=psu[:, 0:512], in1=sl[:], op=ALU.mult)
        for n4 in range(4 if DCUT != 1 else 0):
            wd_ = wdb[wdk % 2]
            wdk += 1
            for fc in range(NFC):
                P.dma("pool", ndesc=8, W=[wd_.s(fc)], out=wd_[:, fc, :],
                      in_=w_down_d[fc * 128:(fc + 1) * 128, n4 * 256:(n4 + 1) * 256])
            for tl in range(8):
                tt = hf * 8 + tl
                ps = P.psum()
                P.ops("pe", [("matmul", dict(out=ps[:, 0:256], lhsT=actT[:, fc, tl * 128:(tl + 1) * 128],
                                             rhs=wd_[:, fc, :], start=(fc == 0), stop=(fc == NFC - 1)))
                             for fc in range(NFC)],
                      R=[actT.s(fc, tl // 4) for fc in range(NFC)] + [wd_.s(fc) for fc in range(NFC)], W=[ps.s()])
                o_ = ost[(n4 * 8 + tl) % 2]
                P.op("dve", "tensor_tensor", R=[ps.s(), h1.s(tt, n4 // 2)], W=[o_.s()], out=o_[:], in0=ps[:, 0:256],
                     in1=h1[:, tt, n4 * 256:(n4 + 1) * 256], op=ALU.add)
                P.dma("sp", R=[o_.s()], out=out_d[tt * 128:(tt + 1) * 128, n4 * 256:(n4 + 1) * 256], in_=o_[:])
    P.barrier()
    P.emit()
    return nc, P


def _host_consts():
    ident = np.eye(128, dtype=np.float32)
    p = np.arange(128)
    ltri = (p[:, None] <= p[None, :]).astype(np.float32)
    ones = np.ones((128, 128), np.float32)
    sel63 = np.zeros((128, 128), np.float32)
    sel63[63, :] = 1.0
    sel127 = np.zeros((128, 128), np.float32)
    sel127[127, :] = 1.0
    return np.ascontiguousarray(np.concatenate([ident, ltri, ones, sel63, sel127], axis=1))


def make_in_maps(inputs, cores):
    f = lambda k: np.ascontiguousarray(np.asarray(inputs[k], dtype=np.float32))
    conv_w = f("conv_w")[0]
    conv_wl = np.ascontiguousarray(conv_w.T.reshape(24, 128, 4).transpose(1, 0, 2).reshape(128, 96))
    lqk = np.ascontiguousarray(np.concatenate([f("lambda_q1")[0], f("lambda_k1")[0], f("lambda_q2")[0],
                                               f("lambda_k2")[0]])[None, :])
    shared = {
        "w_in": f("w_in")[0], "w_out": f("w_out")[0], "w_gate": f("w_gate")[0], "w_up": f("w_up")[0],
        "w_down": f("w_down")[0], "norm1_w": f("norm1_w"), "norm2_w": f("norm2_w"),
        "gdn_norm_w": f("gdn_norm_w"), "subln_w": f("subln_w"), "q_norm_w": f("q_norm_w"),
        "k_norm_w": f("k_norm_w"), "lqk": lqk, "a_log": f("a_log"), "dt_bias": f("dt_bias"),
        "conv_wl": conv_wl, "cmat": _host_consts(),
    }
    x = f("x")
    return [dict(shared, x=np.ascontiguousarray(x[b])) for b in cores]


def kernel(**inputs):
    nc, _ = build_program()
    in_maps = make_in_maps(inputs, list(range(8)))
    res = run_bass_kernel_spmd(nc, in_maps, core_ids=list(range(8)))
    return np.stack([np.asarray(r["out"], dtype=np.float32) for r in res.results], axis=0)
```

```python
import contextlib
import numpy as np
import concourse.bass as bass
import concourse.mybir as mybir
from concourse.bass_utils import run_bass_kernel_spmd

F32 = mybir.dt.float32
BF16 = mybir.dt.bfloat16
AF = mybir.ActivationFunctionType
ALU = mybir.AluOpType
AX = mybir.AxisListType

SBUF_LO = 16512 + 64
SBUF_HI = 229376

S = 2048
D = 1024
NT = 16
H = 8
DFF = 2816
NFC = DFF // 128
EPS = 1e-6
ATT_LOOK = 3
INP_LAG = 1
NHEADS = 8
DCUT = 0
SAME_ENG_DIST = 3
OVERLAP = 0
SCHED = 1
PRIO = 1
BSCALE = 1
PREP_B = 8
PE_C0 = 0.036
PE_RATE = 2800
WRING = 2
D_IN = 9232
COL = dict(gq=0, gk=1024, gv=2048, gz=3072, ga=4096, gb=4104, dq=4112, dk=5136, dv=6160,
           gate_a=7184, gate_b=8208)
HEAD_GROUPS = ("gq", "gk", "gv", "gz", "dq", "dk", "dv", "gate_a", "gate_b")


class Buf:
    __slots__ = ("name", "lw", "rd", "excl")

    def __init__(self, name, excl=False):
        self.name = name
        self.lw = None
        self.rd = set()
        self.excl = excl


class TT:
    def __init__(self, prog, name, h, excl=False):
        self.prog = prog
        self.name = name
        self.h = h
        self.ap = h.ap()
        self.slots = {}
        self.excl = excl

    def s(self, *key):
        b = self.slots.get(key)
        if b is None:
            b = Buf(f"{self.name}{key}", self.excl)
            self.slots[key] = b
            self.prog.allbufs.append(b)
        return b

    def __getitem__(self, idx):
        return self.ap[idx]


class Prog:
    COMPUTE = ("pe", "act", "dve", "pool")
    NDMASEM = 48

    def __init__(self, nc, same_engine_sync=True):
        self.nc = nc
        self.items = {k: [] for k in ("pe", "act", "dve", "pool", "sp")}
        self.seq = {k: 0 for k in self.COMPUTE}
        self.waited = {k: {} for k in self.items}
        self.allbufs = []
        self.same_engine_sync = same_engine_sync
        self.dma_tot = [0] * self.NDMASEM
        self.dma_rr = 0
        self.sb_ptr = SBUF_LO
        self.sb_peak = SBUF_LO
        self.nalloc = 0
        self.psum_banks = []
        self.ps_rr = 0
        self.ps_pinned = set()
        self.n_wait = 0
        self.n_ops = 0
        self.n_pe = 0
        self.marks = []
        self.pool_out = []
        self.nodes = []
        self.seg_bounds = []
        self.seg_stats = []
        self._busy = {}
        self.tag = ""
        self.crit = []

    def sbuf(self, name, shape, dtype):
        esz = 4 if dtype == F32 else 2
        per_part = int(np.prod(shape[1:])) * esz
        off = (self.sb_ptr + 63) // 64 * 64
        assert off + per_part <= SBUF_HI, f"SBUF overflow allocating {name}: {off}+{per_part}"
        self.sb_ptr = off + per_part
        self.sb_peak = max(self.sb_peak, self.sb_ptr)
        self.nalloc += 1
        h = self.nc.alloc_sbuf_tensor_at(f"{name}_{self.nalloc}", list(shape), dtype, offset=off)
        return TT(self, name, h)

    def sbuf_at(self, name, shape, dtype, off):
        esz = 4 if dtype == F32 else 2
        per_part = int(np.prod(shape[1:])) * esz
        assert off % 64 == 0 and off + per_part <= SBUF_HI
        self.nalloc += 1
        h = self.nc.alloc_sbuf_tensor_at(f"{name}_{self.nalloc}", list(shape), dtype, offset=off)
        return TT(self, name, h), off + per_part

    def mark(self):
        return self.sb_ptr

    def release(self, mark):
        self.sb_ptr = mark

    def init_psum(self):
        for i in range(8):
            h = self.nc.alloc_psum_tensor(f"psb{i}", [128, 512], F32)
            self.psum_banks.append(TT(self, f"psb{i}", h, excl=True))

    def psum(self):
        for _ in range(8):
            i = self.ps_rr
            self.ps_rr = (self.ps_rr + 1) % 8
            if i not in self.ps_pinned:
                return self.psum_banks[i]
        raise RuntimeError("all psum pinned")

    def psum_pin(self):
        t = self.psum()
        self.ps_pinned.add(self.psum_banks.index(t))
        return t

    def psum_unpin(self, t):
        self.ps_pinned.discard(self.psum_banks.index(t))

    def _node(self, eng, fn, reads, writes, cost, is_dma=False, ndesc=0, lat=0.0):
        writes = [w for w in writes if w is not None] + [r for r in reads if r is not None and r.excl]
        reads = [r for r in reads if r is not None and not r.excl]
        nid = len(self.nodes)
        deps = set()
        for r in reads:
            if r.lw is not None:
                deps.add(r.lw)
        for w in writes:
            if w.lw is not None:
                deps.add(w.lw)
            deps.update(w.rd)
        deps.discard(nid)
        self.nodes.append(dict(id=nid, eng=eng, fn=fn, deps=deps, cost=cost, dma=is_dma, ndesc=ndesc, lat=lat,
                               tag=self.tag))
        for r in reads:
            r.rd.add(nid)
        for w in writes:
            w.lw = nid
            w.rd = set()
        return nid

    @staticmethod
    def _nfree(ap):
        try:
            sh = list(ap.shape)
            n = 1
            for d in sh[1:]:
                n *= int(d)
            return n
        except Exception:
            return 128

    def _cost(self, eng, method, kw):
        out = kw.get("out", kw.get("ap"))
        n = self._nfree(out) if out is not None else 128
        if eng == "pe":
            if method == "transpose":
                return 0.12
            f32 = False
            try:
                f32 = kw["lhsT"].tensor.dtype == F32
            except Exception:
                pass
            return PE_C0 + (4.0 if f32 else 1.0) * max(64, n) / PE_RATE
        if eng == "act":
            return 0.22 + n / 1400.0 + (0.1 if "accum_out" in kw else 0.0)
        if eng == "dve":
            return 0.12 + n / 960.0 * (8.0 if method == "reciprocal" else 1.0)
        return (0.9 if method == 'tensor_tensor' and n <= 16 else 0.3) + n / 700.0

    def op(self, eng, method, R=(), W=(), **kw):
        def fn(e, method=method, kw=kw):
            return getattr(e, method)(**kw)
        self.n_ops += 1
        if eng == "pe":
            self.n_pe += 1
        self._node(eng, fn, R, W, self._cost(eng, method, kw))

    def mark_phase(self, name):
        self.marks.append((name, self.n_pe))

    def ops(self, eng, calls, R=(), W=()):
        calls = list(calls)
        if eng == "pe":
            self.n_pe += len(calls)
        self.n_ops += 1

        def fn(e, calls=calls):
            ins = None
            for (method, kw) in calls:
                ins = getattr(e, method)(**kw)
            return ins
        self._node(eng, fn, R, W, sum(self._cost(eng, m, kw) for m, kw in calls))

    def dma(self, queue, R=(), W=(), ndesc=64, **kw):
        def fn(e, kw=kw):
            return e.dma_start(**kw)
        nbytes = self._nfree(kw["out"]) * 128 * 4
        self._node(queue, fn, R, W, 1.2 if queue == "pool" else 0.1, is_dma=True, ndesc=ndesc,
                   lat=2.0 + nbytes / 1.5e5)

    def barrier(self):
        self.seg_bounds.append(len(self.nodes))
        for b in self.allbufs:
            b.lw = None
            b.rd = set()

    HOP = 0.35
    POOL_DESC_CAP = 200

    def _schedule_segment(self, lo, hi):
        import heapq
        if not SCHED:
            return list(range(lo, hi))
        nodes = self.nodes
        ndep = {}
        users = {}
        for n in nodes[lo:hi]:
            d = [x for x in n["deps"] if x >= lo]
            ndep[n["id"]] = len(d)
            for x in d:
                users.setdefault(x, []).append(n["id"])
        blevel = {}
        for n in reversed(nodes[lo:hi]):
            nid = n["id"]
            best = 0.0
            for u in users.get(nid, ()):
                un = nodes[u]
                lat = 0.05 if (un["eng"] == n["eng"] and not n["dma"]) else self.HOP
                v = lat + blevel[u]
                if v > best:
                    best = v
            blevel[nid] = n["cost"] + n["lat"] + best
        fin = {}
        ready_t = {}
        engs = ("pe", "act", "dve", "pool", "sp")
        avail = {e: [] for e in engs}
        free_t = {e: 0.0 for e in engs}
        for n in nodes[lo:hi]:
            if ndep[n["id"]] == 0:
                avail[n["eng"]].append(n["id"])
                ready_t[n["id"]] = 0.0
        order = []
        last_on = {}
        crit_dep = {}
        remaining = hi - lo
        while remaining:
            best = None
            for e in engs:
                lst = avail[e]
                if not lst:
                    continue
                ft = free_t[e]
                cb = None
                for nid in lst:
                    rt = ready_t[nid]
                    st = rt if rt > ft else ft
                    key = (st, -blevel[nid] * PRIO, nid)
                    if cb is None or key < cb:
                        cb = key
                if best is None or cb < best[0]:
                    best = (cb, e)
            (st, _, nid), e = best
            avail[e].remove(nid)
            n = nodes[nid]
            rt = ready_t[nid]
            start = st
            n["start"] = start
            n["why"] = ("eng", last_on.get(e)) if free_t[e] >= rt and last_on.get(e) is not None else ("dep", crit_dep.get(nid))
            last_on[e] = nid
            free_t[e] = start + n["cost"]
            fin[nid] = start + n["cost"] + n["lat"]
            order.append(nid)
            remaining -= 1
            self._busy[e] = self._busy.get(e, 0.0) + n["cost"]
            for u in users.get(nid, ()):
                un = nodes[u]
                lat = 0.05 if (un["eng"] == e and not n["dma"]) else self.HOP
                if fin[nid] + lat >= ready_t.get(u, 0.0):
                    crit_dep[u] = nid
                ready_t[u] = max(ready_t.get(u, 0.0), fin[nid] + lat)
                ndep[u] -= 1
                if ndep[u] == 0:
                    avail[un["eng"]].append(u)
        self.seg_stats.append((lo, hi, max(fin.values()), dict(self._busy)))
        cur = max(fin, key=lambda k: fin[k])
        path = []
        while cur is not None:
            n = nodes[cur]
            path.append((cur, n["tag"], n["eng"], n["start"], n["cost"], n["why"][0]))
            cur = n["why"][1]
        self.crit.append(path[::-1])
        self._busy = {}
        return order

    def _lower(self):
        nodes = self.nodes
        bounds = [0] + [b for b in self.seg_bounds if b > 0]
        if bounds[-1] != len(nodes):
            bounds.append(len(nodes))
        items = {k: [] for k in ("pe", "act", "dve", "pool", "sp")}
        seq = {k: 0 for k in self.COMPUTE}
        waited = {k: {} for k in items}
        tok = {}
        dma_tot = [0] * self.NDMASEM
        dma_rr = 0
        dma_rr_sw = 0
        pool_out = []
        for si in range(len(bounds) - 1):
            lo, hi = bounds[si], bounds[si + 1]
            if hi <= lo:
                continue
            order = self._schedule_segment(lo, hi)
            for nid in order:
                n = nodes[nid]
                q = n["eng"]
                deps = [tok[d] for d in n["deps"] if d >= lo]
                if n["dma"]:
                    if q == "pool":
                        k = 16 + dma_rr_sw
                        dma_rr_sw = (dma_rr_sw + 1) % (self.NDMASEM - 16)
                    else:
                        k = dma_rr
                        dma_rr = (dma_rr + 1) % 16
                    key = f"dma{k}"
                    if dma_tot[k] > 0:
                        deps.append((key, dma_tot[k]))
                    if q == "pool":
                        while pool_out and sum(x for _, x in pool_out) + n["ndesc"] > self.POOL_DESC_CAP:
                            t, _ = pool_out.pop(0)
                            deps.append(t)
                    dma_tot[k] += 16
                    tok[nid] = (key, dma_tot[k])
                    semkey, inc = key, 16
                    if q == "pool":
                        pool_out.append((tok[nid], n["ndesc"]))
                else:
                    seq[q] += 1
                    tok[nid] = (q, seq[q])
                    semkey, inc = q, 1
                wd = waited[q]
                best = {}
                for (k, v) in deps:
                    if k == q:
                        if q == "pe":
                            continue
                        if q in ("act", "dve") and seq[q] - v >= SAME_ENG_DIST:
                            continue
                    if wd.get(k, 0) >= v:
                        continue
                    wd[k] = v
                    best[k] = max(best.get(k, 0), v)
                self.n_wait += len(best)
                items[q].append(("op", list(best.items()), n["fn"], semkey, inc))
            targets = [(k, seq[k]) for k in self.COMPUTE if seq[k] > 0]
            targets += [(f"dma{i}", v) for i, v in enumerate(dma_tot) if v > 0]
            for q in items:
                w = []
                for (k, v) in targets:
                    if k != q and waited[q].get(k, 0) < v:
                        waited[q][k] = v
                        w.append((k, v))
                if w:
                    items[q].append(("wait", w, None, None, 0))
        self.items = items

    def emit(self):
        nc = self.nc
        self._lower()
        with contextlib.ExitStack() as st:
            sems = {}
            for k in self.COMPUTE:
                sems[k] = st.enter_context(nc.semaphore(f"s_{k}"))
            for i in range(self.NDMASEM):
                sems[f"dma{i}"] = st.enter_context(nc.semaphore(f"s_dma{i}"))
            block = st.enter_context(nc.Block())
            items = self.items
            targets = {k: set() for k in self.COMPUTE}
            for lst in items.values():
                for (kind, waits, fn, semkey, inc) in lst:
                    for (k, v) in waits:
                        if k in targets:
                            targets[k].add(v)
            rank = {k: {v: i + 1 for i, v in enumerate(sorted(vs))} for k, vs in targets.items()}
            self.n_sig = {k: len(v) for k, v in targets.items()}

            def run(e, lst):
                idx = 0
                for (kind, waits, fn, semkey, inc) in lst:
                    for (k, v) in waits:
                        e.wait_ge(sems[k], rank[k][v] if k in rank else v)
                    if kind == "op":
                        ins = fn(e)
                        if semkey in rank:
                            idx += 1
                            if idx in rank[semkey]:
                                ins.then_inc(sems[semkey], 1)
                        else:
                            ins.then_inc(sems[semkey], inc)

            @block.tensor
            def _(e):
                run(e, items["pe"])

            @block.scalar
            def _(e):
                run(e, items["act"])

            @block.vector
            def _(e):
                run(e, items["dve"])

            @block.gpsimd
            def _(e):
                run(e, items["pool"])

            @block.sync
            def _(e):
                run(e, items["sp"])


def build_program(stop_after=None, taps=()):
    nc = bass.Bass("TRN2", target_bir_lowering=False)
    P = Prog(nc)
    P.init_psum()
    taps = set(taps)

    def din(name, shape):
        return nc.dram_tensor(name, list(shape), F32, kind="ExternalInput").ap()

    x_d = din("x", [S, D])
    w_in_d = din("w_in", [D, D_IN])
    w_out_d = din("w_out", [D, D])
    w_gate_d = din("w_gate", [D, DFF])
    w_up_d = din("w_up", [D, DFF])
    w_down_d = din("w_down", [DFF, D])
    norm1_d = din("norm1_w", [1, D])
    norm2_d = din("norm2_w", [1, D])
    gnw_d = din("gdn_norm_w", [1, 128])
    slw_d = din("subln_w", [1, 128])
    qnw_d = din("q_norm_w", [1, 64])
    knw_d = din("k_norm_w", [1, 64])
    lqk_d = din("lqk", [1, 256])
    alog_d = din("a_log", [1, 8])
    dtb_d = din("dt_bias", [1, 8])
    convw_d = din("conv_wl", [128, 96])
    cmat_d = din("cmat", [128, 5 * 128])
    out_d = nc.dram_tensor("out", [S, D], F32, kind="ExternalOutput").ap()
    tap_d = {}

    def tap(name, tt, sl, shape, reads):
        if name not in taps:
            return
        d = nc.dram_tensor("tap_" + name, list(shape), F32, kind="ExternalOutput").ap()
        tap_d[name] = d
        P.dma("sp", R=reads, out=d, in_=sl)

    cm = P.sbuf("cm", [128, 5, 128], F32)
    identb = P.sbuf("identb", [128, 128], BF16)
    onesb = P.sbuf("onesb", [128, 128], BF16)
    gnw = P.sbuf("gnw", [128, 128], F32)
    slw8 = P.sbuf("slw8", [128, 128], F32)
    qkw = P.sbuf("qkw", [128, 128], F32)
    lqk = P.sbuf("lqk", [128, 256], F32)
    alog = P.sbuf("alog", [128, 8], F32)
    dtb = P.sbuf("dtb", [128, 8], F32)
    convw = P.sbuf("convw", [128, 96], F32)
    sc = P.sbuf("sc", [128, 16], F32)
    negA = P.sbuf("negA", [128, 8], F32)
    cm05 = P.sbuf("cm05", [128, 64], F32)
    epsc = P.sbuf("epsc", [128, 1], F32)
    ident_f = cm[:, 0, :]
    ltri_f = cm[:, 1, :]
    ones_f = cm[:, 2, :]
    sel_f = [cm[:, 3, :], cm[:, 4, :]]
    CM = cm.s()

    P.dma("sp", W=[CM], out=cm[:].rearrange("p a b -> p (a b)"), in_=cmat_d[:, :])
    P.dma("pool", W=[identb.s()], out=identb[:], in_=cmat_d[:, 0:128])
    P.dma("pool", W=[onesb.s()], out=onesb[:], in_=cmat_d[:, 256:384])
    P.dma("sp", W=[gnw.s()], out=gnw[:], in_=gnw_d.partition_broadcast(128))
    P.dma("sp", W=[slw8.s()], out=slw8[:], in_=slw_d.partition_broadcast(128))
    P.dma("sp", W=[qkw.s()], out=qkw[:, 0:64], in_=qnw_d.partition_broadcast(128))
    P.dma("sp", W=[qkw.s()], out=qkw[:, 64:128], in_=knw_d.partition_broadcast(128))
    P.dma("sp", W=[lqk.s()], out=lqk[:], in_=lqk_d.partition_broadcast(128))
    P.dma("sp", W=[alog.s()], out=alog[:], in_=alog_d.partition_broadcast(128))
    P.dma("sp", W=[dtb.s()], out=dtb[:], in_=dtb_d.partition_broadcast(128))
    P.dma("sp", W=[convw.s()], out=convw[:], in_=convw_d[:, :])

    P.op("pool", "memset", W=[cm05.s()], ap=cm05[:], constant=-0.5)
    P.op("pool", "memset", W=[epsc.s()], ap=epsc[:], constant=EPS)
    P.op("pool", "tensor_scalar", R=[slw8.s()], W=[slw8.s()], out=slw8[:], in0=slw8[:], scalar1=0.8,
         scalar2=None, op0=ALU.mult)
    P.op("pool", "tensor_scalar", R=[qkw.s()], W=[qkw.s()], out=qkw[:, 0:64], in0=qkw[:, 0:64],
         scalar1=64 ** -0.5, scalar2=None, op0=ALU.mult)
    P.op("dve", "tensor_tensor", R=[lqk.s()], W=[lqk.s()], out=lqk[:, 0:64], in0=lqk[:, 0:64],
         in1=lqk[:, 64:128], op=ALU.mult)
    P.op("dve", "tensor_tensor", R=[lqk.s()], W=[lqk.s()], out=lqk[:, 128:192], in0=lqk[:, 128:192],
         in1=lqk[:, 192:256], op=ALU.mult)
    P.op("dve", "tensor_reduce", R=[lqk.s()], W=[sc.s()], out=sc[:, 0:1], in_=lqk[:, 0:64], axis=AX.X,
         op=ALU.add)
    P.op("dve", "tensor_reduce", R=[lqk.s(), sc.s()], W=[sc.s()], out=sc[:, 1:2], in_=lqk[:, 128:192],
         axis=AX.X, op=ALU.add)
    P.op("act", "activation", R=[sc.s()], W=[sc.s()], out=sc[:, 2:4], in_=sc[:, 0:2], func=AF.Exp)
    P.op("dve", "tensor_tensor", R=[sc.s()], W=[sc.s()], out=sc[:, 4:5], in0=sc[:, 2:3], in1=sc[:, 3:4],
         op=ALU.subtract)
    P.op("dve", "tensor_scalar", R=[sc.s()], W=[sc.s()], out=sc[:, 5:6], in0=sc[:, 4:5], scalar1=0.2,
         scalar2=-1.0, op0=ALU.add, op1=ALU.mult)
    P.op("act", "activation", R=[alog.s()], W=[negA.s()], out=negA[:], in_=alog[:], func=AF.Exp)
    P.op("dve", "tensor_scalar", R=[negA.s()], W=[negA.s()], out=negA[:], in0=negA[:], scalar1=-1.0,
         scalar2=None, op0=ALU.mult)
    nlam = sc[:, 5:6]

    def cut_here(tag):
        if stop_after == tag:
            d = nc.dram_tensor("tap_sc", [128, 16], F32, kind="ExternalOutput").ap()
            P.dma("sp", R=[sc.s()], out=d, in_=sc[:])
            P.barrier()
            P.emit()
            return True
        return False
    if cut_here("A0"):
        return nc, P

    uT = P.sbuf("uT", [128, 8, S], BF16)
    gcs = P.sbuf("gcs", [128, NT, 8], F32)
    lbs = P.sbuf("lbs", [128, NT, 8], F32)
    glt = P.sbuf("glt", [128, NT, 8], F32)
    egs = P.sbuf("egs", [128, NT, 2, 8], F32)

    mA = P.mark()
    xall = P.sbuf("xall", [128, NT, D], F32)
    wrow1 = P.sbuf("wrow1", [128, D], F32)
    P.dma("sp", W=[wrow1.s()], out=wrow1[:], in_=norm1_d.partition_broadcast(128))
    junk = P.sbuf("junk", [128, D], BF16)
    ssA = P.sbuf("ssA", [128, NT], F32)
    rstd1 = P.sbuf("rstd1", [128, NT], F32)
    xn32 = [P.sbuf(f"xn32_{i}", [128, D], F32) for i in range(2)]
    uT32 = [P.sbuf(f"uT32_{i}", [128, D], F32) for i in range(2)]
    wab = P.sbuf("wab", [128, 8, 16], F32)
    gab = P.sbuf("gab", [128, NT, 16], F32)
    arg = P.sbuf("arg", [128, NT, 16], F32)
    g32 = P.sbuf("g32", [128, NT, 8], F32)

    P.dma("sp", W=[wab.s()], out=wab[:],
          in_=w_in_d[:, COL["ga"]:COL["ga"] + 16].rearrange("(k p) c -> p k c", p=128))
    for tt in range(NT):
        P.dma("sp", W=[xall.s(tt)], out=xall[:, tt, :], in_=x_d[tt * 128:(tt + 1) * 128, :])
    for tt in range(NT):
        P.op("act", "activation", R=[xall.s(tt)], W=[junk.s(), ssA.s(tt)], out=junk[:], in_=xall[:, tt, :],
             func=AF.Square, accum_out=ssA[:, tt:tt + 1])
    if cut_here("A1"):
        return nc, P
    for tt in range(NT):
        P.op("dve", "tensor_scalar", R=[ssA.s(tt)], W=[rstd1.s(tt)], out=rstd1[:, tt:tt + 1], in0=ssA[:, tt:tt + 1],
             scalar1=1.0 / D, scalar2=EPS, op0=ALU.mult, op1=ALU.add)
        P.op("pool", "tensor_tensor", R=[rstd1.s(tt), cm05.s()], W=[rstd1.s(tt)], out=rstd1[:, tt:tt + 1],
             in0=rstd1[:, tt:tt + 1], in1=cm05[:, 0:1], op=ALU.pow)
    psG = P.psum_pin()
    for tt in range(NT):
        xn = xn32[tt % 2]
        u32 = uT32[tt % 2]
        P.op("dve", "scalar_tensor_tensor", R=[xall.s(tt), rstd1.s(tt), wrow1.s()], W=[xn.s()], out=xn[:],
             in0=xall[:, tt, :], scalar=rstd1[:, tt:tt + 1], in1=wrow1[:], op0=ALU.mult, op1=ALU.mult)
        for half in range(2):
            ps = P.psum()
            P.ops("pe", [("transpose", dict(out=ps[:, j * 128:(j + 1) * 128],
                                            in_=xn[:, (half * 4 + j) * 128:(half * 4 + j + 1) * 128],
                                            identity=ident_f)) for j in range(4)],
                  R=[xn.s(), CM], W=[ps.s()])
            if half == 0:
                P.op("act", "activation", R=[ps.s()], W=[u32.s(half)], out=u32[:, 0:512], in_=ps[:, 0:512],
                     func=AF.Copy)
            else:
                P.op("dve", "tensor_copy", R=[ps.s()], W=[u32.s(half)], out=u32[:, 512:1024], in_=ps[:, 0:512])
        P.op("pool", "tensor_copy", R=[u32.s(0), u32.s(1)], W=[uT.s(tt)], out=uT[:, :, tt * 128:(tt + 1) * 128],
             in_=u32[:].rearrange("p (k t) -> p k t", k=8))
        P.ops("pe", [("matmul", dict(out=psG[:, tt * 16:(tt + 1) * 16], lhsT=u32[:, kc * 128:(kc + 1) * 128],
                                     rhs=wab[:, kc, :], start=(kc == 0), stop=(kc == 7))) for kc in range(8)],
              R=[u32.s(0), u32.s(1), wab.s()], W=[psG.s()])
    if cut_here("A2"):
        return nc, P
    P.op("act", "activation", R=[psG.s()], W=[gab.s()], out=gab[:].rearrange("p a b -> p (a b)"),
         in_=psG[:, 0:NT * 16], func=AF.Copy)
    P.psum_unpin(psG)
    for tt in range(NT):
        P.op("dve", "tensor_tensor", R=[gab.s(), dtb.s()], W=[arg.s()], out=arg[:, tt, 0:8], in0=gab[:, tt, 0:8],
             in1=dtb[:], op=ALU.add)
    P.op("dve", "tensor_scalar", R=[gab.s()], W=[arg.s()], out=arg[:, :, 8:16], in0=gab[:, :, 8:16],
         scalar1=-1.0, scalar2=None, op0=ALU.mult)
    argf = arg[:].rearrange("p a b -> p (a b)")
    P.op("act", "activation", R=[arg.s()], W=[arg.s()], out=argf, in_=argf, func=AF.Exp)
    P.op("act", "activation", R=[arg.s()], W=[arg.s()], out=argf, in_=argf, func=AF.Ln, bias=1.0, scale=1.0)
    for tt in range(NT):
        P.op("dve", "tensor_tensor", R=[arg.s(), negA.s()], W=[g32.s()], out=g32[:, tt, :], in0=arg[:, tt, 0:8],
             in1=negA[:], op=ALU.mult)
    P.op("dve", "tensor_scalar", R=[arg.s()], W=[lbs.s()], out=lbs[:], in0=arg[:, :, 8:16], scalar1=-1.0,
         scalar2=None, op0=ALU.mult)
    if cut_here("A3"):
        return nc, P
    ps = P.psum()
    P.ops("pe", [("matmul", dict(out=ps[:, tt * 8:(tt + 1) * 8], lhsT=ltri_f, rhs=g32[:, tt, :], start=True,
                                 stop=True)) for tt in range(NT)], R=[g32.s(), CM], W=[ps.s()])
    P.op("act", "activation", R=[ps.s()], W=[gcs.s()], out=gcs[:].rearrange("p a b -> p (a b)"),
         in_=ps[:, 0:NT * 8], func=AF.Copy)
    ps = P.psum()
    P.ops("pe", [("matmul", dict(out=ps[:, (tt * 2 + c) * 8:(tt * 2 + c + 1) * 8], lhsT=sel_f[c],
                                 rhs=gcs[:, tt, :], start=True, stop=True))
                 for tt in range(NT) for c in range(2)], R=[gcs.s(), CM], W=[ps.s()])
    P.op("act", "activation", R=[ps.s()], W=[egs.s()], out=egs[:].rearrange("p a c b -> p (a c b)"),
         in_=ps[:, 0:NT * 16], func=AF.Exp)
    if cut_here("A4"):
        return nc, P
    psv = ps[:, 0:NT * 16].rearrange("p (a c b) -> p a c b", a=NT, c=2)
    P.op("dve", "tensor_copy", R=[ps.s()], W=[glt.s()], out=glt[:, :, :], in_=psv[:, :, 1, :])
    if cut_here("A5"):
        return nc, P
    tap("gcs", gcs, gcs[:], [128, NT, 8], [gcs.s()])
    tap("lbs", lbs, lbs[:], [128, NT, 8], [lbs.s()])
    tap("egs", egs, egs[:], [128, NT, 2, 8], [egs.s()])
    tap("glt", glt, glt[:], [128, NT, 8], [glt.s()])
    P.barrier()
    P.release(mA)

    if stop_after == "A":
        u_dbg = nc.dram_tensor("tap_uT", [128, 8, S], BF16, kind="ExternalOutput").ap()
        P.dma("sp", R=[uT.s(tt) for tt in range(NT)], out=u_dbg, in_=uT[:])
        P.barrier()
        P.emit()
        return nc, P


    MIXT_OFF = (P.sb_ptr + 63) // 64 * 64
    mixT = P.sbuf("mixT", [128, H, S], BF16)
    mB = P.mark()
    wh = [P.sbuf("wh0", [128, 8, 9 * 128], BF16)] * 2
    xc = P.sbuf("xc", [128, 2, S + 4], BF16)
    dg = P.sbuf("dg", [128, 12, 128], BF16)
    kqv = P.sbuf("kqv", [128, 3, S], BF16)
    sgt = [P.sbuf("sgt0", [128, 512], BF16)]
    GAb = [P.sbuf(f"GA{i}", [128, NT, 128], BF16) for i in range(2)]
    GBt = P.sbuf("GBt", [128, NT, 128], BF16)
    vd = P.sbuf("vd", [128, NT, 130], BF16)
    dqT = P.sbuf("dqT", [128, S], BF16)
    dkT = P.sbuf("dkT", [128, S], BF16)
    tz = [P.sbuf(f"tz{i}", [128, 384], BF16) for i in range(2)]
    sq4 = [P.sbuf(f"sq4{i}", [128, 256], BF16) for i in range(2)]
    rs4 = [P.sbuf(f"rs4{i}", [128, 4], F32) for i in range(2)]
    qkn = [P.sbuf(f"qkn{i}", [128, 256], BF16) for i in range(2)]
    t1 = [P.sbuf(f"t1{i}", [128, 128], BF16) for i in range(2)]
    t2 = [P.sbuf(f"t2{i}", [128, 128], BF16) for i in range(2)]
    ssq = P.sbuf("ssq", [128, NT, 2], F32)
    tsc = P.sbuf("tsc", [128, 10, NT], F32)
    kbg = P.sbuf("kbg", [128, NT, 128], BF16)
    kdec = P.sbuf("kdec", [128, NT, 128], BF16)
    vb = P.sbuf("vb", [128, NT, 128], BF16)
    dgf = [P.sbuf(f"dgf{i}", [128, 256], F32) for i in range(2)] * 2
    eaq = [P.sbuf(f"eaq{i}", [128, 256], F32) for i in range(2)] * 2
    wlv = [[P.sbuf(f"wlv{i}_{k}", [128, 384], BF16) for k in range(2)] for i in range(PREP_B)]
    aqt = P.sbuf("aqt", [128, NT, 128], BF16)
    u32 = P.sbuf("u32", [128, NT, 128], F32)
    wT = P.sbuf("wT", [128, S], BF16)
    S32 = P.sbuf("S32", [128, 128], F32)
    Sb = P.sbuf("Sb", [128, 128], BF16)
    vn = P.sbuf("vn", [128, 2, 128], BF16)
    o1s = P.sbuf("o1s", [128, 2, 128], F32)
    oa = P.sbuf("oa", [128, NT, 128], BF16)
    sso = P.sbuf("sso", [128, NT], F32)
    junk2 = P.sbuf("junk2", [128, 128], BF16)
    rso = P.sbuf("rso", [128, NT], F32)
    pt = [P.sbuf(f"pt{i}", [128, 512], BF16) for i in range(4)]
    ob = [P.sbuf(f"ob{i}", [128, 128], F32) for i in range(4)]
    tb = [P.sbuf(f"tb{i}", [128, 128], F32) for i in range(4)]
    rsum = [P.sbuf(f"rsum{i}", [128, 4], F32) for i in range(4)]
    ob16 = P.sbuf("ob16", [128, NT, 128], BF16)
    rs16 = P.sbuf("rs16", [128, NT], F32)
    junk4 = P.sbuf("junk4", [128, 128], BF16)
    rr16 = P.sbuf("rr16", [128, NT], F32)
    TS_LRQ, TS_LRK, TS_CA, TS_CQ, TS_BJ, TS_SK1, TS_SK2, TS_BETA, TS_F, TS_TMP = range(10)

    P.op("pool", "memset", W=[xc.s(g, "pad") for g in range(2)], ap=xc[:, :, 0:3], constant=0.0)
    P.op("pool", "memset", W=[vd.s(tt) for tt in range(NT)], ap=vd[:, :, 128:130], constant=1.0)

    def load_head_weights(h):
        w = wh[h % 2]
        for g, nm in enumerate(HEAD_GROUPS):
            c0 = COL[nm] + h * 128
            P.dma("pool", W=[w.s(g)], out=w[:, :, g * 128:(g + 1) * 128],
                  in_=w_in_d[:, c0:c0 + 128].rearrange("(k p) c -> p k c", p=128))

    evac_rr = [0]

    def evac_copy(out, in_, R, W):
        evac_rr[0] ^= 1
        if evac_rr[0]:
            P.op("act", "activation", R=R, W=W, out=out, in_=in_, func=AF.Copy)
        else:
            P.op("dve", "tensor_copy", R=R, W=W, out=out, in_=in_)

    def inproj_T(h):
        w = wh[h % 2]
        GA = GAb[h % 2]
        pst = {}

        def s12(tt):
            i2 = tt % 2
            psA = P.psum()
            psB = P.psum()
            pst[tt] = psA
            lhs = lambda kc: uT[:, kc, tt * 128:(tt + 1) * 128]
            P.ops("pe", [("matmul", dict(out=psA[:, 0:512], lhsT=lhs(kc), rhs=w[:, kc, 3 * 128:7 * 128],
                                         start=(kc == 0), stop=(kc == 7))) for kc in range(8)],
                  R=[uT.s(tt)] + [w.s(g) for g in (3, 4, 5, 6)], W=[psA.s()])
            P.ops("pe", [("matmul", dict(out=psB[:, 0:256], lhsT=lhs(kc), rhs=w[:, kc, 7 * 128:9 * 128],
                                         start=(kc == 0), stop=(kc == 7))) for kc in range(8)],
                  R=[uT.s(tt)] + [w.s(g) for g in (7, 8)], W=[psB.s()])
            z = tz[i2]
            P.op("act", "activation", R=[psA.s()], W=[z.s()], out=z[:, 0:128], in_=psA[:, 0:128], func=AF.Sigmoid)
            P.op("act", "activation", R=[psB.s(), z.s()], W=[z.s()], out=z[:, 128:384], in_=psB[:, 0:256],
                 func=AF.Sigmoid)
            P.op("act", "activation", R=[psA.s()], W=[sq4[i2].s()], out=sq4[i2][:], in_=psA[:, 128:384],
                 func=AF.Square)
            P.op("act", "activation", R=[psA.s()], W=[vd.s(tt)], out=vd[:, tt, 0:128], in_=psA[:, 384:512],
                 func=AF.Copy)
            P.op("dve", "tensor_tensor", R=[psA.s(), z.s()], W=[t1[i2].s()], out=t1[i2][:], in0=psA[:, 0:128],
                 in1=z[:, 0:128], op=ALU.mult)
            P.op("pool", "tensor_tensor", R=[z.s(), gnw.s()], W=[t2[i2].s()], out=t2[i2][:], in0=z[:, 128:256],
                 in1=gnw[:], op=ALU.mult)
            P.op("pool", "tensor_tensor", R=[t1[i2].s(), t2[i2].s()], W=[GA.s(tt)], out=GA[:, tt, :],
                 in0=t1[i2][:], in1=t2[i2][:], op=ALU.mult)
            P.op("pool", "tensor_tensor", R=[z.s(), slw8.s()], W=[GBt.s(tt)], out=GBt[:, tt, :],
                 in0=z[:, 256:384], in1=slw8[:], op=ALU.mult)
            r4 = rs4[i2]
            P.op("dve", "tensor_reduce", R=[sq4[i2].s()], W=[r4.s()], out=r4[:],
                 in_=sq4[i2][:].rearrange("p (a b) -> p a b", a=4), axis=AX.X, op=ALU.add)
            P.op("dve", "tensor_scalar", R=[r4.s()], W=[r4.s()], out=r4[:], in0=r4[:], scalar1=1.0 / 64,
                 scalar2=EPS, op0=ALU.mult, op1=ALU.add)
            P.op("pool", "tensor_tensor", R=[r4.s(), cm05.s()], W=[r4.s()], out=r4[:], in0=r4[:],
                 in1=cm05[:, 0:4], op=ALU.pow)

        def s3(tt):
            i2 = tt % 2
            psA = pst.pop(tt)
            r4 = rs4[i2]
            qn = qkn[i2]
            for sgi in range(4):
                wsl = qkw[:, 0:64] if sgi < 2 else qkw[:, 64:128]
                P.op("dve", "scalar_tensor_tensor", R=[psA.s(), r4.s(), qkw.s()], W=[qn.s()],
                     out=qn[:, sgi * 64:(sgi + 1) * 64], in0=psA[:, 128 + sgi * 64:128 + (sgi + 1) * 64],
                     scalar=r4[:, sgi:sgi + 1], in1=wsl, op0=ALU.mult, op1=ALU.mult)
            ps = P.psum()
            psb = ps.ap.bitcast(BF16)
            P.ops("pe", [("transpose", dict(out=psb[:, j * 128:(j + 1) * 128], in_=qn[:, j * 128:(j + 1) * 128],
                                            identity=identb[:])) for j in range(2)],
                  R=[qn.s(), identb.s()], W=[ps.s()])
            P.op("act", "activation", R=[ps.s()], W=[dqT.s(tt)], out=dqT[:, tt * 128:(tt + 1) * 128],
                 in_=psb[:, 0:128], func=AF.Copy)
            P.op("dve", "tensor_copy", R=[ps.s()], W=[dkT.s(tt)], out=dkT[:, tt * 128:(tt + 1) * 128],
                 in_=psb[:, 128:256])

        for tt in range(NT):
            s12(tt)
            s3(tt)
            yield 2.0

    def inproj_F(h):
        w = wh[h % 2]
        for g in range(3):
            for j in range(4):
                col = (g * 8 + h) * 4 + j
                P.op("act", "activation", R=[identb.s(), convw.s()], W=[dg.s(g)], out=dg[:, g * 4 + j, :],
                     in_=identb[:], func=AF.Copy, scale=convw[:, col:col + 1])
        for g in range(3):
            xs = g % 2
            for tc in range(4):
                ps = P.psum()
                P.ops("pe", [("matmul", dict(out=ps[:, 0:512], lhsT=w[:, kc, g * 128:(g + 1) * 128],
                                             rhs=uT[:, kc, tc * 512:(tc + 1) * 512], start=(kc == 0),
                                             stop=(kc == 7))) for kc in range(8)],
                      R=[w.s(g)] + [uT.s(tt) for tt in range(4 * tc, 4 * tc + 4)], W=[ps.s()])
                evac_copy(xc[:, xs, 3 + tc * 512:3 + (tc + 1) * 512], ps[:, 0:512], [ps.s()], [xc.s(xs, tc)])
            for tc in range(4):
                ps = P.psum()
                R = [dg.s(g), xc.s(xs, tc)] + ([xc.s(xs, tc - 1)] if tc > 0 else [xc.s(xs, "pad")])
                P.ops("pe", [("matmul", dict(out=ps[:, 0:512], lhsT=dg[:, g * 4 + j, :],
                                             rhs=xc[:, xs, tc * 512 + j:tc * 512 + j + 512], start=(j == 0),
                                             stop=(j == 3))) for j in range(4)], R=R, W=[ps.s()])
                sg = sgt[0]
                P.op("act", "activation", R=[ps.s()], W=[sg.s()], out=sg[:], in_=ps[:, 0:512], func=AF.Sigmoid)
                P.op("dve", "tensor_tensor", R=[ps.s(), sg.s()], W=[kqv.s(g, tc)],
                     out=kqv[:, g, tc * 512:(tc + 1) * 512], in0=ps[:, 0:512], in1=sg[:], op=ALU.mult)

    def tsl(i):
        return tsc[:, i, :]

    def gdn_scalars(h):
        for g in range(2):
            P.op("act", "activation", R=[kqv.s(g, tc) for tc in range(4)], W=[xc.s(g, tc) for tc in range(4)],
                 out=xc[:, g, 3:3 + S], in_=kqv[:, g, :], func=AF.Square)
        ps = P.psum()
        P.ops("pe", [("matmul", dict(out=ps[:, tt * 2 + g:tt * 2 + g + 1],
                                     lhsT=xc[:, g, 3 + tt * 128:3 + (tt + 1) * 128],
                                     rhs=onesb[:, 0:1], start=True, stop=True))
                     for tt in range(NT) for g in range(2)],
              R=[xc.s(g, tc) for g in range(2) for tc in range(4)] + [onesb.s()], W=[ps.s()])
        P.op("act", "activation", R=[ps.s()], W=[ssq.s()], out=ssq[:].rearrange("p a b -> p (a b)"),
             in_=ps[:, 0:2 * NT], func=AF.Ln, bias=epsc[:, 0:1], scale=1.0)
        T = tsc.s()
        gch = gcs[:, :, h]
        lbh = lbs[:, :, h]
        glh = glt[:, :, h]
        P.op("dve", "tensor_scalar", R=[ssq.s()], W=[T], out=tsl(TS_LRQ), in0=ssq[:, :, 0], scalar1=-0.5,
             scalar2=-0.5 * float(np.log(128.0)), op0=ALU.mult, op1=ALU.add)
        P.op("dve", "tensor_scalar", R=[ssq.s(), T], W=[T], out=tsl(TS_LRK), in0=ssq[:, :, 1], scalar1=-0.5,
             scalar2=None, op0=ALU.mult)
        P.op("dve", "tensor_tensor", R=[gcs.s(), T], W=[T], out=tsl(TS_CQ), in0=gch, in1=tsl(TS_LRQ), op=ALU.add)
        P.op("dve", "tensor_tensor", R=[gcs.s(), T], W=[T], out=tsl(TS_TMP), in0=gch, in1=tsl(TS_LRK), op=ALU.add)
        P.op("dve", "tensor_tensor", R=[lbs.s(), T], W=[T], out=tsl(TS_CA), in0=tsl(TS_TMP), in1=lbh, op=ALU.add)
        P.op("dve", "tensor_tensor", R=[gcs.s(), T], W=[T], out=tsl(TS_BJ), in0=tsl(TS_LRK), in1=gch,
             op=ALU.subtract)
        P.op("act", "activation", R=[T], W=[T], out=tsl(TS_SK1), in_=tsl(TS_CA), func=AF.Exp)
        P.op("dve", "tensor_tensor", R=[glt.s(), T], W=[T], out=tsl(TS_TMP), in0=tsl(TS_BJ), in1=glh, op=ALU.add)
        P.op("act", "activation", R=[T], W=[T], out=tsl(TS_SK2), in_=tsl(TS_TMP), func=AF.Exp)
        P.op("act", "activation", R=[lbs.s(), T], W=[T], out=tsl(TS_BETA), in_=lbh, func=AF.Exp)
        P.op("act", "activation", R=[T], W=[T], out=tsl(TS_F), in_=tsl(TS_CQ), func=AF.Exp)

    def gdn_prep(h):
        T = tsc.s()
        for t0 in range(0, NT, PREP_B):
            tiles = list(range(t0, t0 + PREP_B))
            for i, tt in enumerate(tiles):
                tc = tt // 4
                tsl_ = slice(tt * 128, (tt + 1) * 128)
                ps = P.psum()
                psb = ps.ap.bitcast(BF16)
                P.ops("pe", [("transpose", dict(out=psb[:, 0:128], in_=kqv[:, 1, tsl_], identity=identb[:])),
                             ("transpose", dict(out=psb[:, 128:256], in_=kqv[:, 2, tsl_], identity=identb[:]))],
                      R=[kqv.s(1, tc), kqv.s(2, tc), identb.s()], W=[ps.s()])
                P.op("act", "activation", R=[ps.s(), T], W=[kbg.s(tt)], out=kbg[:, tt, :], in_=psb[:, 0:128],
                     func=AF.Copy, scale=tsc[:, TS_SK1, tt:tt + 1])
                P.op("dve", "tensor_scalar", R=[ps.s(), T], W=[kdec.s(tt)], out=kdec[:, tt, :], in0=psb[:, 0:128],
                     scalar1=tsc[:, TS_SK2, tt:tt + 1], scalar2=None, op0=ALU.mult)
                P.op("act", "activation", R=[ps.s(), T], W=[vb.s(tt)], out=vb[:, tt, :], in_=psb[:, 128:256],
                     func=AF.Copy, scale=tsc[:, TS_BETA, tt:tt + 1])
                d = dgf[i % 2]
                P.op("act", "activation", R=[CM, T], W=[d.s()], out=d[:, 0:128], in_=ident_f, func=AF.Copy,
                     scale=tsc[:, TS_CA, tt:tt + 1])
                P.op("act", "activation", R=[CM, T, d.s()], W=[d.s()], out=d[:, 128:256], in_=ident_f, func=AF.Copy,
                     scale=tsc[:, TS_CQ, tt:tt + 1])
                psE = P.psum()
                P.ops("pe", [("matmul", dict(out=psE[:, 0:128], lhsT=ones_f, rhs=d[:, 0:128], start=True, stop=True)),
                             ("matmul", dict(out=psE[:, 128:256], lhsT=ones_f, rhs=d[:, 128:256], start=True,
                                             stop=True))], R=[CM, d.s()], W=[psE.s()])
                e = eaq[i % 2]
                P.op("act", "activation", R=[psE.s(), T], W=[e.s()], out=e[:], in_=psE[:, 0:256], func=AF.Exp,
                     bias=tsc[:, TS_BJ, tt:tt + 1], scale=1.0)
                P.op("pool", "affine_select", R=[e.s()], W=[e.s()], out=e[:, 0:128], in_=e[:, 0:128],
                     pattern=[[1, 128]], compare_op=ALU.is_gt, fill=0.0, base=0, channel_multiplier=-1)
                P.op("pool", "affine_select", R=[e.s()], W=[e.s()], out=e[:, 128:256], in_=e[:, 128:256],
                     pattern=[[1, 128]], compare_op=ALU.is_ge, fill=0.0, base=0, channel_multiplier=-1)
                psK = P.psum()
                P.ops("pe", [("matmul", dict(out=psK[:, 0:128], lhsT=kqv[:, 1, tsl_], rhs=kqv[:, 1, tsl_],
                                             start=True, stop=True)),
                             ("matmul", dict(out=psK[:, 128:256], lhsT=kqv[:, 1, tsl_], rhs=kqv[:, 0, tsl_],
                                             start=True, stop=True))],
                      R=[kqv.s(0, tc), kqv.s(1, tc)], W=[psK.s()])
                w0 = wlv[i][0]
                P.op("dve", "scalar_tensor_tensor", R=[psK.s(), e.s()], W=[w0.s()], out=w0[:, 0:128],
                     in0=psK[:, 0:128], scalar=-1.0, in1=e[:, 0:128], op0=ALU.mult, op1=ALU.mult)
                P.op("dve", "tensor_tensor", R=[psK.s(), e.s()], W=[aqt.s(tt)], out=aqt[:, tt, :],
                     in0=psK[:, 128:256], in1=e[:, 128:256], op=ALU.mult)
                P.op("pool", "tensor_copy", R=[identb.s(), w0.s()], W=[w0.s()], out=w0[:, 128:256], in_=identb[:])
                psN = P.psum()
                psNb = psN.ap.bitcast(BF16)
                P.op("pe", "transpose", R=[w0.s(), identb.s()], W=[psN.s()], out=psNb[:, 0:128], in_=w0[:, 0:128],
                     identity=identb[:])
                P.op("act", "activation", R=[psN.s(), w0.s()], W=[w0.s()], out=w0[:, 256:384], in_=psNb[:, 0:128],
                     func=AF.Copy)
                yield 4.0
            for lvl in range(7):
                for i, tt in enumerate(tiles):
                    wc = wlv[i][lvl % 2]
                    wn = wlv[i][(lvl + 1) % 2]
                    ps = P.psum()
                    Mk, Rk, Nk = wc[:, 0:128], wc[:, 128:256], wc[:, 256:384]
                    calls = []
                    if lvl <= 4:
                        calls.append(("matmul", dict(out=ps[:, 0:256], lhsT=Nk, rhs=wc[:, 0:256], start=True, stop=True)))
                    else:
                        calls.append(("matmul", dict(out=ps[:, 128:256], lhsT=Nk, rhs=Rk, start=True, stop=True)))
                    if lvl <= 5:
                        calls.append(("matmul", dict(out=ps[:, 256:384], lhsT=Mk, rhs=Nk, start=True, stop=True)))
                    P.ops("pe", calls, R=[wc.s()], W=[ps.s()])
                    P.op("dve", "tensor_tensor", R=[ps.s(), wc.s()], W=[wn.s()], out=wn[:, 128:256],
                         in0=ps[:, 128:256], in1=Rk, op=ALU.add)
                    if lvl <= 4:
                        o3 = wn[:, 0:384].rearrange("p (a b) -> p a b", a=3)[:, 0:3:2, :]
                        i3 = ps[:, 0:384].rearrange("p (a b) -> p a b", a=3)[:, 0:3:2, :]
                    elif lvl == 5:
                        o3, i3 = wn[:, 256:384], ps[:, 256:384]
                    if lvl <= 5:
                        if i % 2 == 0:
                            P.op("act", "activation", R=[ps.s(), wn.s()], W=[wn.s()], out=o3, in_=i3, func=AF.Copy)
                        else:
                            P.op("dve", "tensor_copy", R=[ps.s(), wn.s()], W=[wn.s()], out=o3, in_=i3)
                yield 5.0
            for i, tt in enumerate(tiles):
                wf = wlv[i][1]
                ps = P.psum()
                P.ops("pe", [("matmul", dict(out=ps[:, 0:128], lhsT=wf[:, 128:256], rhs=vb[:, tt, :], start=True,
                                             stop=True)),
                             ("matmul", dict(out=ps[:, 128:256], lhsT=kbg[:, tt, :], rhs=wf[:, 128:256], start=True,
                                             stop=True))], R=[wf.s(), vb.s(tt), kbg.s(tt)], W=[ps.s()])
                P.op("act", "activation", R=[ps.s()], W=[u32.s(tt)], out=u32[:, tt, :], in_=ps[:, 0:128], func=AF.Copy)
                P.op("dve", "tensor_copy", R=[ps.s()], W=[wT.s(tt)], out=wT[:, tt * 128:(tt + 1) * 128],
                     in_=ps[:, 128:256])
            yield 2.0

    def gdn_recurrence(h):
        T = tsc.s()
        GA = GAb[h % 2]
        P.op("pool", "memset", W=[S32.s()], ap=S32[:], constant=0.0)
        P.op("pool", "memset", W=[Sb.s()], ap=Sb[:], constant=0.0)
        for tt in range(NT):
            tc = tt // 4
            for c in (1,):
                rows = slice(0, 128)
                cols = slice(tt * 128, (tt + 1) * 128)
                ps = P.psum()
                P.ops("pe", [("matmul", dict(out=ps[rows, 0:128], lhsT=wT[:, cols], rhs=Sb[:], start=True, stop=True)),
                             ("matmul", dict(out=ps[rows, 128:256], lhsT=kqv[:, 0, cols], rhs=Sb[:], start=True,
                                             stop=True))], R=[wT.s(tt), kqv.s(0, tc), Sb.s()], W=[ps.s()])
                P.op("dve", "tensor_tensor", R=[ps.s(), u32.s(tt)], W=[vn.s(tt % 2, c)], out=vn[rows, tt % 2, :],
                     in0=u32[rows, tt, :], in1=ps[rows, 0:128], op=ALU.subtract)
                P.op("act", "activation", R=[ps.s(), T], W=[o1s.s(tt % 2, c)], out=o1s[rows, tt % 2, :], in_=ps[rows, 128:256],
                     func=AF.Copy, scale=tsc[rows, TS_F, tt:tt + 1])
                psS = P.psum()
                P.op("pe", "matmul", R=[kdec.s(tt), vn.s(tt % 2, c)], W=[psS.s()], out=psS[:, 0:128],
                     lhsT=kdec[rows, tt, :], rhs=vn[rows, tt % 2, :], start=True, stop=True)
                eg = egs[:, tt, c, h:h + 1]
                P.op("dve", "scalar_tensor_tensor", R=[psS.s(), S32.s(), egs.s()], W=[Sb.s()], out=Sb[:],
                     in0=S32[:], scalar=eg, in1=psS[:, 0:128], op0=ALU.mult, op1=ALU.add)
                P.op("dve", "scalar_tensor_tensor", R=[psS.s(), S32.s(), egs.s()], W=[S32.s()], out=S32[:],
                     in0=S32[:], scalar=eg, in1=psS[:, 0:128], op0=ALU.mult, op1=ALU.add)
                yield 3.0
            ps = P.psum()
            P.op("pe", "matmul", R=[aqt.s(tt), vn.s(tt % 2, 1)], W=[ps.s()], out=ps[:, 0:128],
                 lhsT=aqt[:, tt, :], rhs=vn[:, tt % 2, :], start=True, stop=True)
            P.op("dve", "tensor_tensor", R=[ps.s(), o1s.s(tt % 2, 1)], W=[oa.s(tt)], out=oa[:, tt, :],
                 in0=ps[:, 0:128], in1=o1s[:, tt % 2, :], op=ALU.add)
            P.op("act", "activation", R=[oa.s(tt)], W=[junk2.s(), sso.s(tt)], out=junk2[:], in_=oa[:, tt, :],
                 func=AF.Square, accum_out=sso[:, tt:tt + 1])
        allso = [sso.s(tt) for tt in range(NT)]
        P.op("dve", "tensor_scalar", R=allso, W=[rso.s()], out=rso[:], in0=sso[:], scalar1=1.0 / 128,
             scalar2=EPS, op0=ALU.mult, op1=ALU.add)
        P.op("pool", "tensor_tensor", R=[rso.s(), cm05.s()], W=[rso.s()], out=rso[:], in0=rso[:],
             in1=cm05[:, 0:NT], op=ALU.pow)
        for tt in range(NT):
            P.op("dve", "scalar_tensor_tensor", R=[oa.s(tt), rso.s(), GA.s(tt)], W=[oa.s(tt)], out=oa[:, tt, :],
                 in0=oa[:, tt, :], scalar=rso[:, tt:tt + 1], in1=GA[:, tt, :], op0=ALU.mult, op1=ALU.mult)

    def attention(h):
        ptk = [0]
        for qc in range(4):
            accA = P.psum_pin()
            accB = P.psum_pin()
            accC = P.psum_pin()
            accs = (accA, accB, accC)

            def acc_ap(c, ql):
                if ql < 3:
                    return (accA, accB)[c], ql * 129
                return accC, c * 129
            first = {id(a): True for a in accs}
            nkb = 4 * qc + 4
            steps = [(kb, c) for kb in range(nkb) for c in range(2)]
            pbuf = {}

            def emit_qk(i):
                kb, c = steps[i]
                ql0 = max(0, kb - 4 * qc)
                ncol = (4 - ql0) * 128
                q0 = qc * 512 + ql0 * 128
                ps = P.psum()
                P.op("pe", "matmul", R=[dkT.s(kb)] + [dqT.s(4 * qc + ql) for ql in range(ql0, 4)], W=[ps.s()],
                     out=ps[:, 0:ncol], lhsT=dkT[c * 64:(c + 1) * 64, kb * 128:(kb + 1) * 128],
                     rhs=dqT[c * 64:(c + 1) * 64, q0:q0 + ncol], start=True, stop=True)
                p = pt[ptk[0] % len(pt)]
                ptk[0] += 1
                pbuf[i] = p
                P.op("act", "activation", R=[ps.s()], W=[p.s()], out=p[:, 0:ncol], in_=ps[:, 0:ncol], func=AF.Exp)
                if kb >= 4 * qc:
                    P.op("pool", "affine_select", R=[p.s()], W=[p.s()], out=p[:, 0:128], in_=p[:, 0:128],
                         pattern=[[1, 128]], compare_op=ALU.is_ge, fill=0.0, base=0, channel_multiplier=-1)

            def emit_pv(i):
                kb, c = steps[i]
                ql0 = max(0, kb - 4 * qc)
                p = pbuf.pop(i)
                calls = []
                touched = []
                for ql in range(ql0, 4):
                    a, off = acc_ap(c, ql)
                    st = first[id(a)]
                    first[id(a)] = False
                    calls.append(("matmul", dict(out=a[:, off:off + 129],
                                                 lhsT=p[:, (ql - ql0) * 128:(ql - ql0 + 1) * 128],
                                                 rhs=vd[:, kb, 0:129], start=st, stop=False,
                                                 skip_group_check=True)))
                    if a not in touched:
                        touched.append(a)
                P.ops("pe", calls, R=[p.s(), vd.s(kb)], W=[a.s() for a in touched])

            LOOK = ATT_LOOK
            nst = len(steps) + LOOK
            for i in range(nst):
                if i < len(steps):
                    emit_qk(i)
                if i - LOOK >= 0:
                    emit_pv(i - LOOK)
                yield 1.8
            QL = range(4)
            accp = {ql: (acc_ap(0, ql), acc_ap(1, ql)) for ql in QL}
            for ql in QL:
                (a0, off0), (a1, off1) = accp[ql]
                rs = rsum[ql]
                P.op("dve", "reciprocal", R=[a0.s()], W=[rs.s()], out=rs[:, 0:1], in_=a0[:, off0 + 128:off0 + 129])
                P.op("dve", "reciprocal", R=[a1.s(), rs.s()], W=[rs.s()], out=rs[:, 1:2],
                     in_=a1[:, off1 + 128:off1 + 129])
            for ql in QL:
                (a0, off0), (a1, off1) = accp[ql]
                rs = rsum[ql]
                P.op("act", "activation", R=[a0.s(), rs.s()], W=[ob[ql].s()], out=ob[ql][:], in_=a0[:, off0:off0 + 128],
                     func=AF.Copy, scale=rs[:, 0:1])
                P.op("act", "activation", R=[a1.s(), rs.s()], W=[tb[ql].s()], out=tb[ql][:], in_=a1[:, off1:off1 + 128],
                     func=AF.Copy, scale=rs[:, 1:2])
            for a in accs:
                P.psum_unpin(a)
            yield 1.0
            for ql in QL:
                qb = 4 * qc + ql
                P.op("dve", "scalar_tensor_tensor", R=[ob[ql].s(), tb[ql].s(), sc.s()], W=[ob16.s(qb)],
                     out=ob16[:, qb, :], in0=tb[ql][:], scalar=nlam, in1=ob[ql][:], op0=ALU.mult, op1=ALU.add)
                P.op("act", "activation", R=[ob16.s(qb)], W=[junk4.s(), rs16.s(qb)], out=junk4[:], in_=ob16[:, qb, :],
                     func=AF.Square, accum_out=rs16[:, qb:qb + 1])
            csl = slice(4 * qc, 4 * qc + 4)
            P.op("dve", "tensor_scalar", R=[rs16.s(4 * qc + ql) for ql in QL], W=[rr16.s(qc)], out=rr16[:, csl],
                 in0=rs16[:, csl], scalar1=1.0 / 128, scalar2=EPS, op0=ALU.mult, op1=ALU.add)
            P.op("pool", "tensor_tensor", R=[rr16.s(qc), cm05.s()], W=[rr16.s(qc)], out=rr16[:, csl], in0=rr16[:, csl],
                 in1=cm05[:, 0:4], op=ALU.pow)
            for ql in QL:
                qb = 4 * qc + ql
                P.op("dve", "scalar_tensor_tensor", R=[ob16.s(qb), rr16.s(qc), GBt.s(qb)], W=[ob16.s(qb)],
                     out=ob16[:, qb, :], in0=ob16[:, qb, :], scalar=rr16[:, qb:qb + 1], in1=GBt[:, qb, :],
                     op0=ALU.mult, op1=ALU.mult)
            yield 1.0

    def attn_post(h):
        for q0 in range(0, NT, 4):
            QB = range(q0, q0 + 4)
            for qb in QB:
                m = ob[qb % 4]
                P.op("pool", "tensor_tensor", R=[ob16.s(qb), oa.s(qb)], W=[m.s()], out=m[:], in0=ob16[:, qb, :],
                     in1=oa[:, qb, :], op=ALU.add)
            for qb in QB:
                m = ob[qb % 4]
                ps = P.psum()
                P.op("pe", "transpose", R=[m.s(), CM], W=[ps.s()], out=ps[:, 0:128], in_=m[:], identity=ident_f)
                evac_copy(mixT[:, h, qb * 128:(qb + 1) * 128], ps[:, 0:128], [ps.s()], [mixT.s(h, qb)])

    def interleave(ga, gb):
        ca = cb = 0.0
        da = db = False
        while not (da and db):
            if not da and (db or ca <= cb):
                try:
                    P.tag = "gdn"
                    ca += next(ga)
                except StopIteration:
                    da = True
            elif not db:
                try:
                    P.tag = "attn+inT"
                    cb += next(gb) * BSCALE
                except StopIteration:
                    db = True

    def chain(*gens):
        for g in gens:
            yield from g

    heads = list(range(NHEADS)) if stop_after not in ("B0",) else [0]
    load_head_weights(heads[0])
    P.tag = "inproj"
    for _ in inproj_T(heads[0]):
        pass
    inproj_F(heads[0])
    for hi, h in enumerate(heads):
        nxt = heads[hi + 1] if hi + 1 < len(heads) else None
        if nxt is not None:
            load_head_weights(nxt)
        P.mark_phase(f"h{h}.mix")
        P.tag = "mix"
        gdn_scalars(h)
        sb = [attention(h)] + ([inproj_T(nxt)] if nxt is not None else [])
        interleave(chain(gdn_prep(h), gdn_recurrence(h)), chain(*sb))
        P.tag = "attn_post"
        attn_post(h)
        if nxt is not None:
            P.tag = "inproj"
            inproj_F(nxt)
        if "oa" in taps and h == 0:
            tap("oa", oa, oa[:], [128, NT, 128], [oa.s(tt) for tt in range(NT)])
            tap("u32", u32, u32[:], [128, NT, 128], [u32.s(tt) for tt in range(NT)])
    P.mark_phase("C")
    P.barrier()
    if stop_after in ("B", "B0"):
        d = nc.dram_tensor("tap_mixT", [128, H, S], BF16, kind="ExternalOutput").ap()
        P.dma("sp", out=d, in_=mixT[:])
        P.barrier()
        P.emit()
        return nc, P
    P.release(mB)

    h1 = P.sbuf("h1", [128, NT, D], F32)
    mC = P.mark()
    wout = P.sbuf("wout", [128, 8, D], BF16)
    wrow2 = P.sbuf("wrow2", [128, D], F32)
    xb = [P.sbuf(f"xb{i}", [128, D], F32) for i in range(2)]
    xn2 = [P.sbuf(f"xn2{i}", [128, D], F32) for i in range(2)]
    ssB = P.sbuf("ssB", [128, NT], F32)
    rstd2 = P.sbuf("rstd2", [128, NT], F32)
    junk3 = P.sbuf("junk3", [128, D], BF16)
    P.dma("sp", W=[wrow2.s()], out=wrow2[:], in_=norm2_d.partition_broadcast(128))
    for hh in range(8):
        P.dma("pool", W=[wout.s(hh)], out=wout[:, hh, :], in_=w_out_d[hh * 128:(hh + 1) * 128, :])
    for tt in range(NT):
        x_ = xb[tt % 2]
        P.dma("sp", W=[x_.s()], out=x_[:], in_=x_d[tt * 128:(tt + 1) * 128, :])
        for n in range(2):
            ps = P.psum()
            P.ops("pe", [("matmul", dict(out=ps[:, 0:512], lhsT=mixT[:, hh, tt * 128:(tt + 1) * 128],
                                         rhs=wout[:, hh, n * 512:(n + 1) * 512], start=(hh == 0), stop=(hh == 7)))
                         for hh in range(8)],
                  R=[mixT.s(hh, tt) for hh in range(8)] + [wout.s(hh) for hh in range(8)], W=[ps.s()])
            P.op("dve", "tensor_tensor", R=[ps.s(), x_.s()], W=[h1.s(tt, n)], out=h1[:, tt, n * 512:(n + 1) * 512],
                 in0=ps[:, 0:512], in1=x_[:, n * 512:(n + 1) * 512], op=ALU.add)
        P.op("act", "activation", R=[h1.s(tt, 0), h1.s(tt, 1)], W=[junk3.s(), ssB.s(tt)], out=junk3[:],
             in_=h1[:, tt, :], func=AF.Square, accum_out=ssB[:, tt:tt + 1])
    for tt in range(NT):
        P.op("dve", "tensor_scalar", R=[ssB.s(tt)], W=[rstd2.s(tt)], out=rstd2[:, tt:tt + 1], in0=ssB[:, tt:tt + 1],
             scalar1=1.0 / D, scalar2=EPS, op0=ALU.mult, op1=ALU.add)
        P.op("pool", "tensor_tensor", R=[rstd2.s(tt), cm05.s()], W=[rstd2.s(tt)], out=rstd2[:, tt:tt + 1],
             in0=rstd2[:, tt:tt + 1], in1=cm05[:, 0:1], op=ALU.pow)
    for tt in range(NT):
        xn = xn2[tt % 2]
        P.op("dve", "scalar_tensor_tensor", R=[h1.s(tt, 0), h1.s(tt, 1), rstd2.s(tt), wrow2.s()], W=[xn.s()],
             out=xn[:], in0=h1[:, tt, :], scalar=rstd2[:, tt:tt + 1], in1=wrow2[:], op0=ALU.mult, op1=ALU.mult)
        for half in range(2):
            ps = P.psum()
            P.ops("pe", [("transpose", dict(out=ps[:, j * 128:(j + 1) * 128],
                                            in_=xn[:, (half * 4 + j) * 128:(half * 4 + j + 1) * 128],
                                            identity=ident_f)) for j in range(4)],
                  R=[xn.s(), CM], W=[ps.s()])
            evac_copy(uT[:, half * 4:(half + 1) * 4, tt * 128:(tt + 1) * 128],
                      ps[:, 0:512].rearrange("p (k t) -> p k t", k=4), [ps.s()], [uT.s(tt, half)])
    P.barrier()
    if stop_after == "C":
        for tt in range(NT):
            P.dma("sp", out=out_d[tt * 128:(tt + 1) * 128, :], in_=h1[:, tt, :])
        d = nc.dram_tensor("tap_uT2", [128, 8, S], BF16, kind="ExternalOutput").ap()
        P.dma("sp", out=d, in_=uT[:])
        P.barrier()
        P.emit()
        return nc, P
    P.release(mC)

    P.mark_phase("D")
    r1 = MIXT_OFF
    wgu = []
    for i in range(4):
        a, r1 = P.sbuf_at(f"wg{i}", [128, 8, 128], BF16, r1)
        b, r1 = P.sbuf_at(f"wu{i}", [128, 8, 128], BF16, r1)
        wgu.append((a, b))
    sil = []
    for i in range(2):
        a, r1 = P.sbuf_at(f"sil{i}", [128, 512], BF16, r1)
        sil.append(a)
    ost = []
    for i in range(2):
        a, r1 = P.sbuf_at(f"ost{i}", [128, 256], F32, r1)
        ost.append(a)
    assert r1 <= MIXT_OFF + H * S * 2
    actT = P.sbuf("actT", [128, NFC, 1024], BF16)
    wdb = [P.sbuf(f"wdb{i}", [128, NFC, 256], BF16) for i in range(2)]
    wdk = 0
    for hf in range(2 if DCUT == 0 else 1):
        for fc in range(NFC):
            wg_, wu_ = wgu[fc % WRING]
            P.dma("pool", W=[wg_.s()], out=wg_[:],
                  in_=w_gate_d[:, fc * 128:(fc + 1) * 128].rearrange("(k p) c -> p k c", p=128))
            P.dma("pool", W=[wu_.s()], out=wu_[:],
                  in_=w_up_d[:, fc * 128:(fc + 1) * 128].rearrange("(k p) c -> p k c", p=128))
            for tcl in range(2):
                tok = slice(hf * 1024 + tcl * 512, hf * 1024 + (tcl + 1) * 512)
                tts = [(hf * 1024 + tcl * 512) // 128 + j for j in range(4)]
                Ru = [uT.s(tt, half) for tt in tts for half in range(2)]
                psg = P.psum()
                psu = P.psum()
                P.ops("pe", [("matmul", dict(out=psg[:, 0:512], lhsT=wg_[:, kc, :], rhs=uT[:, kc, tok],
                                             start=(kc == 0), stop=(kc == 7))) for kc in range(8)],
                      R=Ru + [wg_.s()], W=[psg.s()])
                P.ops("pe", [("matmul", dict(out=psu[:, 0:512], lhsT=wu_[:, kc, :], rhs=uT[:, kc, tok],
                                             start=(kc == 0), stop=(kc == 7))) for kc in range(8)],
                      R=Ru + [wu_.s()], W=[psu.s()])
                sl = sil[tcl]
                P.op("act", "activation", R=[psg.s()], W=[sl.s()], out=sl[:], in_=psg[:, 0:512], func=AF.Silu)
                P.op("dve", "tensor_tensor", R=[psu.s(), sl.s()], W=[actT.s(fc, tcl)],
                     out=actT[:, fc, tcl * 512:(tcl + 1) * 512], in0=psu[:, 0:512], in1=sl[:], op=ALU.mult)
        for n4 in range(4 if DCUT != 1 else 0):
            wd_ = wdb[wdk % 2]
            wdk += 1
            for fc in range(NFC):
                P.dma("pool", ndesc=8, W=[wd_.s(fc)], out=wd_[:, fc, :],
                      in_=w_down_d[fc * 128:(fc + 1) * 128, n4 * 256:(n4 + 1) * 256])
            for tl in range(8):
                tt = hf * 8 + tl
                ps = P.psum()
                P.ops("pe", [("matmul", dict(out=ps[:, 0:256], lhsT=actT[:, fc, tl * 128:(tl + 1) * 128],
                                             rhs=wd_[:, fc, :], start=(fc == 0), stop=(fc == NFC - 1)))
                             for fc in range(NFC)],
                      R=[actT.s(fc, tl // 4) for fc in range(NFC)] + [wd_.s(fc) for fc in range(NFC)], W=[ps.s()])
                o_ = ost[(n4 * 8 + tl) % 2]
                P.op("dve", "tensor_tensor", R=[ps.s(), h1.s(tt, n4 // 2)], W=[o_.s()], out=o_[:], in0=ps[:, 0:256],
                     in1=h1[:, tt, n4 * 256:(n4 + 1) * 256], op=ALU.add)
                P.dma("sp", R=[o_.s()], out=out_d[tt * 128:(tt + 1) * 128, n4 * 256:(n4 + 1) * 256], in_=o_[:])
    P.barrier()
    P.emit()
    return nc, P


def _host_consts():
    ident = np.eye(128, dtype=np.float32)
    p = np.arange(128)
    ltri = (p[:, None] <= p[None, :]).astype(np.float32)
    ones = np.ones((128, 128), np.float32)
    sel63 = np.zeros((128, 128), np.float32)
    sel63[63, :] = 1.0
    sel127 = np.zeros((128, 128), np.float32)
    sel127[127, :] = 1.0
    return np.ascontiguousarray(np.concatenate([ident, ltri, ones, sel63, sel127], axis=1))


def make_in_maps(inputs, cores):
    f = lambda k: np.ascontiguousarray(np.asarray(inputs[k], dtype=np.float32))
    conv_w = f("conv_w")[0]
    conv_wl = np.ascontiguousarray(conv_w.T.reshape(24, 128, 4).transpose(1, 0, 2).reshape(128, 96))
    lqk = np.ascontiguousarray(np.concatenate([f("lambda_q1")[0], f("lambda_k1")[0], f("lambda_q2")[0],
                                               f("lambda_k2")[0]])[None, :])
    shared = {
        "w_in": f("w_in")[0], "w_out": f("w_out")[0], "w_gate": f("w_gate")[0], "w_up": f("w_up")[0],
        "w_down": f("w_down")[0], "norm1_w": f("norm1_w"), "norm2_w": f("norm2_w"),
        "gdn_norm_w": f("gdn_norm_w"), "subln_w": f("subln_w"), "q_norm_w": f("q_norm_w"),
        "k_norm_w": f("k_norm_w"), "lqk": lqk, "a_log": f("a_log"), "dt_bias": f("dt_bias"),
        "conv_wl": conv_wl, "cmat": _host_consts(),
    }
    x = f("x")
    return [dict(shared, x=np.ascontiguousarray(x[b])) for b in cores]


def kernel(**inputs):
    nc, _ = build_program()
    in_maps = make_in_maps(inputs, list(range(8)))
    res = run_bass_kernel_spmd(nc, in_maps, core_ids=list(range(8)))
    return np.stack([np.asarray(r["out"], dtype=np.float32) for r in res.results], axis=0)
```

```python
import contextlib
import numpy as np
import concourse.bass as bass
import concourse.mybir as mybir
from concourse.bass_utils import run_bass_kernel_spmd

F32 = mybir.dt.float32
BF16 = mybir.dt.bfloat16
AF = mybir.ActivationFunctionType
ALU = mybir.AluOpType
AX = mybir.AxisListType

SBUF_LO = 16512 + 64
SBUF_HI = 229376

S = 2048
D = 1024
NT = 16
H = 8
DFF = 2816
NFC = DFF // 128
EPS = 1e-6
ATT_LOOK = 3
INP_LAG = 1
NHEADS = 8
DCUT = 0
SAME_ENG_DIST = 3
OVERLAP = 0
SCHED = 1
PRIO = 1
BSCALE = 1
PREP_B = 8
PE_C0 = 0.036
PE_RATE = 2800
WRING = 2
D_IN = 9232
COL = dict(gq=0, gk=1024, gv=2048, gz=3072, ga=4096, gb=4104, dq=4112, dk=5136, dv=6160,
           gate_a=7184, gate_b=8208)
HEAD_GROUPS = ("gq", "gk", "gv", "gz", "dq", "dk", "dv", "gate_a", "gate_b")


class Buf:
    __slots__ = ("name", "lw", "rd", "excl")

    def __init__(self, name, excl=False):
        self.name = name
        self.lw = None
        self.rd = set()
        self.excl = excl


class TT:
    def __init__(self, prog, name, h, excl=False):
        self.prog = prog
        self.name = name
        self.h = h
        self.ap = h.ap()
        self.slots = {}
        self.excl = excl

    def s(self, *key):
        b = self.slots.get(key)
        if b is None:
            b = Buf(f"{self.name}{key}", self.excl)
            self.slots[key] = b
            self.prog.allbufs.append(b)
        return b

    def __getitem__(self, idx):
        return self.ap[idx]


class Prog:
    COMPUTE = ("pe", "act", "dve", "pool")
    NDMASEM = 48

    def __init__(self, nc, same_engine_sync=True):
        self.nc = nc
        self.items = {k: [] for k in ("pe", "act", "dve", "pool", "sp")}
        self.seq = {k: 0 for k in self.COMPUTE}
        self.waited = {k: {} for k in self.items}
        self.allbufs = []
        self.same_engine_sync = same_engine_sync
        self.dma_tot = [0] * self.NDMASEM
        self.dma_rr = 0
        self.sb_ptr = SBUF_LO
        self.sb_peak = SBUF_LO
        self.nalloc = 0
        self.psum_banks = []
        self.ps_rr = 0
        self.ps_pinned = set()
        self.n_wait = 0
        self.n_ops = 0
        self.n_pe = 0
        self.marks = []
        self.pool_out = []
        self.nodes = []
        self.seg_bounds = []
        self.seg_stats = []
        self._busy = {}
        self.tag = ""
        self.crit = []

    def sbuf(self, name, shape, dtype):
        esz = 4 if dtype == F32 else 2
        per_part = int(np.prod(shape[1:])) * esz
        off = (self.sb_ptr + 63) // 64 * 64
        assert off + per_part <= SBUF_HI, f"SBUF overflow allocating {name}: {off}+{per_part}"
        self.sb_ptr = off + per_part
        self.sb_peak = max(self.sb_peak, self.sb_ptr)
        self.nalloc += 1
        h = self.nc.alloc_sbuf_tensor_at(f"{name}_{self.nalloc}", list(shape), dtype, offset=off)
        return TT(self, name, h)

    def sbuf_at(self, name, shape, dtype, off):
        esz = 4 if dtype == F32 else 2
        per_part = int(np.prod(shape[1:])) * esz
        assert off % 64 == 0 and off + per_part <= SBUF_HI
        self.nalloc += 1
        h = self.nc.alloc_sbuf_tensor_at(f"{name}_{self.nalloc}", list(shape), dtype, offset=off)
        return TT(self, name, h), off + per_part

    def mark(self):
        return self.sb_ptr

    def release(self, mark):
        self.sb_ptr = mark

    def init_psum(self):
        for i in range(8):
            h = self.nc.alloc_psum_tensor(f"psb{i}", [128, 512], F32)
            self.psum_banks.append(TT(self, f"psb{i}", h, excl=True))

    def psum(self):
        for _ in range(8):
            i = self.ps_rr
            self.ps_rr = (self.ps_rr + 1) % 8
            if i not in self.ps_pinned:
                return self.psum_banks[i]
        raise RuntimeError("all psum pinned")

    def psum_pin(self):
        t = self.psum()
        self.ps_pinned.add(self.psum_banks.index(t))
        return t

    def psum_unpin(self, t):
        self.ps_pinned.discard(self.psum_banks.index(t))

    def _node(self, eng, fn, reads, writes, cost, is_dma=False, ndesc=0, lat=0.0):
        writes = [w for w in writes if w is not None] + [r for r in reads if r is not None and r.excl]
        reads = [r for r in reads if r is not None and not r.excl]
        nid = len(self.nodes)
        deps = set()
        for r in reads:
            if r.lw is not None:
                deps.add(r.lw)
        for w in writes:
            if w.lw is not None:
                deps.add(w.lw)
            deps.update(w.rd)
        deps.discard(nid)
        self.nodes.append(dict(id=nid, eng=eng, fn=fn, deps=deps, cost=cost, dma=is_dma, ndesc=ndesc, lat=lat,
                               tag=self.tag))
        for r in reads:
            r.rd.add(nid)
        for w in writes:
            w.lw = nid
            w.rd = set()
        return nid

    @staticmethod
    def _nfree(ap):
        try:
            sh = list(ap.shape)
            n = 1
            for d in sh[1:]:
                n *= int(d)
            return n
        except Exception:
            return 128

    def _cost(self, eng, method, kw):
        out = kw.get("out", kw.get("ap"))
        n = self._nfree(out) if out is not None else 128
        if eng == "pe":
            if method == "transpose":
                return 0.12
            f32 = False
            try:
                f32 = kw["lhsT"].tensor.dtype == F32
            except Exception:
                pass
            return PE_C0 + (4.0 if f32 else 1.0) * max(64, n) / PE_RATE
        if eng == "act":
            return 0.22 + n / 1400.0 + (0.1 if "accum_out" in kw else 0.0)
        if eng == "dve":
            return 0.12 + n / 960.0 * (8.0 if method == "reciprocal" else 1.0)
        return (0.9 if method == 'tensor_tensor' and n <= 16 else 0.3) + n / 700.0

    def op(self, eng, method, R=(), W=(), **kw):
        def fn(e, method=method, kw=kw):
            return getattr(e, method)(**kw)
        self.n_ops += 1
        if eng == "pe":
            self.n_pe += 1
        self._node(eng, fn, R, W, self._cost(eng, method, kw))

    def mark_phase(self, name):
        self.marks.append((name, self.n_pe))

    def ops(self, eng, calls, R=(), W=()):
        calls = list(calls)
        if eng == "pe":
            self.n_pe += len(calls)
        self.n_ops += 1

        def fn(e, calls=calls):
            ins = None
            for (method, kw) in calls:
                ins = getattr(e, method)(**kw)
            return ins
        self._node(eng, fn, R, W, sum(self._cost(eng, m, kw) for m, kw in calls))

    def dma(self, queue, R=(), W=(), ndesc=64, **kw):
        def fn(e, kw=kw):
            return e.dma_start(**kw)
        nbytes = self._nfree(kw["out"]) * 128 * 4
        self._node(queue, fn, R, W, 1.2 if queue == "pool" else 0.1, is_dma=True, ndesc=ndesc,
                   lat=2.0 + nbytes / 1.5e5)

    def barrier(self):
        self.seg_bounds.append(len(self.nodes))
        for b in self.allbufs:
            b.lw = None
            b.rd = set()

    HOP = 0.35
    POOL_DESC_CAP = 200

    def _schedule_segment(self, lo, hi):
        import heapq
        if not SCHED:
            return list(range(lo, hi))
        nodes = self.nodes
        ndep = {}
        users = {}
        for n in nodes[lo:hi]:
            d = [x for x in n["deps"] if x >= lo]
            ndep[n["id"]] = len(d)
            for x in d:
                users.setdefault(x, []).append(n["id"])
        blevel = {}
        for n in reversed(nodes[lo:hi]):
            nid = n["id"]
            best = 0.0
            for u in users.get(nid, ()):
                un = nodes[u]
                lat = 0.05 if (un["eng"] == n["eng"] and not n["dma"]) else self.HOP
                v = lat + blevel[u]
                if v > best:
                    best = v
            blevel[nid] = n["cost"] + n["lat"] + best
        fin = {}
        ready_t = {}
        engs = ("pe", "act", "dve", "pool", "sp")
        avail = {e: [] for e in engs}
        free_t = {e: 0.0 for e in engs}
        for n in nodes[lo:hi]:
            if ndep[n["id"]] == 0:
                avail[n["eng"]].append(n["id"])
                ready_t[n["id"]] = 0.0
        order = []
        last_on = {}
        crit_dep = {}
        remaining = hi - lo
        while remaining:
            best = None
            for e in engs:
                lst = avail[e]
                if not lst:
                    continue
                ft = free_t[e]
                cb = None
                for nid in lst:
                    rt = ready_t[nid]
                    st = rt if rt > ft else ft
                    key = (st, -blevel[nid] * PRIO, nid)
                    if cb is None or key < cb:
                        cb = key
                if best is None or cb < best[0]:
                    best = (cb, e)
            (st, _, nid), e = best
            avail[e].remove(nid)
            n = nodes[nid]
            rt = ready_t[nid]
            start = st
            n["start"] = start
            n["why"] = ("eng", last_on.get(e)) if free_t[e] >= rt and last_on.get(e) is not None else ("dep", crit_dep.get(nid))
            last_on[e] = nid
            free_t[e] = start + n["cost"]
            fin[nid] = start + n["cost"] + n["lat"]
            order.append(nid)
            remaining -= 1
            self._busy[e] = self._busy.get(e, 0.0) + n["cost"]
            for u in users.get(nid, ()):
                un = nodes[u]
                lat = 0.05 if (un["eng"] == e and not n["dma"]) else self.HOP
                if fin[nid] + lat >= ready_t.get(u, 0.0):
                    crit_dep[u] = nid
                ready_t[u] = max(ready_t.get(u, 0.0), fin[nid] + lat)
                ndep[u] -= 1
                if ndep[u] == 0:
                    avail[un["eng"]].append(u)
        self.seg_stats.append((lo, hi, max(fin.values()), dict(self._busy)))
        cur = max(fin, key=lambda k: fin[k])
        path = []
        while cur is not None:
            n = nodes[cur]
            path.append((cur, n["tag"], n["eng"], n["start"], n["cost"], n["why"][0]))
            cur = n["why"][1]
        self.crit.append(path[::-1])
        self._busy = {}
        return order

    def _lower(self):
        nodes = self.nodes
        bounds = [0] + [b for b in self.seg_bounds if b > 0]
        if bounds[-1] != len(nodes):
            bounds.append(len(nodes))
        items = {k: [] for k in ("pe", "act", "dve", "pool", "sp")}
        seq = {k: 0 for k in self.COMPUTE}
        waited = {k: {} for k in items}
        tok = {}
        dma_tot = [0] * self.NDMASEM
        dma_rr = 0
        dma_rr_sw = 0
        pool_out = []
        for si in range(len(bounds) - 1):
            lo, hi = bounds[si], bounds[si + 1]
            if hi <= lo:
                continue
            order = self._schedule_segment(lo, hi)
            for nid in order:
                n = nodes[nid]
                q = n["eng"]
                deps = [tok[d] for d in n["deps"] if d >= lo]
                if n["dma"]:
                    if q == "pool":
                        k = 16 + dma_rr_sw
                        dma_rr_sw = (dma_rr_sw + 1) % (self.NDMASEM - 16)
                    else:
                        k = dma_rr
                        dma_rr = (dma_rr + 1) % 16
                    key = f"dma{k}"
                    if dma_tot[k] > 0:
                        deps.append((key, dma_tot[k]))
                    if q == "pool":
                        while pool_out and sum(x for _, x in pool_out) + n["ndesc"] > self.POOL_DESC_CAP:
                            t, _ = pool_out.pop(0)
                            deps.append(t)
                    dma_tot[k] += 16
                    tok[nid] = (key, dma_tot[k])
                    semkey, inc = key, 16
                    if q == "pool":
                        pool_out.append((tok[nid], n["ndesc"]))
                else:
                    seq[q] += 1
                    tok[nid] = (q, seq[q])
                    semkey, inc = q, 1
                wd = waited[q]
                best = {}
                for (k, v) in deps:
                    if k == q:
                        if q == "pe":
                            continue
                        if q in ("act", "dve") and seq[q] - v >= SAME_ENG_DIST:
                            continue
                    if wd.get(k, 0) >= v:
                        continue
                    wd[k] = v
                    best[k] = max(best.get(k, 0), v)
                self.n_wait += len(best)
                items[q].append(("op", list(best.items()), n["fn"], semkey, inc))
            targets = [(k, seq[k]) for k in self.COMPUTE if seq[k] > 0]
            targets += [(f"dma{i}", v) for i, v in enumerate(dma_tot) if v > 0]
            for q in items:
                w = []
                for (k, v) in targets:
                    if k != q and waited[q].get(k, 0) < v:
                        waited[q][k] = v
                        w.append((k, v))
                if w:
                    items[q].append(("wait", w, None, None, 0))
        self.items = items

    def emit(self):
        nc = self.nc
        self._lower()
        with contextlib.ExitStack() as st:
            sems = {}
            for k in self.COMPUTE:
                sems[k] = st.enter_context(nc.semaphore(f"s_{k}"))
            for i in range(self.NDMASEM):
                sems[f"dma{i}"] = st.enter_context(nc.semaphore(f"s_dma{i}"))
            block = st.enter_context(nc.Block())
            items = self.items
            targets = {k: set() for k in self.COMPUTE}
            for lst in items.values():
                for (kind, waits, fn, semkey, inc) in lst:
                    for (k, v) in waits:
                        if k in targets:
                            targets[k].add(v)
            rank = {k: {v: i + 1 for i, v in enumerate(sorted(vs))} for k, vs in targets.items()}
            self.n_sig = {k: len(v) for k, v in targets.items()}

            def run(e, lst):
                idx = 0
                for (kind, waits, fn, semkey, inc) in lst:
                    for (k, v) in waits:
                        e.wait_ge(sems[k], rank[k][v] if k in rank else v)
                    if kind == "op":
                        ins = fn(e)
                        if semkey in rank:
                            idx += 1
                            if idx in rank[semkey]:
                                ins.then_inc(sems[semkey], 1)
                        else:
                            ins.then_inc(sems[semkey], inc)

            @block.tensor
            def _(e):
                run(e, items["pe"])

            @block.scalar
            def _(e):
                run(e, items["act"])

            @block.vector
            def _(e):
                run(e, items["dve"])

            @block.gpsimd
            def _(e):
                run(e, items["pool"])

            @block.sync
            def _(e):
                run(e, items["sp"])


def build_program(stop_after=None, taps=()):
    nc = bass.Bass("TRN2", target_bir_lowering=False)
    P = Prog(nc)
    P.init_psum()
    taps = set(taps)

    def din(name, shape):
        return nc.dram_tensor(name, list(shape), F32, kind="ExternalInput").ap()

    x_d = din("x", [S, D])
    w_in_d = din("w_in", [D, D_IN])
    w_out_d = din("w_out", [D, D])
    w_gate_d = din("w_gate", [D, DFF])
    w_up_d = din("w_up", [D, DFF])
    w_down_d = din("w_down", [DFF, D])
    norm1_d = din("norm1_w", [1, D])
    norm2_d = din("norm2_w", [1, D])
    gnw_d = din("gdn_norm_w", [1, 128])
    slw_d = din("subln_w", [1, 128])
    qnw_d = din("q_norm_w", [1, 64])
    knw_d = din("k_norm_w", [1, 64])
    lqk_d = din("lqk", [1, 256])
    alog_d = din("a_log", [1, 8])
    dtb_d = din("dt_bias", [1, 8])
    convw_d = din("conv_wl", [128, 96])
    cmat_d = din("cmat", [128, 5 * 128])
    out_d = nc.dram_tensor("out", [S, D], F32, kind="ExternalOutput").ap()
    tap_d = {}

    def tap(name, tt, sl, shape, reads):
        if name not in taps:
            return
        d = nc.dram_tensor("tap_" + name, list(shape), F32, kind="ExternalOutput").ap()
        tap_d[name] = d
        P.dma("sp", R=reads, out=d, in_=sl)

    cm = P.sbuf("cm", [128, 5, 128], F32)
    identb = P.sbuf("identb", [128, 128], BF16)
    onesb = P.sbuf("onesb", [128, 128], BF16)
    gnw = P.sbuf("gnw", [128, 128], F32)
    slw8 = P.sbuf("slw8", [128, 128], F32)
    qkw = P.sbuf("qkw", [128, 128], F32)
    lqk = P.sbuf("lqk", [128, 256], F32)
    alog = P.sbuf("alog", [128, 8], F32)
    dtb = P.sbuf("dtb", [128, 8], F32)
    convw = P.sbuf("convw", [128, 96], F32)
    sc = P.sbuf("sc", [128, 16], F32)
    negA = P.sbuf("negA", [128, 8], F32)
    cm05 = P.sbuf("cm05", [128, 64], F32)
    epsc = P.sbuf("epsc", [128, 1], F32)
    ident_f = cm[:, 0, :]
    ltri_f = cm[:, 1, :]
    ones_f = cm[:, 2, :]
    sel_f = [cm[:, 3, :], cm[:, 4, :]]
    CM = cm.s()

    P.dma("sp", W=[CM], out=cm[:].rearrange("p a b -> p (a b)"), in_=cmat_d[:, :])
    P.dma("pool", W=[identb.s()], out=identb[:], in_=cmat_d[:, 0:128])
    P.dma("pool", W=[onesb.s()], out=onesb[:], in_=cmat_d[:, 256:384])
    P.dma("sp", W=[gnw.s()], out=gnw[:], in_=gnw_d.partition_broadcast(128))
    P.dma("sp", W=[slw8.s()], out=slw8[:], in_=slw_d.partition_broadcast(128))
    P.dma("sp", W=[qkw.s()], out=qkw[:, 0:64], in_=qnw_d.partition_broadcast(128))
    P.dma("sp", W=[qkw.s()], out=qkw[:, 64:128], in_=knw_d.partition_broadcast(128))
    P.dma("sp", W=[lqk.s()], out=lqk[:], in_=lqk_d.partition_broadcast(128))
    P.dma("sp", W=[alog.s()], out=alog[:], in_=alog_d.partition_broadcast(128))
    P.dma("sp", W=[dtb.s()], out=dtb[:], in_=dtb_d.partition_broadcast(128))
    P.dma("sp", W=[convw.s()], out=convw[:], in_=convw_d[:, :])

    P.op("pool", "memset", W=[cm05.s()], ap=cm05[:], constant=-0.5)
    P.op("pool", "memset", W=[epsc.s()], ap=epsc[:], constant=EPS)
    P.op("pool", "tensor_scalar", R=[slw8.s()], W=[slw8.s()], out=slw8[:], in0=slw8[:], scalar1=0.8,
         scalar2=None, op0=ALU.mult)
    P.op("pool", "tensor_scalar", R=[qkw.s()], W=[qkw.s()], out=qkw[:, 0:64], in0=qkw[:, 0:64],
         scalar1=64 ** -0.5, scalar2=None, op0=ALU.mult)
    P.op("dve", "tensor_tensor", R=[lqk.s()], W=[lqk.s()], out=lqk[:, 0:64], in0=lqk[:, 0:64],
         in1=lqk[:, 64:128], op=ALU.mult)
    P.op("dve", "tensor_tensor", R=[lqk.s()], W=[lqk.s()], out=lqk[:, 128:192], in0=lqk[:, 128:192],
         in1=lqk[:, 192:256], op=ALU.mult)
    P.op("dve", "tensor_reduce", R=[lqk.s()], W=[sc.s()], out=sc[:, 0:1], in_=lqk[:, 0:64], axis=AX.X,
         op=ALU.add)
    P.op("dve", "tensor_reduce", R=[lqk.s(), sc.s()], W=[sc.s()], out=sc[:, 1:2], in_=lqk[:, 128:192],
         axis=AX.X, op=ALU.add)
    P.op("act", "activation", R=[sc.s()], W=[sc.s()], out=sc[:, 2:4], in_=sc[:, 0:2], func=AF.Exp)
    P.op("dve", "tensor_tensor", R=[sc.s()], W=[sc.s()], out=sc[:, 4:5], in0=sc[:, 2:3], in1=sc[:, 3:4],
         op=ALU.subtract)
    P.op("dve", "tensor_scalar", R=[sc.s()], W=[sc.s()], out=sc[:, 5:6], in0=sc[:, 4:5], scalar1=0.2,
         scalar2=-1.0, op0=ALU.add, op1=ALU.mult)
    P.op("act", "activation", R=[alog.s()], W=[negA.s()], out=negA[:], in_=alog[:], func=AF.Exp)
    P.op("dve", "tensor_scalar", R=[negA.s()], W=[negA.s()], out=negA[:], in0=negA[:], scalar1=-1.0,
         scalar2=None, op0=ALU.mult)
    nlam = sc[:, 5:6]

    def cut_here(tag):
        if stop_after == tag:
            d = nc.dram_tensor("tap_sc", [128, 16], F32, kind="ExternalOutput").ap()
            P.dma("sp", R=[sc.s()], out=d, in_=sc[:])
            P.barrier()
            P.emit()
            return True
        return False
    if cut_here("A0"):
        return nc, P

    uT = P.sbuf("uT", [128, 8, S], BF16)
    gcs = P.sbuf("gcs", [128, NT, 8], F32)
    lbs = P.sbuf("lbs", [128, NT, 8], F32)
    glt = P.sbuf("glt", [128, NT, 8], F32)
    egs = P.sbuf("egs", [128, NT, 2, 8], F32)

    mA = P.mark()
    xall = P.sbuf("xall", [128, NT, D], F32)
    wrow1 = P.sbuf("wrow1", [128, D], F32)
    P.dma("sp", W=[wrow1.s()], out=wrow1[:], in_=norm1_d.partition_broadcast(128))
    junk = P.sbuf("junk", [128, D], BF16)
    ssA = P.sbuf("ssA", [128, NT], F32)
    rstd1 = P.sbuf("rstd1", [128, NT], F32)
    xn32 = [P.sbuf(f"xn32_{i}", [128, D], F32) for i in range(2)]
    uT32 = [P.sbuf(f"uT32_{i}", [128, D], F32) for i in range(2)]
    wab = P.sbuf("wab", [128, 8, 16], F32)
    gab = P.sbuf("gab", [128, NT, 16], F32)
    arg = P.sbuf("arg", [128, NT, 16], F32)
    g32 = P.sbuf("g32", [128, NT, 8], F32)

    P.dma("sp", W=[wab.s()], out=wab[:],
          in_=w_in_d[:, COL["ga"]:COL["ga"] + 16].rearrange("(k p) c -> p k c", p=128))
    for tt in range(NT):
        P.dma("sp", W=[xall.s(tt)], out=xall[:, tt, :], in_=x_d[tt * 128:(tt + 1) * 128, :])
    for tt in range(NT):
        P.op("act", "activation", R=[xall.s(tt)], W=[junk.s(), ssA.s(tt)], out=junk[:], in_=xall[:, tt, :],
             func=AF.Square, accum_out=ssA[:, tt:tt + 1])
    if cut_here("A1"):
        return nc, P
    for tt in range(NT):
        P.op("dve", "tensor_scalar", R=[ssA.s(tt)], W=[rstd1.s(tt)], out=rstd1[:, tt:tt + 1], in0=ssA[:, tt:tt + 1],
             scalar1=1.0 / D, scalar2=EPS, op0=ALU.mult, op1=ALU.add)
        P.op("pool", "tensor_tensor", R=[rstd1.s(tt), cm05.s()], W=[rstd1.s(tt)], out=rstd1[:, tt:tt + 1],
             in0=rstd1[:, tt:tt + 1], in1=cm05[:, 0:1], op=ALU.pow)
    psG = P.psum_pin()
    for tt in range(NT):
        xn = xn32[tt % 2]
        u32 = uT32[tt % 2]
        P.op("dve", "scalar_tensor_tensor", R=[xall.s(tt), rstd1.s(tt), wrow1.s()], W=[xn.s()], out=xn[:],
             in0=xall[:, tt, :], scalar=rstd1[:, tt:tt + 1], in1=wrow1[:], op0=ALU.mult, op1=ALU.mult)
        for half in range(2):
            ps = P.psum()
            P.ops("pe", [("transpose", dict(out=ps[:, j * 128:(j + 1) * 128],
                                            in_=xn[:, (half * 4 + j) * 128:(half * 4 + j + 1) * 128],
                                            identity=ident_f)) for j in range(4)],
                  R=[xn.s(), CM], W=[ps.s()])
            if half == 0:
                P.op("act", "activation", R=[ps.s()], W=[u32.s(half)], out=u32[:, 0:512], in_=ps[:, 0:512],
                     func=AF.Copy)
            else:
                P.op("dve", "tensor_copy", R=[ps.s()], W=[u32.s(half)], out=u32[:, 512:1024], in_=ps[:, 0:512])
        P.op("pool", "tensor_copy", R=[u32.s(0), u32.s(1)], W=[uT.s(tt)], out=uT[:, :, tt * 128:(tt + 1) * 128],
             in_=u32[:].rearrange("p (k t) -> p k t", k=8))
        P.ops("pe", [("matmul", dict(out=psG[:, tt * 16:(tt + 1) * 16], lhsT=u32[:, kc * 128:(kc + 1) * 128],
                                     rhs=wab[:, kc, :], start=(kc == 0), stop=(kc == 7))) for kc in range(8)],
              R=[u32.s(0), u32.s(1), wab.s()], W=[psG.s()])
    if cut_here("A2"):
        return nc, P
    P.op("act", "activation", R=[psG.s()], W=[gab.s()], out=gab[:].rearrange("p a b -> p (a b)"),
         in_=psG[:, 0:NT * 16], func=AF.Copy)
    P.psum_unpin(psG)
    for tt in range(NT):
        P.op("dve", "tensor_tensor", R=[gab.s(), dtb.s()], W=[arg.s()], out=arg[:, tt, 0:8], in0=gab[:, tt, 0:8],
             in1=dtb[:], op=ALU.add)
    P.op("dve", "tensor_scalar", R=[gab.s()], W=[arg.s()], out=arg[:, :, 8:16], in0=gab[:, :, 8:16],
         scalar1=-1.0, scalar2=None, op0=ALU.mult)
    argf = arg[:].rearrange("p a b -> p (a b)")
    P.op("act", "activation", R=[arg.s()], W=[arg.s()], out=argf, in_=argf, func=AF.Exp)
    P.op("act", "activation", R=[arg.s()], W=[arg.s()], out=argf, in_=argf, func=AF.Ln, bias=1.0, scale=1.0)
    for tt in range(NT):
        P.op("dve", "tensor_tensor", R=[arg.s(), negA.s()], W=[g32.s()], out=g32[:, tt, :], in0=arg[:, tt, 0:8],
             in1=negA[:], op=ALU.mult)
    P.op("dve", "tensor_scalar", R=[arg.s()], W=[lbs.s()], out=lbs[:], in0=arg[:, :, 8:16], scalar1=-1.0,
         scalar2=None, op0=ALU.mult)
    if cut_here("A3"):
        return nc, P
    ps = P.psum()
    P.ops("pe", [("matmul", dict(out=ps[:, tt * 8:(tt + 1) * 8], lhsT=ltri_f, rhs=g32[:, tt, :], start=True,
                                 stop=True)) for tt in range(NT)], R=[g32.s(), CM], W=[ps.s()])
    P.op("act", "activation", R=[ps.s()], W=[gcs.s()], out=gcs[:].rearrange("p a b -> p (a b)"),
         in_=ps[:, 0:NT * 8], func=AF.Copy)
    ps = P.psum()
    P.ops("pe", [("matmul", dict(out=ps[:, (tt * 2 + c) * 8:(tt * 2 + c + 1) * 8], lhsT=sel_f[c],
                                 rhs=gcs[:, tt, :], start=True, stop=True))
                 for tt in range(NT) for c in range(2)], R=[gcs.s(), CM], W=[ps.s()])
    P.op("act", "activation", R=[ps.s()], W=[egs.s()], out=egs[:].rearrange("p a c b -> p (a c b)"),
         in_=ps[:, 0:NT * 16], func=AF.Exp)
    if cut_here("A4"):
        return nc, P
    psv = ps[:, 0:NT * 16].rearrange("p (a c b) -> p a c b", a=NT, c=2)
    P.op("dve", "tensor_copy", R=[ps.s()], W=[glt.s()], out=glt[:, :, :], in_=psv[:, :, 1, :])
    if cut_here("A5"):
        return nc, P
    tap("gcs", gcs, gcs[:], [128, NT, 8], [gcs.s()])
    tap("lbs", lbs, lbs[:], [128, NT, 8], [lbs.s()])
    tap("egs", egs, egs[:], [128, NT, 2, 8], [egs.s()])
    tap("glt", glt, glt[:], [128, NT, 8], [glt.s()])
    P.barrier()
    P.release(mA)

    if stop_after == "A":
        u_dbg = nc.dram_tensor("tap_uT", [128, 8, S], BF16, kind="ExternalOutput").ap()
        P.dma("sp", R=[uT.s(tt) for tt in range(NT)], out=u_dbg, in_=uT[:])
        P.barrier()
        P.emit()
        return nc, P


    MIXT_OFF = (P.sb_ptr + 63) // 64 * 64
    mixT = P.sbuf("mixT", [128, H, S], BF16)
    mB = P.mark()
    wh = [P.sbuf("wh0", [128, 8, 9 * 128], BF16)] * 2
    xc = P.sbuf("xc", [128, 2, S + 4], BF16)
    dg = P.sbuf("dg", [128, 12, 128], BF16)
    kqv = P.sbuf("kqv", [128, 3, S], BF16)
    sgt = [P.sbuf("sgt0", [128, 512], BF16)]
    GAb = [P.sbuf(f"GA{i}", [128, NT, 128], BF16) for i in range(2)]
    GBt = P.sbuf("GBt", [128, NT, 128], BF16)
    vd = P.sbuf("vd", [128, NT, 130], BF16)
    dqT = P.sbuf("dqT", [128, S], BF16)
    dkT = P.sbuf("dkT", [128, S], BF16)
    tz = [P.sbuf(f"tz{i}", [128, 384], BF16) for i in range(2)]
    sq4 = [P.sbuf(f"sq4{i}", [128, 256], BF16) for i in range(2)]
    rs4 = [P.sbuf(f"rs4{i}", [128, 4], F32) for i in range(2)]
    qkn = [P.sbuf(f"qkn{i}", [128, 256], BF16) for i in range(2)]
    t1 = [P.sbuf(f"t1{i}", [128, 128], BF16) for i in range(2)]
    t2 = [P.sbuf(f"t2{i}", [128, 128], BF16) for i in range(2)]
    ssq = P.sbuf("ssq", [128, NT, 2], F32)
    tsc = P.sbuf("tsc", [128, 10, NT], F32)
    kbg = P.sbuf("kbg", [128, NT, 128], BF16)
    kdec = P.sbuf("kdec", [128, NT, 128], BF16)
    vb = P.sbuf("vb", [128, NT, 128], BF16)
    dgf = [P.sbuf(f"dgf{i}", [128, 256], F32) for i in range(2)] * 2
    eaq = [P.sbuf(f"eaq{i}", [128, 256], F32) for i in range(2)] * 2
    wlv = [[P.sbuf(f"wlv{i}_{k}", [128, 384], BF16) for k in range(2)] for i in range(PREP_B)]
    aqt = P.sbuf("aqt", [128, NT, 128], BF16)
    u32 = P.sbuf("u32", [128, NT, 128], F32)
    wT = P.sbuf("wT", [128, S], BF16)
    S32 = P.sbuf("S32", [128, 128], F32)
    Sb = P.sbuf("Sb", [128, 128], BF16)
    vn = P.sbuf("vn", [128, 2, 128], BF16)
    o1s = P.sbuf("o1s", [128, 2, 128], F32)
    oa = P.sbuf("oa", [128, NT, 128], BF16)
    sso = P.sbuf("sso", [128, NT], F32)
    junk2 = P.sbuf("junk2", [128, 128], BF16)
    rso = P.sbuf("rso", [128, NT], F32)
    pt = [P.sbuf(f"pt{i}", [128, 512], BF16) for i in range(4)]
    ob = [P.sbuf(f"ob{i}", [128, 128], F32) for i in range(4)]
    tb = [P.sbuf(f"tb{i}", [128, 128], F32) for i in range(4)]
    rsum = [P.sbuf(f"rsum{i}", [128, 4], F32) for i in range(4)]
    ob16 = P.sbuf("ob16", [128, NT, 128], BF16)
    rs16 = P.sbuf("rs16", [128, NT], F32)
    junk4 = P.sbuf("junk4", [128, 128], BF16)
    rr16 = P.sbuf("rr16", [128, NT], F32)
    TS_LRQ, TS_LRK, TS_CA, TS_CQ, TS_BJ, TS_SK1, TS_SK2, TS_BETA, TS_F, TS_TMP = range(10)

    P.op("pool", "memset", W=[xc.s(g, "pad") for g in range(2)], ap=xc[:, :, 0:3], constant=0.0)
    P.op("pool", "memset", W=[vd.s(tt) for tt in range(NT)], ap=vd[:, :, 128:130], constant=1.0)

    def load_head_weights(h):
        w = wh[h % 2]
        for g, nm in enumerate(HEAD_GROUPS):
            c0 = COL[nm] + h * 128
            P.dma("pool", W=[w.s(g)], out=w[:, :, g * 128:(g + 1) * 128],
                  in_=w_in_d[:, c0:c0 + 128].rearrange("(k p) c -> p k c", p=128))

    evac_rr = [0]

    def evac_copy(out, in_, R, W):
        evac_rr[0] ^= 1
        if evac_rr[0]:
            P.op("act", "activation", R=R, W=W, out=out, in_=in_, func=AF.Copy)
        else:
            P.op("dve", "tensor_copy", R=R, W=W, out=out, in_=in_)

    def inproj_T(h):
        w = wh[h % 2]
        GA = GAb[h % 2]
        pst = {}

        def s12(tt):
            i2 = tt % 2
            psA = P.psum()
            psB = P.psum()
            pst[tt] = psA
            lhs = lambda kc: uT[:, kc, tt * 128:(tt + 1) * 128]
            P.ops("pe", [("matmul", dict(out=psA[:, 0:512], lhsT=lhs(kc), rhs=w[:, kc, 3 * 128:7 * 128],
                                         start=(kc == 0), stop=(kc == 7))) for kc in range(8)],
                  R=[uT.s(tt)] + [w.s(g) for g in (3, 4, 5, 6)], W=[psA.s()])
            P.ops("pe", [("matmul", dict(out=psB[:, 0:256], lhsT=lhs(kc), rhs=w[:, kc, 7 * 128:9 * 128],
                                         start=(kc == 0), stop=(kc == 7))) for kc in range(8)],
                  R=[uT.s(tt)] + [w.s(g) for g in (7, 8)], W=[psB.s()])
            z = tz[i2]
            P.op("act", "activation", R=[psA.s()], W=[z.s()], out=z[:, 0:128], in_=psA[:, 0:128], func=AF.Sigmoid)
            P.op("act", "activation", R=[psB.s(), z.s()], W=[z.s()], out=z[:, 128:384], in_=psB[:, 0:256],
                 func=AF.Sigmoid)
            P.op("act", "activation", R=[psA.s()], W=[sq4[i2].s()], out=sq4[i2][:], in_=psA[:, 128:384],
                 func=AF.Square)
            P.op("act", "activation", R=[psA.s()], W=[vd.s(tt)], out=vd[:, tt, 0:128], in_=psA[:, 384:512],
                 func=AF.Copy)
            P.op("dve", "tensor_tensor", R=[psA.s(), z.s()], W=[t1[i2].s()], out=t1[i2][:], in0=psA[:, 0:128],
                 in1=z[:, 0:128], op=ALU.mult)
            P.op("pool", "tensor_tensor", R=[z.s(), gnw.s()], W=[t2[i2].s()], out=t2[i2][:], in0=z[:, 128:256],
                 in1=gnw[:], op=ALU.mult)
            P.op("pool", "tensor_tensor", R=[t1[i2].s(), t2[i2].s()], W=[GA.s(tt)], out=GA[:, tt, :],
                 in0=t1[i2][:], in1=t2[i2][:], op=ALU.mult)
            P.op("pool", "tensor_tensor", R=[z.s(), slw8.s()], W=[GBt.s(tt)], out=GBt[:, tt, :],
                 in0=z[:, 256:384], in1=slw8[:], op=ALU.mult)
            r4 = rs4[i2]
            P.op("dve", "tensor_reduce", R=[sq4[i2].s()], W=[r4.s()], out=r4[:],
                 in_=sq4[i2][:].rearrange("p (a b) -> p a b", a=4), axis=AX.X, op=ALU.add)
            P.op("dve", "tensor_scalar", R=[r4.s()], W=[r4.s()], out=r4[:], in0=r4[:], scalar1=1.0 / 64,
                 scalar2=EPS, op0=ALU.mult, op1=ALU.add)
            P.op("pool", "tensor_tensor", R=[r4.s(), cm05.s()], W=[r4.s()], out=r4[:], in0=r4[:],
                 in1=cm05[:, 0:4], op=ALU.pow)

        def s3(tt):
            i2 = tt % 2
            psA = pst.pop(tt)
            r4 = rs4[i2]
            qn = qkn[i2]
            for sgi in range(4):
                wsl = qkw[:, 0:64] if sgi < 2 else qkw[:, 64:128]
                P.op("dve", "scalar_tensor_tensor", R=[psA.s(), r4.s(), qkw.s()], W=[qn.s()],
                     out=qn[:, sgi * 64:(sgi + 1) * 64], in0=psA[:, 128 + sgi * 64:128 + (sgi + 1) * 64],
                     scalar=r4[:, sgi:sgi + 1], in1=wsl, op0=ALU.mult, op1=ALU.mult)
            ps = P.psum()
            psb = ps.ap.bitcast(BF16)
            P.ops("pe", [("transpose", dict(out=psb[:, j * 128:(j + 1) * 128], in_=qn[:, j * 128:(j + 1) * 128],
                                            identity=identb[:])) for j in range(2)],
                  R=[qn.s(), identb.s()], W=[ps.s()])
            P.op("act", "activation", R=[ps.s()], W=[dqT.s(tt)], out=dqT[:, tt * 128:(tt + 1) * 128],
                 in_=psb[:, 0:128], func=AF.Copy)
            P.op("dve", "tensor_copy", R=[ps.s()], W=[dkT.s(tt)], out=dkT[:, tt * 128:(tt + 1) * 128],
                 in_=psb[:, 128:256])

        for tt in range(NT):
            s12(tt)
            s3(tt)
            yield 2.0

    def inproj_F(h):
        w = wh[h % 2]
        for g in range(3):
            for j in range(4):
                col = (g * 8 + h) * 4 + j
                P.op("act", "activation", R=[identb.s(), convw.s()], W=[dg.s(g)], out=dg[:, g * 4 + j, :],
                     in_=identb[:], func=AF.Copy, scale=convw[:, col:col + 1])
        for g in range(3):
            xs = g % 2
            for tc in range(4):
                ps = P.psum()
                P.ops("pe", [("matmul", dict(out=ps[:, 0:512], lhsT=w[:, kc, g * 128:(g + 1) * 128],
                                             rhs=uT[:, kc, tc * 512:(tc + 1) * 512], start=(kc == 0),
                                             stop=(kc == 7))) for kc in range(8)],
                      R=[w.s(g)] + [uT.s(tt) for tt in range(4 * tc, 4 * tc + 4)], W=[ps.s()])
                evac_copy(xc[:, xs, 3 + tc * 512:3 + (tc + 1) * 512], ps[:, 0:512], [ps.s()], [xc.s(xs, tc)])
            for tc in range(4):
                ps = P.psum()
                R = [dg.s(g), xc.s(xs, tc)] + ([xc.s(xs, tc - 1)] if tc > 0 else [xc.s(xs, "pad")])
                P.ops("pe", [("matmul", dict(out=ps[:, 0:512], lhsT=dg[:, g * 4 + j, :],
                                             rhs=xc[:, xs, tc * 512 + j:tc * 512 + j + 512], start=(j == 0),
                                             stop=(j == 3))) for j in range(4)], R=R, W=[ps.s()])
                sg = sgt[0]
                P.op("act", "activation", R=[ps.s()], W=[sg.s()], out=sg[:], in_=ps[:, 0:512], func=AF.Sigmoid)
                P.op("dve", "tensor_tensor", R=[ps.s(), sg.s()], W=[kqv.s(g, tc)],
                     out=kqv[:, g, tc * 512:(tc + 1) * 512], in0=ps[:, 0:512], in1=sg[:], op=ALU.mult)

    def tsl(i):
        return tsc[:, i, :]

    def gdn_scalars(h):
        for g in range(2):
            P.op("act", "activation", R=[kqv.s(g, tc) for tc in range(4)], W=[xc.s(g, tc) for tc in range(4)],
                 out=xc[:, g, 3:3 + S], in_=kqv[:, g, :], func=AF.Square)
        ps = P.psum()
        P.ops("pe", [("matmul", dict(out=ps[:, tt * 2 + g:tt * 2 + g + 1],
                                     lhsT=xc[:, g, 3 + tt * 128:3 + (tt + 1) * 128],
                                     rhs=onesb[:, 0:1], start=True, stop=True))
                     for tt in range(NT) for g in range(2)],
              R=[xc.s(g, tc) for g in range(2) for tc in range(4)] + [onesb.s()], W=[ps.s()])
        P.op("act", "activation", R=[ps.s()], W=[ssq.s()], out=ssq[:].rearrange("p a b -> p (a b)"),
             in_=ps[:, 0:2 * NT], func=AF.Ln, bias=epsc[:, 0:1], scale=1.0)
        T = tsc.s()
        gch = gcs[:, :, h]
        lbh = lbs[:, :, h]
        glh = glt[:, :, h]
        P.op("dve", "tensor_scalar", R=[ssq.s()], W=[T], out=tsl(TS_LRQ), in0=ssq[:, :, 0], scalar1=-0.5,
             scalar2=-0.5 * float(np.log(128.0)), op0=ALU.mult, op1=ALU.add)
        P.op("dve", "tensor_scalar", R=[ssq.s(), T], W=[T], out=tsl(TS_LRK), in0=ssq[:, :, 1], scalar1=-0.5,
             scalar2=None, op0=ALU.mult)
        P.op("dve", "tensor_tensor", R=[gcs.s(), T], W=[T], out=tsl(TS_CQ), in0=gch, in1=tsl(TS_LRQ), op=ALU.add)
        P.op("dve", "tensor_tensor", R=[gcs.s(), T], W=[T], out=tsl(TS_TMP), in0=gch, in1=tsl(TS_LRK), op=ALU.add)
        P.op("dve", "tensor_tensor", R=[lbs.s(), T], W=[T], out=tsl(TS_CA), in0=tsl(TS_TMP), in1=lbh, op=ALU.add)
        P.op("dve", "tensor_tensor", R=[gcs.s(), T], W=[T], out=tsl(TS_BJ), in0=tsl(TS_LRK), in1=gch,
             op=ALU.subtract)
        P.op("act", "activation", R=[T], W=[T], out=tsl(TS_SK1), in_=tsl(TS_CA), func=AF.Exp)
        P.op("dve", "tensor_tensor", R=[glt.s(), T], W=[T], out=tsl(TS_TMP), in0=tsl(TS_BJ), in1=glh, op=ALU.add)
        P.op("act", "activation", R=[T], W=[T], out=tsl(TS_SK2), in_=tsl(TS_TMP), func=AF.Exp)
        P.op("act", "activation", R=[lbs.s(), T], W=[T], out=tsl(TS_BETA), in_=lbh, func=AF.Exp)
        P.op("act", "activation", R=[T], W=[T], out=tsl(TS_F), in_=tsl(TS_CQ), func=AF.Exp)

    def gdn_prep(h):
        T = tsc.s()
        for t0 in range(0, NT, PREP_B):
            tiles = list(range(t0, t0 + PREP_B))
            for i, tt in enumerate(tiles):
                tc = tt // 4
                tsl_ = slice(tt * 128, (tt + 1) * 128)
                ps = P.psum()
                psb = ps.ap.bitcast(BF16)
                P.ops("pe", [("transpose", dict(out=psb[:, 0:128], in_=kqv[:, 1, tsl_], identity=identb[:])),
                             ("transpose", dict(out=psb[:, 128:256], in_=kqv[:, 2, tsl_], identity=identb[:]))],
                      R=[kqv.s(1, tc), kqv.s(2, tc), identb.s()], W=[ps.s()])
                P.op("act", "activation", R=[ps.s(), T], W=[kbg.s(tt)], out=kbg[:, tt, :], in_=psb[:, 0:128],
                     func=AF.Copy, scale=tsc[:, TS_SK1, tt:tt + 1])
                P.op("dve", "tensor_scalar", R=[ps.s(), T], W=[kdec.s(tt)], out=kdec[:, tt, :], in0=psb[:, 0:128],
                     scalar1=tsc[:, TS_SK2, tt:tt + 1], scalar2=None, op0=ALU.mult)
                P.op("act", "activation", R=[ps.s(), T], W=[vb.s(tt)], out=vb[:, tt, :], in_=psb[:, 128:256],
                     func=AF.Copy, scale=tsc[:, TS_BETA, tt:tt + 1])
                d = dgf[i % 2]
                P.op("act", "activation", R=[CM, T], W=[d.s()], out=d[:, 0:128], in_=ident_f, func=AF.Copy,
                     scale=tsc[:, TS_CA, tt:tt + 1])
                P.op("act", "activation", R=[CM, T, d.s()], W=[d.s()], out=d[:, 128:256], in_=ident_f, func=AF.Copy,
                     scale=tsc[:, TS_CQ, tt:tt + 1])
                psE = P.psum()
                P.ops("pe", [("matmul", dict(out=psE[:, 0:128], lhsT=ones_f, rhs=d[:, 0:128], start=True, stop=True)),
                             ("matmul", dict(out=psE[:, 128:256], lhsT=ones_f, rhs=d[:, 128:256], start=True,
                                             stop=True))], R=[CM, d.s()], W=[psE.s()])
                e = eaq[i % 2]
                P.op("act", "activation", R=[psE.s(), T], W=[e.s()], out=e[:], in_=psE[:, 0:256], func=AF.Exp,
                     bias=tsc[:, TS_BJ, tt:tt + 1], scale=1.0)
                P.op("pool", "affine_select", R=[e.s()], W=[e.s()], out=e[:, 0:128], in_=e[:, 0:128],
                     pattern=[[1, 128]], compare_op=ALU.is_gt, fill=0.0, base=0, channel_multiplier=-1)
                P.op("pool", "affine_select", R=[e.s()], W=[e.s()], out=e[:, 128:256], in_=e[:, 128:256],
                     pattern=[[1, 128]], compare_op=ALU.is_ge, fill=0.0, base=0, channel_multiplier=-1)
                psK = P.psum()
                P.ops("pe", [("matmul", dict(out=psK[:, 0:128], lhsT=kqv[:, 1, tsl_], rhs=kqv[:, 1, tsl_],
                                             start=True, stop=True)),
                             ("matmul", dict(out=psK[:, 128:256], lhsT=kqv[:, 1, tsl_], rhs=kqv[:, 0, tsl_],
                                             start=True, stop=True))],
                      R=[kqv.s(0, tc), kqv.s(1, tc)], W=[psK.s()])
                w0 = wlv[i][0]
                P.op("dve", "scalar_tensor_tensor", R=[psK.s(), e.s()], W=[w0.s()], out=w0[:, 0:128],
                     in0=psK[:, 0:128], scalar=-1.0, in1=e[:, 0:128], op0=ALU.mult, op1=ALU.mult)
                P.op("dve", "tensor_tensor", R=[psK.s(), e.s()], W=[aqt.s(tt)], out=aqt[:, tt, :],
                     in0=psK[:, 128:256], in1=e[:, 128:256], op=ALU.mult)
                P.op("pool", "tensor_copy", R=[identb.s(), w0.s()], W=[w0.s()], out=w0[:, 128:256], in_=identb[:])
                psN = P.psum()
                psNb = psN.ap.bitcast(BF16)
                P.op("pe", "transpose", R=[w0.s(), identb.s()], W=[psN.s()], out=psNb[:, 0:128], in_=w0[:, 0:128],
                     identity=identb[:])
                P.op("act", "activation", R=[psN.s(), w0.s()], W=[w0.s()], out=w0[:, 256:384], in_=psNb[:, 0:128],
                     func=AF.Copy)
                yield 4.0
            for lvl in range(7):
                for i, tt in enumerate(tiles):
                    wc = wlv[i][lvl % 2]
                    wn = wlv[i][(lvl + 1) % 2]
                    ps = P.psum()
                    Mk, Rk, Nk = wc[:, 0:128], wc[:, 128:256], wc[:, 256:384]
                    calls = []
                    if lvl <= 4:
                        calls.append(("matmul", dict(out=ps[:, 0:256], lhsT=Nk, rhs=wc[:, 0:256], start=True, stop=True)))
                    else:
                        calls.append(("matmul", dict(out=ps[:, 128:256], lhsT=Nk, rhs=Rk, start=True, stop=True)))
                    if lvl <= 5:
                        calls.append(("matmul", dict(out=ps[:, 256:384], lhsT=Mk, rhs=Nk, start=True, stop=True)))
                    P.ops("pe", calls, R=[wc.s()], W=[ps.s()])
                    P.op("dve", "tensor_tensor", R=[ps.s(), wc.s()], W=[wn.s()], out=wn[:, 128:256],
                         in0=ps[:, 128:256], in1=Rk, op=ALU.add)
                    if lvl <= 4:
                        o3 = wn[:, 0:384].rearrange("p (a b) -> p a b", a=3)[:, 0:3:2, :]
                        i3 = ps[:, 0:384].rearrange("p (a b) -> p a b", a=3)[:, 0:3:2, :]
                    elif lvl == 5:
                        o3, i3 = wn[:, 256:384], ps[:, 256:384]
                    if lvl <= 5:
                        if i % 2 == 0:
                            P.op("act", "activation", R=[ps.s(), wn.s()], W=[wn.s()], out=o3, in_=i3, func=AF.Copy)
                        else:
                            P.op("dve", "tensor_copy", R=[ps.s(), wn.s()], W=[wn.s()], out=o3, in_=i3)
                yield 5.0
            for i, tt in enumerate(tiles):
                wf = wlv[i][1]
                ps = P.psum()
                P.ops("pe", [("matmul", dict(out=ps[:, 0:128], lhsT=wf[:, 128:256], rhs=vb[:, tt, :], start=True,
                                             stop=True)),
                             ("matmul", dict(out=ps[:, 128:256], lhsT=kbg[:, tt, :], rhs=wf[:, 128:256], start=True,
                                             stop=True))], R=[wf.s(), vb.s(tt), kbg.s(tt)], W=[ps.s()])
                P.op("act", "activation", R=[ps.s()], W=[u32.s(tt)], out=u32[:, tt, :], in_=ps[:, 0:128], func=AF.Copy)
                P.op("dve", "tensor_copy", R=[ps.s()], W=[wT.s(tt)], out=wT[:, tt * 128:(tt + 1) * 128],
                     in_=ps[:, 128:256])
            yield 2.0

    def gdn_recurrence(h):
        T = tsc.s()
        GA = GAb[h % 2]
        P.op("pool", "memset", W=[S32.s()], ap=S32[:], constant=0.0)
        P.op("pool", "memset", W=[Sb.s()], ap=Sb[:], constant=0.0)
        for tt in range(NT):
            tc = tt // 4
            for c in (1,):
                rows = slice(0, 128)
                cols = slice(tt * 128, (tt + 1) * 128)
                ps = P.psum()
                P.ops("pe", [("matmul", dict(out=ps[rows, 0:128], lhsT=wT[:, cols], rhs=Sb[:], start=True, stop=True)),
                             ("matmul", dict(out=ps[rows, 128:256], lhsT=kqv[:, 0, cols], rhs=Sb[:], start=True,
                                             stop=True))], R=[wT.s(tt), kqv.s(0, tc), Sb.s()], W=[ps.s()])
                P.op("dve", "tensor_tensor", R=[ps.s(), u32.s(tt)], W=[vn.s(tt % 2, c)], out=vn[rows, tt % 2, :],
                     in0=u32[rows, tt, :], in1=ps[rows, 0:128], op=ALU.subtract)
                P.op("act", "activation", R=[ps.s(), T], W=[o1s.s(tt % 2, c)], out=o1s[rows, tt % 2, :], in_=ps[rows, 128:256],
                     func=AF.Copy, scale=tsc[rows, TS_F, tt:tt + 1])
                psS = P.psum()
                P.op("pe", "matmul", R=[kdec.s(tt), vn.s(tt % 2, c)], W=[psS.s()], out=psS[:, 0:128],
                     lhsT=kdec[rows, tt, :], rhs=vn[rows, tt % 2, :], start=True, stop=True)
                eg = egs[:, tt, c, h:h + 1]
                P.op("dve", "scalar_tensor_tensor", R=[psS.s(), S32.s(), egs.s()], W=[Sb.s()], out=Sb[:],
                     in0=S32[:], scalar=eg, in1=psS[:, 0:128], op0=ALU.mult, op1=ALU.add)
                P.op("dve", "scalar_tensor_tensor", R=[psS.s(), S32.s(), egs.s()], W=[S32.s()], out=S32[:],
                     in0=S32[:], scalar=eg, in1=psS[:, 0:128], op0=ALU.mult, op1=ALU.add)
                yield 3.0
            ps = P.psum()
            P.op("pe", "matmul", R=[aqt.s(tt), vn.s(tt % 2, 1)], W=[ps.s()], out=ps[:, 0:128],
                 lhsT=aqt[:, tt, :], rhs=vn[:, tt % 2, :], start=True, stop=True)
            P.op("dve", "tensor_tensor", R=[ps.s(), o1s.s(tt % 2, 1)], W=[oa.s(tt)], out=oa[:, tt, :],
                 in0=ps[:, 0:128], in1=o1s[:, tt % 2, :], op=ALU.add)
            P.op("act", "activation", R=[oa.s(tt)], W=[junk2.s(), sso.s(tt)], out=junk2[:], in_=oa[:, tt, :],
                 func=AF.Square, accum_out=sso[:, tt:tt + 1])
        allso = [sso.s(tt) for tt in range(NT)]
        P.op("dve", "tensor_scalar", R=allso, W=[rso.s()], out=rso[:], in0=sso[:], scalar1=1.0 / 128,
             scalar2=EPS, op0=ALU.mult, op1=ALU.add)
        P.op("pool", "tensor_tensor", R=[rso.s(), cm05.s()], W=[rso.s()], out=rso[:], in0=rso[:],
             in1=cm05[:, 0:NT], op=ALU.pow)
        for tt in range(NT):
            P.op("dve", "scalar_tensor_tensor", R=[oa.s(tt), rso.s(), GA.s(tt)], W=[oa.s(tt)], out=oa[:, tt, :],
                 in0=oa[:, tt, :], scalar=rso[:, tt:tt + 1], in1=GA[:, tt, :], op0=ALU.mult, op1=ALU.mult)

    def attention(h):
        ptk = [0]
        for qc in range(4):
            accA = P.psum_pin()
            accB = P.psum_pin()
            accC = P.psum_pin()
            accs = (accA, accB, accC)

            def acc_ap(c, ql):
                if ql < 3:
                    return (accA, accB)[c], ql * 129
                return accC, c * 129
            first = {id(a): True for a in accs}
            nkb = 4 * qc + 4
            steps = [(kb, c) for kb in range(nkb) for c in range(2)]
            pbuf = {}

            def emit_qk(i):
                kb, c = steps[i]
                ql0 = max(0, kb - 4 * qc)
                ncol = (4 - ql0) * 128
                q0 = qc * 512 + ql0 * 128
                ps = P.psum()
                P.op("pe", "matmul", R=[dkT.s(kb)] + [dqT.s(4 * qc + ql) for ql in range(ql0, 4)], W=[ps.s()],
                     out=ps[:, 0:ncol], lhsT=dkT[c * 64:(c + 1) * 64, kb * 128:(kb + 1) * 128],
                     rhs=dqT[c * 64:(c + 1) * 64, q0:q0 + ncol], start=True, stop=True)
                p = pt[ptk[0] % len(pt)]
                ptk[0] += 1
                pbuf[i] = p
                P.op("act", "activation", R=[ps.s()], W=[p.s()], out=p[:, 0:ncol], in_=ps[:, 0:ncol], func=AF.Exp)
                if kb >= 4 * qc:
                    P.op("pool", "affine_select", R=[p.s()], W=[p.s()], out=p[:, 0:128], in_=p[:, 0:128],
                         pattern=[[1, 128]], compare_op=ALU.is_ge, fill=0.0, base=0, channel_multiplier=-1)

            def emit_pv(i):
                kb, c = steps[i]
                ql0 = max(0, kb - 4 * qc)
                p = pbuf.pop(i)
                calls = []
                touched = []
                for ql in range(ql0, 4):
                    a, off = acc_ap(c, ql)
                    st = first[id(a)]
                    first[id(a)] = False
                    calls.append(("matmul", dict(out=a[:, off:off + 129],
                                                 lhsT=p[:, (ql - ql0) * 128:(ql - ql0 + 1) * 128],
                                                 rhs=vd[:, kb, 0:129], start=st, stop=False,
                                                 skip_group_check=True)))
                    if a not in touched:
                        touched.append(a)
                P.ops("pe", calls, R=[p.s(), vd.s(kb)], W=[a.s() for a in touched])

            LOOK = ATT_LOOK
            nst = len(steps) + LOOK
            for i in range(nst):
                if i < len(steps):
                    emit_qk(i)
                if i - LOOK >= 0:
                    emit_pv(i - LOOK)
                yield 1.8
            QL = range(4)
            accp = {ql: (acc_ap(0, ql), acc_ap(1, ql)) for ql in QL}
            for ql in QL:
                (a0, off0), (a1, off1) = accp[ql]
                rs = rsum[ql]
                P.op("dve", "reciprocal", R=[a0.s()], W=[rs.s()], out=rs[:, 0:1], in_=a0[:, off0 + 128:off0 + 129])
                P.op("dve", "reciprocal", R=[a1.s(), rs.s()], W=[rs.s()], out=rs[:, 1:2],
                     in_=a1[:, off1 + 128:off1 + 129])
            for ql in QL:
                (a0, off0), (a1, off1) = accp[ql]
                rs = rsum[ql]
                P.op("act", "activation", R=[a0.s(), rs.s()], W=[ob[ql].s()], out=ob[ql][:], in_=a0[:, off0:off0 + 128],
                     func=AF.Copy, scale=rs[:, 0:1])
                P.op("act", "activation", R=[a1.s(), rs.s()], W=[tb[ql].s()], out=tb[ql][:], in_=a1[:, off1:off1 + 128],
                     func=AF.Copy, scale=rs[:, 1:2])
            for a in accs:
                P.psum_unpin(a)
            yield 1.0
            for ql in QL:
                qb = 4 * qc + ql
                P.op("dve", "scalar_tensor_tensor", R=[ob[ql].s(), tb[ql].s(), sc.s()], W=[ob16.s(qb)],
                     out=ob16[:, qb, :], in0=tb[ql][:], scalar=nlam, in1=ob[ql][:], op0=ALU.mult, op1=ALU.add)
                P.op("act", "activation", R=[ob16.s(qb)], W=[junk4.s(), rs16.s(qb)], out=junk4[:], in_=ob16[:, qb, :],
                     func=AF.Square, accum_out=rs16[:, qb:qb + 1])
            csl = slice(4 * qc, 4 * qc + 4)
            P.op("dve", "tensor_scalar", R=[rs16.s(4 * qc + ql) for ql in QL], W=[rr16.s(qc)], out=rr16[:, csl],
                 in0=rs16[:, csl], scalar1=1.0 / 128, scalar2=EPS, op0=ALU.mult, op1=ALU.add)
            P.op("pool", "tensor_tensor", R=[rr16.s(qc), cm05.s()], W=[rr16.s(qc)], out=rr16[:, csl], in0=rr16[:, csl],
                 in1=cm05[:, 0:4], op=ALU.pow)
            for ql in QL:
                qb = 4 * qc + ql
                P.op("dve", "scalar_tensor_tensor", R=[ob16.s(qb), rr16.s(qc), GBt.s(qb)], W=[ob16.s(qb)],
                     out=ob16[:, qb, :], in0=ob16[:, qb, :], scalar=rr16[:, qb:qb + 1], in1=GBt[:, qb, :],
                     op0=ALU.mult, op1=ALU.mult)
            yield 1.0

    def attn_post(h):
        for q0 in range(0, NT, 4):
            QB = range(q0, q0 + 4)
            for qb in QB:
                m = ob[qb % 4]
                P.op("pool", "tensor_tensor", R=[ob16.s(qb), oa.s(qb)], W=[m.s()], out=m[:], in0=ob16[:, qb, :],
                     in1=oa[:, qb, :], op=ALU.add)
            for qb in QB:
                m = ob[qb % 4]
                ps = P.psum()
                P.op("pe", "transpose", R=[m.s(), CM], W=[ps.s()], out=ps[:, 0:128], in_=m[:], identity=ident_f)
                evac_copy(mixT[:, h, qb * 128:(qb + 1) * 128], ps[:, 0:128], [ps.s()], [mixT.s(h, qb)])

    def interleave(ga, gb):
        ca = cb = 0.0
        da = db = False
        while not (da and db):
            if not da and (db or ca <= cb):
                try:
                    P.tag = "gdn"
                    ca += next(ga)
                except StopIteration:
                    da = True
            elif not db:
                try:
                    P.tag = "attn+inT"
                    cb += next(gb) * BSCALE
                except StopIteration:
                    db = True

    def chain(*gens):
        for g in gens:
            yield from g

    heads = list(range(NHEADS)) if stop_after not in ("B0",) else [0]
    load_head_weights(heads[0])
    P.tag = "inproj"
    for _ in inproj_T(heads[0]):
        pass
    inproj_F(heads[0])
    for hi, h in enumerate(heads):
        nxt = heads[hi + 1] if hi + 1 < len(heads) else None
        if nxt is not None:
            load_head_weights(nxt)
        P.mark_phase(f"h{h}.mix")
        P.tag = "mix"
        gdn_scalars(h)
        sb = [attention(h)] + ([inproj_T(nxt)] if nxt is not None else [])
        interleave(chain(gdn_prep(h), gdn_recurrence(h)), chain(*sb))
        if nxt is not None:
            P.tag = "inproj"
            inproj_F(nxt)
        P.tag = "attn_post"
        attn_post(h)
        if "oa" in taps and h == 0:
            tap("oa", oa, oa[:], [128, NT, 128], [oa.s(tt) for tt in range(NT)])
            tap("u32", u32, u32[:], [128, NT, 128], [u32.s(tt) for tt in range(NT)])
    P.mark_phase("C")
    P.barrier()
    if stop_after in ("B", "B0"):
        d = nc.dram_tensor("tap_mixT", [128, H, S], BF16, kind="ExternalOutput").ap()
        P.dma("sp", out=d, in_=mixT[:])
        P.barrier()
        P.emit()
        return nc, P
    P.release(mB)

    h1 = P.sbuf("h1", [128, NT, D], F32)
    mC = P.mark()
    wout = P.sbuf("wout", [128, 8, D], BF16)
    wrow2 = P.sbuf("wrow2", [128, D], F32)
    xb = [P.sbuf(f"xb{i}", [128, D], F32) for i in range(2)]
    xn2 = [P.sbuf(f"xn2{i}", [128, D], F32) for i in range(2)]
    ssB = P.sbuf("ssB", [128, NT], F32)
    rstd2 = P.sbuf("rstd2", [128, NT], F32)
    junk3 = P.sbuf("junk3", [128, D], BF16)
    P.dma("sp", W=[wrow2.s()], out=wrow2[:], in_=norm2_d.partition_broadcast(128))
    for hh in range(8):
        P.dma("pool", W=[wout.s(hh)], out=wout[:, hh, :], in_=w_out_d[hh * 128:(hh + 1) * 128, :])
    for tt in range(NT):
        x_ = xb[tt % 2]
        P.dma("sp", W=[x_.s()], out=x_[:], in_=x_d[tt * 128:(tt + 1) * 128, :])
        for n in range(2):
            ps = P.psum()
            P.ops("pe", [("matmul", dict(out=ps[:, 0:512], lhsT=mixT[:, hh, tt * 128:(tt + 1) * 128],
                                         rhs=wout[:, hh, n * 512:(n + 1) * 512], start=(hh == 0), stop=(hh == 7)))
                         for hh in range(8)],
                  R=[mixT.s(hh, tt) for hh in range(8)] + [wout.s(hh) for hh in range(8)], W=[ps.s()])
            P.op("dve", "tensor_tensor", R=[ps.s(), x_.s()], W=[h1.s(tt, n)], out=h1[:, tt, n * 512:(n + 1) * 512],
                 in0=ps[:, 0:512], in1=x_[:, n * 512:(n + 1) * 512], op=ALU.add)
        P.op("act", "activation", R=[h1.s(tt, 0), h1.s(tt, 1)], W=[junk3.s(), ssB.s(tt)], out=junk3[:],
             in_=h1[:, tt, :], func=AF.Square, accum_out=ssB[:, tt:tt + 1])
    for tt in range(NT):
        P.op("dve", "tensor_scalar", R=[ssB.s(tt)], W=[rstd2.s(tt)], out=rstd2[:, tt:tt + 1], in0=ssB[:, tt:tt + 1],
             scalar1=1.0 / D, scalar2=EPS, op0=ALU.mult, op1=ALU.add)
        P.op("pool", "tensor_tensor", R=[rstd2.s(tt), cm05.s()], W=[rstd2.s(tt)], out=rstd2[:, tt:tt + 1],
             in0=rstd2[:, tt:tt + 1], in1=cm05[:, 0:1], op=ALU.pow)
    for tt in range(NT):
        xn = xn2[tt % 2]
        P.op("dve", "scalar_tensor_tensor", R=[h1.s(tt, 0), h1.s(tt, 1), rstd2.s(tt), wrow2.s()], W=[xn.s()],
             out=xn[:], in0=h1[:, tt, :], scalar=rstd2[:, tt:tt + 1], in1=wrow2[:], op0=ALU.mult, op1=ALU.mult)
        for half in range(2):
            ps = P.psum()
            P.ops("pe", [("transpose", dict(out=ps[:, j * 128:(j + 1) * 128],
                                            in_=xn[:, (half * 4 + j) * 128:(half * 4 + j + 1) * 128],
                                            identity=ident_f)) for j in range(4)],
                  R=[xn.s(), CM], W=[ps.s()])
            evac_copy(uT[:, half * 4:(half + 1) * 4, tt * 128:(tt + 1) * 128],
                      ps[:, 0:512].rearrange("p (k t) -> p k t", k=4), [ps.s()], [uT.s(tt, half)])
    P.barrier()
    if stop_after == "C":
        for tt in range(NT):
            P.dma("sp", out=out_d[tt * 128:(tt + 1) * 128, :], in_=h1[:, tt, :])
        d = nc.dram_tensor("tap_uT2", [128, 8, S], BF16, kind="ExternalOutput").ap()
        P.dma("sp", out=d, in_=uT[:])
        P.barrier()
        P.emit()
        return nc, P
    P.release(mC)

    P.mark_phase("D")
    r1 = MIXT_OFF
    wgu = []
    for i in range(4):
        a, r1 = P.sbuf_at(f"wg{i}", [128, 8, 128], BF16, r1)
        b, r1 = P.sbuf_at(f"wu{i}", [128, 8, 128], BF16, r1)
        wgu.append((a, b))
    sil = []
    for i in range(2):
        a, r1 = P.sbuf_at(f"sil{i}", [128, 512], BF16, r1)
        sil.append(a)
    ost = []
    for i in range(2):
        a, r1 = P.sbuf_at(f"ost{i}", [128, 256], F32, r1)
        ost.append(a)
    assert r1 <= MIXT_OFF + H * S * 2
    actT = P.sbuf("actT", [128, NFC, 1024], BF16)
    wdb = [P.sbuf(f"wdb{i}", [128, NFC, 256], BF16) for i in range(2)]
    wdk = 0
    for hf in range(2 if DCUT == 0 else 1):
        for fc in range(NFC):
            wg_, wu_ = wgu[fc % WRING]
            P.dma("pool", W=[wg_.s()], out=wg_[:],
                  in_=w_gate_d[:, fc * 128:(fc + 1) * 128].rearrange("(k p) c -> p k c", p=128))
            P.dma("pool", W=[wu_.s()], out=wu_[:],
                  in_=w_up_d[:, fc * 128:(fc + 1) * 128].rearrange("(k p) c -> p k c", p=128))
            for tcl in range(2):
                tok = slice(hf * 1024 + tcl * 512, hf * 1024 + (tcl + 1) * 512)
                tts = [(hf * 1024 + tcl * 512) // 128 + j for j in range(4)]
                Ru = [uT.s(tt, half) for tt in tts for half in range(2)]
                psg = P.psum()
                psu = P.psum()
                P.ops("pe", [("matmul", dict(out=psg[:, 0:512], lhsT=wg_[:, kc, :], rhs=uT[:, kc, tok],
                                             start=(kc == 0), stop=(kc == 7))) for kc in range(8)],
                      R=Ru + [wg_.s()], W=[psg.s()])
                P.ops("pe", [("matmul", dict(out=psu[:, 0:512], lhsT=wu_[:, kc, :], rhs=uT[:, kc, tok],
                                             start=(kc == 0), stop=(kc == 7))) for kc in range(8)],
                      R=Ru + [wu_.s()], W=[psu.s()])
                sl = sil[tcl]
                P.op("act", "activation", R=[psg.s()], W=[sl.s()], out=sl[:], in_=psg[:, 0:512], func=AF.Silu)
                P.op("dve", "tensor_tensor", R=[psu.s(), sl.s()], W=[actT.s(fc, tcl)],
                     out=actT[:, fc, tcl * 512:(tcl + 1) * 512], in0=psu[:, 0:512], in1=sl[:], op=ALU.mult)
        for n4 in range(4 if DCUT != 1 else 0):
            wd_ = wdb[wdk % 2]
            wdk += 1
            for fc in range(NFC):
                P.dma("pool", ndesc=8, W=[wd_.s(fc)], out=wd_[:, fc, :],
                      in_=w_down_d[fc * 128:(fc + 1) * 128, n4 * 256:(n4 + 1) * 256])
            for tl in range(8):
                tt = hf * 8 + tl
                ps = P.psum()
                P.ops("pe", [("matmul", dict(out=ps[:, 0:256], lhsT=actT[:, fc, tl * 128:(tl + 1) * 128],
                                             rhs=wd_[:, fc, :], start=(fc == 0), stop=(fc == NFC - 1)))
                             for fc in range(NFC)],
                      R=[actT.s(fc, tl // 4) for fc in range(NFC)] + [wd_.s(fc) for fc in range(NFC)], W=[ps.s()])
                o_ = ost[(n4 * 8 + tl) % 2]
                P.op("dve", "tensor_tensor", R=[ps.s(), h1.s(tt, n4 // 2)], W=[o_.s()], out=o_[:], in0=ps[:, 0:256],
                     in1=h1[:, tt, n4 * 256:(n4 + 1) * 256], op=ALU.add)
                P.dma("sp", R=[o_.s()], out=out_d[tt * 128:(tt + 1) * 128, n4 * 256:(n4 + 1) * 256], in_=o_[:])
    P.barrier()
    P.emit()
    return nc, P


def _host_consts():
    ident = np.eye(128, dtype=np.float32)
    p = np.arange(128)
    ltri = (p[:, None] <= p[None, :]).astype(np.float32)
    ones = np.ones((128, 128), np.float32)
    sel63 = np.zeros((128, 128), np.float32)
    sel63[63, :] = 1.0
    sel127 = np.zeros((128, 128), np.float32)
    sel127[127, :] = 1.0
    return np.ascontiguousarray(np.concatenate([ident, ltri, ones, sel63, sel127], axis=1))


def make_in_maps(inputs, cores):
    f = lambda k: np.ascontiguousarray(np.asarray(inputs[k], dtype=np.float32))
    conv_w = f("conv_w")[0]
    conv_wl = np.ascontiguousarray(conv_w.T.reshape(24, 128, 4).transpose(1, 0, 2).reshape(128, 96))
    lqk = np.ascontiguousarray(np.concatenate([f("lambda_q1")[0], f("lambda_k1")[0], f("lambda_q2")[0],
                                               f("lambda_k2")[0]])[None, :])
    shared = {
        "w_in": f("w_in")[0], "w_out": f("w_out")[0], "w_gate": f("w_gate")[0], "w_up": f("w_up")[0],
        "w_down": f("w_down")[0], "norm1_w": f("norm1_w"), "norm2_w": f("norm2_w"),
        "gdn_norm_w": f("gdn_norm_w"), "subln_w": f("subln_w"), "q_norm_w": f("q_norm_w"),
        "k_norm_w": f("k_norm_w"), "lqk": lqk, "a_log": f("a_log"), "dt_bias": f("dt_bias"),
        "conv_wl": conv_wl, "cmat": _host_consts(),
    }
    x = f("x")
    return [dict(shared, x=np.ascontiguousarray(x[b])) for b in cores]


def kernel(**inputs):
    nc, _ = build_program()
    in_maps = make_in_maps(inputs, list(range(8)))
    res = run_bass_kernel_spmd(nc, in_maps, core_ids=list(range(8)))
    return np.stack([np.asarray(r["out"], dtype=np.float32) for r in res.results], axis=0)
```
